# Optimizing a Trainium2 kernel written in Bass

```python
import math
import jax, jax.numpy as jnp
from jax import lax
import numpy as np

D_MODEL = 2048
BATCH = 1
SEQ = 8192
DEPTH = 4

DIFF_HEADS = 4
DIFF_QK_DIM = 64
DIFF_V_DIM = 2 * DIFF_QK_DIM
DIFF_WIDTH = DIFF_HEADS * DIFF_V_DIM
DIFF_QK_COLS = DIFF_HEADS * 2 * DIFF_QK_DIM
SWA_Q_HEADS = 8
SWA_KV_HEADS = 2
SWA_GROUP = SWA_Q_HEADS // SWA_KV_HEADS
SWA_HEAD_DIM = 64
SWA_WINDOW = 128
SWA_WIDTH = SWA_Q_HEADS * SWA_HEAD_DIM
SWA_KV_COLS = SWA_KV_HEADS * SWA_HEAD_DIM
RWKV_HEADS = 16
RWKV_HEAD_DIM = 64
RWKV_WIDTH = RWKV_HEADS * RWKV_HEAD_DIM
DECAY_LORA = 96
ICLR_LORA = 96
GATE_LORA = 256
RWKV_COLS = 3 * RWKV_WIDTH + DECAY_LORA + ICLR_LORA + GATE_LORA

MIX_WIDTH = DIFF_WIDTH + SWA_WIDTH + RWKV_WIDTH
IN_COLS = 2 * DIFF_QK_COLS + DIFF_WIDTH + SWA_WIDTH + 2 * SWA_KV_COLS + RWKV_COLS
FFN_HIDDEN = (8 * D_MODEL + 3 * 256 - 1) // (3 * 256) * 256

ALIBI_HEADS = DIFF_HEADS + SWA_Q_HEADS
Q_BLOCK = 128
NORM_EPS = 1e-5
RWKV_GN_EPS = 64e-5
NEG_INF = -1e30

kernel_name = 'hybrid_diffattn_swa_sink_rwkv7_swiglu'


def _split(z, sizes):
    out, start = [], 0
    for s in sizes:
        out.append(z[..., start:start + s])
        start += s
    return out


def rms_norm(x, g):
    xf = x.astype(jnp.float32)
    y = xf * lax.rsqrt(jnp.mean(xf * xf, axis=-1, keepdims=True) + NORM_EPS)
    return (y * g.astype(jnp.float32)).astype(x.dtype)


def alibi_slopes():
    idx = jnp.arange(1, ALIBI_HEADS + 1, dtype=jnp.float32)
    m = jnp.exp2(-8.0 * idx / ALIBI_HEADS)
    diff_idx = np.arange(2, ALIBI_HEADS, 3)
    swa_idx = np.setdiff1d(np.arange(ALIBI_HEADS), diff_idx)
    return m[diff_idx], m[swa_idx]


def token_shift(h, mu):
    prev = jnp.pad(h, ((0, 0), (1, 0), (0, 0)))[:, :-1]
    return h + (prev - h) * mu


def diff_attention(q, k, v, lam, lambda_init, subln_g, slopes):
    b, t = q.shape[:2]
    nblk = t // Q_BLOCK
    qf = q.astype(jnp.float32) * (DIFF_QK_DIM ** -0.5)
    kf = k.astype(jnp.float32)
    vf = v.astype(jnp.float32)
    q_blocks = jnp.moveaxis(qf.reshape(b, nblk, Q_BLOCK, DIFF_HEADS, 2, DIFF_QK_DIM), 1, 0)
    kpos = jnp.arange(t)

    def one_block(args):
        q_blk, blk = args
        s = jnp.einsum('bqhcd,bkhcd->bhcqk', q_blk, kf)
        qpos = blk * Q_BLOCK + jnp.arange(Q_BLOCK)
        dist = qpos[:, None] - kpos[None, :]
        s = s - slopes[None, :, None, None, None] * dist.astype(jnp.float32)
        s = jnp.where(dist >= 0, s, NEG_INF)
        p = jax.nn.softmax(s, axis=-1)
        w = p[:, :, 0] - lam * p[:, :, 1]
        return jnp.einsum('bhqk,bkhe->bqhe', w, vf)

    o = lax.map(one_block, (q_blocks, jnp.arange(nblk)))
    o = jnp.moveaxis(o, 0, 1).reshape(b, t, DIFF_HEADS, DIFF_V_DIM)
    o = o * lax.rsqrt(jnp.mean(o * o, axis=-1, keepdims=True) + NORM_EPS)
    o = o * subln_g.astype(jnp.float32) * (1.0 - lambda_init)
    return o.reshape(b, t, DIFF_WIDTH)


def sliding_window_attention(q, k, v, sinks, slopes):
    b, t = q.shape[:2]
    W = SWA_WINDOW
    nb = t // W
    qb = (q.astype(jnp.float32) * (SWA_HEAD_DIM ** -0.5)).reshape(
        b, nb, W, SWA_KV_HEADS, SWA_GROUP, SWA_HEAD_DIM)

    def band(z):
        zb = z.astype(jnp.float32).reshape(b, nb, W, SWA_KV_HEADS, SWA_HEAD_DIM)
        prev = jnp.pad(zb, ((0, 0), (1, 0), (0, 0), (0, 0), (0, 0)))[:, :-1]
        return jnp.concatenate([prev, zb], axis=2)

    kw, vw = band(k), band(v)
    s = jnp.einsum('bnqhgd,bnkhd->bnhgqk', qb, kw)
    i = jnp.arange(W)
    j = jnp.arange(2 * W)
    dist = i[:, None] + W - j[None, :]
    key_abs = jnp.arange(nb)[:, None, None] * W - W + j[None, None, :]
    valid = (dist >= 0) & (dist < W) & (key_abs >= 0)
    bias = -slopes.reshape(SWA_KV_HEADS, SWA_GROUP)[:, :, None, None] * dist.astype(jnp.float32)
    s = jnp.where(valid[None, :, None, None], s + bias, NEG_INF)
    sink = jnp.broadcast_to(
        sinks.astype(jnp.float32).reshape(SWA_KV_HEADS, SWA_GROUP)[None, None, :, :, None, None],
        s.shape[:-1] + (1,))
    p = jax.nn.softmax(jnp.concatenate([s, sink], axis=-1), axis=-1)[..., :-1]
    o = jnp.einsum('bnhgqk,bnkhd->bnqhgd', p, vw)
    return o.reshape(b, t, SWA_WIDTH)


def wkv7_scan(r, w, k, v, a, bvec):
    b, t, H, N = r.shape

    def step(S, inp):
        r_t, w_t, k_t, v_t, a_t, b_t = inp
        sa = jnp.einsum('bhvk,bhk->bhv', S, a_t)
        S = S * w_t[:, :, None, :] + sa[..., None] * b_t[:, :, None, :] + v_t[..., None] * k_t[:, :, None, :]
        return S, jnp.einsum('bhvk,bhk->bhv', S, r_t)

    xs = tuple(jnp.moveaxis(z, 1, 0) for z in (r, w, k, v, a, bvec))
    S0 = jnp.zeros((b, H, N, N), jnp.float32)
    _, y = lax.scan(step, S0, xs)
    return jnp.moveaxis(y, 0, 1)


def rwkv7_time_mix(feats, w0, w2, a0, a2, g2, k_k, k_a, r_k, ln_w, ln_b):
    b, t = feats.shape[:2]
    H, N = RWKV_HEADS, RWKV_HEAD_DIM
    f32 = jnp.float32
    r, k, v, wl, al, gl = _split(feats.astype(f32),
                                 [RWKV_WIDTH] * 3 + [DECAY_LORA, ICLR_LORA, GATE_LORA])
    logw = -jax.nn.softplus(-(w0.astype(f32) + jnp.tanh(wl) @ w2.astype(f32))) - 0.5
    decay = jnp.exp(-jnp.exp(logw))
    a = jax.nn.sigmoid(a0.astype(f32) + al @ a2.astype(f32))
    g = jax.nn.sigmoid(gl) @ g2.astype(f32)

    def hs(z):
        return z.reshape(b, t, H, N)

    kk = hs(k * k_k.astype(f32))
    kk = kk / jnp.maximum(jnp.sqrt(jnp.sum(kk * kk, axis=-1, keepdims=True)), 1e-12)
    k = k * (1.0 + (a - 1.0) * k_a.astype(f32))
    r4, k4, v4, a4 = hs(r), hs(k), hs(v), hs(a)
    y = wkv7_scan(r4, hs(decay), k4, v4, -kk, kk * a4)
    mu = jnp.mean(y, axis=-1, keepdims=True)
    var = jnp.mean(jnp.square(y - mu), axis=-1, keepdims=True)
    y = ((y - mu) * lax.rsqrt(var + RWKV_GN_EPS)).reshape(b, t, RWKV_WIDTH)
    y = y * ln_w.astype(f32) + ln_b.astype(f32)
    bonus = jnp.sum(r4 * k4 * r_k.astype(f32), axis=-1, keepdims=True) * v4
    y = y + bonus.reshape(b, t, RWKV_WIDTH)
    return y * g


def setup_inputs(seed: int = 0) -> dict:
    key = jax.random.key(seed)
    ks = jax.random.split(key, 22)
    f32 = jnp.float32
    L = DEPTH

    def nrm(k, shape, scale):
        return jax.random.normal(k, shape, f32) * scale

    return {
        'x': nrm(ks[0], (BATCH, SEQ, D_MODEL), 1.0),
        'attn_norm_g': 1.0 + nrm(ks[1], (L, D_MODEL), 0.02),
        'w_in': nrm(ks[2], (L, D_MODEL, IN_COLS), D_MODEL ** -0.5),
        'diff_lambda': nrm(ks[3], (L, 4, DIFF_QK_DIM), 0.1),
        'diff_subln_g': 1.0 + nrm(ks[4], (L, DIFF_V_DIM), 0.02),
        'swa_sinks': nrm(ks[5], (L, SWA_Q_HEADS), 0.5),
        'rwkv_mu': jax.random.uniform(ks[6], (L, RWKV_COLS), f32),
        'rwkv_w0': jax.random.uniform(ks[7], (L, RWKV_WIDTH), f32, minval=-5.0, maxval=0.0),
        'rwkv_w2': nrm(ks[8], (L, DECAY_LORA, RWKV_WIDTH), 0.3 * DECAY_LORA ** -0.5),
        'rwkv_a0': nrm(ks[9], (L, RWKV_WIDTH), 0.2),
        'rwkv_a2': nrm(ks[10], (L, ICLR_LORA, RWKV_WIDTH), 0.3 * ICLR_LORA ** -0.5),
        'rwkv_g2': nrm(ks[11], (L, GATE_LORA, RWKV_WIDTH), GATE_LORA ** -0.5),
        'rwkv_k_k': 0.85 + nrm(ks[12], (L, RWKV_WIDTH), 0.02),
        'rwkv_k_a': 1.0 + nrm(ks[13], (L, RWKV_WIDTH), 0.02),
        'rwkv_r_k': nrm(ks[14], (L, RWKV_HEADS, RWKV_HEAD_DIM), 0.1),
        'rwkv_ln_w': 1.0 + nrm(ks[15], (L, RWKV_WIDTH), 0.02),
        'rwkv_ln_b': nrm(ks[16], (L, RWKV_WIDTH), 0.02),
        'w_out': nrm(ks[17], (L, MIX_WIDTH, D_MODEL), MIX_WIDTH ** -0.5),
        'ffn_norm_g': 1.0 + nrm(ks[18], (L, D_MODEL), 0.02),
        'w_gate_up': nrm(ks[19], (L, D_MODEL, 2 * FFN_HIDDEN), D_MODEL ** -0.5),
        'w_down': nrm(ks[20], (L, FFN_HIDDEN, D_MODEL), FFN_HIDDEN ** -0.5),
        'final_norm_g': 1.0 + nrm(ks[21], (D_MODEL,), 0.02),
    }


def reference(x, attn_norm_g, w_in, diff_lambda, diff_subln_g, swa_sinks, rwkv_mu,
              rwkv_w0, rwkv_w2, rwkv_a0, rwkv_a2, rwkv_g2, rwkv_k_k, rwkv_k_a, rwkv_r_k,
              rwkv_ln_w, rwkv_ln_b, w_out, ffn_norm_g, w_gate_up, w_down, final_norm_g):
    b, t, _ = x.shape
    diff_slopes, swa_slopes = alibi_slopes()
    for l in range(DEPTH):
        h = rms_norm(x, attn_norm_g[l])
        proj = h @ w_in[l]
        qa, ka, va, qb, kb, vb, rw = _split(
            proj, [DIFF_QK_COLS, DIFF_QK_COLS, DIFF_WIDTH, SWA_WIDTH, SWA_KV_COLS, SWA_KV_COLS, RWKV_COLS])

        lambda_init = 0.8 - 0.6 * math.exp(-0.3 * l)
        lamv = diff_lambda[l].astype(jnp.float32)
        lam = jnp.exp(jnp.sum(lamv[0] * lamv[1])) - jnp.exp(jnp.sum(lamv[2] * lamv[3])) + lambda_init
        ya = diff_attention(qa.reshape(b, t, DIFF_HEADS, 2, DIFF_QK_DIM),
                            ka.reshape(b, t, DIFF_HEADS, 2, DIFF_QK_DIM),
                            va.reshape(b, t, DIFF_HEADS, DIFF_V_DIM),
                            lam, lambda_init, diff_subln_g[l], diff_slopes)

        yb = sliding_window_attention(qb.reshape(b, t, SWA_Q_HEADS, SWA_HEAD_DIM),
                                      kb.reshape(b, t, SWA_KV_HEADS, SWA_HEAD_DIM),
                                      vb.reshape(b, t, SWA_KV_HEADS, SWA_HEAD_DIM),
                                      swa_sinks[l], swa_slopes)

        feats = token_shift(rw, rwkv_mu[l])
        yc = rwkv7_time_mix(feats, rwkv_w0[l], rwkv_w2[l], rwkv_a0[l], rwkv_a2[l], rwkv_g2[l],
                            rwkv_k_k[l], rwkv_k_a[l], rwkv_r_k[l], rwkv_ln_w[l], rwkv_ln_b[l])

        mix = jnp.concatenate([ya.astype(x.dtype), yb.astype(x.dtype), yc.astype(x.dtype)], axis=-1)
        x = x + mix @ w_out[l]

        h = rms_norm(x, ffn_norm_g[l])
        gate, up = _split(h @ w_gate_up[l], [FFN_HIDDEN, FFN_HIDDEN])
        x = x + (jax.nn.silu(gate) * up) @ w_down[l]
    return rms_norm(x, final_norm_g)
```

```python
import math
import numpy as np
import concourse.bass as bass
import concourse.mybir as mybir
from concourse.bass_utils import run_bass_kernel_spmd

F32 = mybir.dt.float32
BF16 = mybir.dt.bfloat16
AF = mybir.ActivationFunctionType
ALU = mybir.AluOpType
AX = mybir.AxisListType

NCORES = 8
NA = 8
NC = 4
D_MODEL = 2048
SEQ = 8192
DEPTH = 4
IN_COLS = 5824
FFN_HIDDEN = 5632
NORM_EPS = 1e-5
DEBUG = False
RW_STAGE = 0


class _Op:
    __slots__ = ("eng", "fn", "deps", "dma", "sem", "val", "signal", "idx")


class Prog:
    ENGS = ("pe", "act", "dve", "pool", "sp")
    N_DMA_SEMS = 6
    SEM_LIMIT = 30000

    def __init__(self, nc):
        self.nc = nc
        self.ops = []
        self.last_w = {}
        self.readers = {}
        self.dma_rr = {e: 0 for e in self.ENGS}
        self.dma_last = {}

    def add(self, eng, fn, reads=(), writes=(), dma=False, args=(), kwargs=None):
        if isinstance(fn, str):
            name, a, kw = fn, tuple(args), dict(kwargs or {})
            fn = lambda e, name=name, a=a, kw=kw: getattr(e, name)(*a, **kw)
        op = _Op()
        op.eng, op.fn, op.dma = eng, fn, dma
        op.idx = len(self.ops)
        deps = set()
        for r in reads:
            lw = self.last_w.get(r)
            if lw is not None:
                deps.add(lw)
        for w in writes:
            lw = self.last_w.get(w)
            if lw is not None:
                deps.add(lw)
            deps.update(self.readers.get(w, ()))
        for r in reads:
            self.readers.setdefault(r, []).append(op.idx)
        for w in writes:
            self.last_w[w] = op.idx
            self.readers[w] = []
        if dma:
            slot = (eng, self.dma_rr[eng] % self.N_DMA_SEMS)
            self.dma_rr[eng] += 1
            prev = self.dma_last.get(slot)
            if prev is not None:
                deps.add(prev)
            self.dma_last[slot] = op.idx
            op.sem = slot
        deps.discard(op.idx)
        op.deps = deps
        op.signal = False
        self.ops.append(op)
        return op.idx

    def pe(self, fn, *a, r=(), w=(), **kw):
        return self.add("pe", fn, r, w, args=a, kwargs=kw)

    def act(self, fn, *a, r=(), w=(), **kw):
        return self.add("act", fn, r, w, args=a, kwargs=kw)

    def dve(self, fn, *a, r=(), w=(), **kw):
        return self.add("dve", fn, r, w, args=a, kwargs=kw)

    def pool(self, fn, *a, r=(), w=(), **kw):
        return self.add("pool", fn, r, w, args=a, kwargs=kw)

    def dma(self, eng, fn=None, r=(), w=(), **kw):
        if fn is None:
            fn = "dma_start"
        return self.add(eng, fn, r, w, dma=True, kwargs=kw)

    def barrier(self):
        last = {}
        for op in self.ops:
            if op.fn is None:
                continue
            if op.dma:
                last[("dma", id(op.sem) if not isinstance(op.sem, tuple) else op.sem)] = op.idx
            else:
                last[op.eng] = op.idx
        deps = set(last.values())
        for e in self.ENGS:
            op = _Op()
            op.eng, op.fn, op.dma = e, None, False
            op.idx = len(self.ops)
            op.deps = set(deps)
            op.signal = False
            self.ops.append(op)
        self.last_w = {}
        self.readers = {}

    def emit(self):
        nc = self.nc
        ops = self.ops
        for op in ops:
            for d in op.deps:
                dop = ops[d]
                if dop.dma:
                    continue
                if dop.eng == "pe" and op.eng == "pe" and not op.dma:
                    continue
                dop.signal = True
        last_of = {}
        for op in ops:
            if not op.dma and op.fn is not None:
                last_of[op.eng] = op
        for op in last_of.values():
            op.signal = True
        import contextlib
        stack = contextlib.ExitStack()
        eng_sems = {e: [stack.enter_context(nc.semaphore(f"s_{e}_0"))] for e in self.ENGS}
        eng_cnt = {e: 0 for e in self.ENGS}
        dma_sems = {}
        dma_cnt = {}
        for op in ops:
            if op.dma:
                slot = op.sem
                if slot not in dma_sems:
                    dma_sems[slot] = stack.enter_context(nc.semaphore(f"d_{slot[0]}_{slot[1]}"))
                    dma_cnt[slot] = 0
                dma_cnt[slot] += 16
                op.sem = dma_sems[slot]
                op.val = dma_cnt[slot]
            elif op.signal:
                if eng_cnt[op.eng] >= self.SEM_LIMIT:
                    eng_sems[op.eng].append(
                        stack.enter_context(nc.semaphore(f"s_{op.eng}_{len(eng_sems[op.eng])}")))
                    eng_cnt[op.eng] = 0
                eng_cnt[op.eng] += 1
                op.sem = eng_sems[op.eng][-1]
                op.val = eng_cnt[op.eng]
        streams = {e: [] for e in self.ENGS}
        for op in ops:
            streams[op.eng].append(op)
        final_waits = {}
        for op in ops:
            if op.dma or op.signal:
                key = id(op.sem)
                if key not in final_waits or final_waits[key][1] < op.val:
                    final_waits[key] = (op.sem, op.val)

        def run_stream(ename, eng):
            waited = {}
            for op in streams[ename]:
                need = {}
                for d in op.deps:
                    dop = ops[d]
                    if (not dop.dma) and dop.eng == "pe" and ename == "pe" and not op.dma:
                        continue
                    k = id(dop.sem)
                    if k not in need or need[k][1] < dop.val:
                        need[k] = (dop.sem, dop.val)
                for k, (sem, val) in need.items():
                    if waited.get(k, 0) >= val:
                        continue
                    eng.wait_ge(sem, val)
                    waited[k] = val
                if op.fn is None:
                    continue
                ins = op.fn(eng)
                if op.dma:
                    ins.then_inc(op.sem, 16)
                elif op.signal:
                    ins.then_inc(op.sem, 1)
            if ename == "sp":
                for k, (sem, val) in final_waits.items():
                    if waited.get(k, 0) >= val:
                        continue
                    eng.wait_ge(sem, val)

        all_sems = [s for lst in eng_sems.values() for s in lst] + list(dma_sems.values())
        with stack:
            for s in all_sems:
                nc.gpsimd.sem_clear(s)
            nc.all_engine_barrier()
            with nc.Block() as block:
                @block.tensor
                def _(e):
                    run_stream("pe", e)

                @block.scalar
                def _(e):
                    run_stream("act", e)

                @block.vector
                def _(e):
                    run_stream("dve", e)

                @block.gpsimd
                def _(e):
                    run_stream("pool", e)

                @block.sync
                def _(e):
                    run_stream("sp", e)
            nc.all_engine_barrier()
            for s in all_sems:
                nc.gpsimd.sem_clear(s)


def build_proj(T, D, NOUT, PANEL=512, out_dtype=F32, TP=1024):
    nc = bass.Bass("TRN2", target_bir_lowering=False)
    KC = D // 128
    NT = TP // 512
    xT = nc.dram_tensor("xT", [D, T], F32, kind="ExternalInput").ap()
    g = nc.dram_tensor("g", [128, KC], F32, kind="ExternalInput").ap()
    w = nc.dram_tensor("w", [D, NOUT], F32, kind="ExternalInput").ap()
    yT = nc.dram_tensor("yT", [NOUT, T], out_dtype, kind="ExternalOutput").ap()
    P = Prog(nc)
    import contextlib
    with contextlib.ExitStack() as es:
        def sb(name, shape, dt):
            return es.enter_context(nc.sbuf_tensor(name, shape, dt))

        def ps(name, shape, dt=F32):
            return es.enter_context(nc.psum_tensor(name, shape, dt))

        x_sb = sb("x_sb", [128, KC, TP], F32)
        h_sb = sb("h_sb", [128, KC, TP], BF16)
        g_sb = sb("g_sb", [128, KC], F32)
        sq = [sb(f"sq{i}", [128, TP], F32) for i in range(2)]
        ones = sb("ones", [128, 128], F32)
        rstd = sb("rstd", [128, TP], F32)
        wp = [sb(f"wp{i}", [128, KC, PANEL], BF16) for i in range(2)]
        ob = [sb(f"ob{i}", [128, 512], out_dtype) for i in range(4)]
        pss = [ps(f"pss{i}", [128, 512]) for i in range(NT)]
        pacc = [ps(f"pacc{i}", [128, 512]) for i in range(4)]
        w_v = w.rearrange("(k p) c -> p k c", p=128)
        xT_v = xT.rearrange("(k p) t -> p k t", p=128)

        P.dma("sp", out=g_sb[:], in_=g, w=["g"])
        P.dve("memset", ones[:], 1.0, w=["ones"])
        cnt = 0
        wcnt = 0
        for tp in range(T // TP):
            tb = tp * TP
            for c0 in range(0, KC, 4):
                P.dma("sp", out=x_sb[:, c0:c0 + 4, :], in_=xT_v[:, c0:c0 + 4, tb:tb + TP], w=[("x", c) for c in range(c0, c0 + 4)])
            for c in range(KC):
                s = sq[c % 2]
                P.act("activation", out=s[:], in_=x_sb[:, c, :], func=AF.Square, r=[("x", c)], w=[("sq", c % 2)])
                for n in range(NT):
                    P.pe("matmul", pss[n][:], lhsT=ones[:], rhs=s[:, n * 512:(n + 1) * 512], start=(c == 0), stop=(c == KC - 1),
                         r=["ones", ("sq", c % 2)], w=[("pss", n)])
            for n in range(NT):
                sl = slice(n * 512, (n + 1) * 512)
                P.dve("tensor_scalar", out=rstd[:, sl], in0=pss[n][:], scalar1=1.0 / D, scalar2=NORM_EPS, op0=ALU.mult, op1=ALU.add,
                      r=[("pss", n)], w=[("rstd", n)])
                P.act("activation", out=rstd[:, sl], in_=rstd[:, sl], func=AF.Sqrt, r=[("rstd", n)], w=[("rstd", n)])
                P.dve("reciprocal", out=rstd[:, sl], in_=rstd[:, sl], r=[("rstd", n)], w=[("rstd", n)])
            for c in range(KC):
                for n in range(NT):
                    sl = slice(n * 512, (n + 1) * 512)
                    P.dve("scalar_tensor_tensor", out=h_sb[:, c, sl], in0=x_sb[:, c, sl], scalar=g_sb[:, c:c + 1], in1=rstd[:, sl],
                          op0=ALU.mult, op1=ALU.mult, r=[("x", c), "g", ("rstd", n)], w=[("h", c, n)])
            npan = (NOUT + PANEL - 1) // PANEL
            for pi in range(npan):
                c0 = pi * PANEL
                pw = min(PANEL, NOUT - c0)
                wb = wcnt % 2
                wcnt += 1
                wt = wp[wb]
                P.dma("pool", out=wt[:, :, :pw], in_=w_v[:, :, c0:c0 + pw], w=[("wp", wb)])
                for m0 in range(0, pw, 128):
                    mw = min(128, pw - m0)
                    for n in range(NT):
                        sl = slice(n * 512, (n + 1) * 512)
                        pa = pacc[cnt % 4]
                        o = ob[cnt % 4]
                        for k in range(KC):
                            P.pe("matmul", pa[:mw, :], lhsT=wt[:, k, m0:m0 + mw], rhs=h_sb[:, k, sl], start=(k == 0), stop=(k == KC - 1),
                                 r=[("wp", wb), ("h", k, n)], w=[("pacc", cnt % 4)])
                        if cnt % 2 == 0:
                            P.act("activation", out=o[:mw, :], in_=pa[:mw, :], func=AF.Copy, r=[("pacc", cnt % 4)], w=[("ob", cnt % 4)])
                        else:
                            P.dve("tensor_copy", out=o[:mw, :], in_=pa[:mw, :], r=[("pacc", cnt % 4)], w=[("ob", cnt % 4)])
                        r0 = c0 + m0
                        P.dma("sp", out=yT[r0:r0 + mw, tb + n * 512:tb + (n + 1) * 512], in_=o[:mw, :], r=[("ob", cnt % 4)])
                        cnt += 1
        P.emit()
    return nc


class _Ctx:
    def __init__(self, nc, es):
        self.nc, self.es = nc, es

    def sb(self, name, shape, dt=F32):
        return self.es.enter_context(self.nc.sbuf_tensor(name, shape, dt))

    def din(self, name, shape, dt=F32):
        return self.nc.dram_tensor(name, shape, dt, kind="ExternalInput").ap()

    def dout(self, name, shape, dt=F32):
        return self.nc.dram_tensor(name, shape, dt, kind="ExternalOutput").ap()


def emit_diff(P, C, PS, T):
    nc = C.nc
    NI = T // 1024
    NQ = NI * 512
    NKB = T // 128
    GW = (NI - 1) * 1024 + 896 + 512
    dq = C.din("dq", [128, NQ])
    dk = C.din("dk", [128, T])
    dv = C.din("dv", [T, 128])
    dcst = C.din("dcst", [128, 8])
    dlam = C.din("dlam", [128, 256])
    ya = C.dout("ya", [128, NQ])

    qT = C.sb("d_qT", [128, NQ], BF16)
    kT = [C.sb(f"d_kT{m}", [128, T], BF16) for m in range(2)]
    V = C.sb("d_V", [128, NKB, 128], BF16)
    cst = C.sb("d_cst", [128, 8])
    lam = C.sb("d_lam", [128, 256])
    lt = C.sb("d_lt", [128, 128])
    sc = C.sb("d_sc", [128, 8])
    GC = GW // 4
    Gi = C.sb("d_Gi", [128, GC], mybir.dt.int32)
    G = C.sb("d_G", [128, GW])
    Gm = C.sb("d_Gm", [128, GC])
    ones_b = C.sb("d_ones_b", [128, 128], BF16)
    ones_f = C.sb("d_ones_f", [128, 128])
    E = [[C.sb(f"d_E{m}{b}", [128, 512]) for b in range(2)] for m in range(2)]
    Pm = [[C.sb(f"d_P{m}{b}", [128, 512], BF16) for b in range(2)] for m in range(2)]
    t1 = C.sb("d_t1", [128, 512])
    t2 = C.sb("d_t2", [128, 512])
    t3 = C.sb("d_t3", [128, 512])
    ob = [C.sb(f"d_ob{b}", [128, 512]) for b in range(2)]

    P.dma("sp", lambda e: e.dma_start(out=cst[:], in_=dcst), w=["d_cst"])
    P.dma("sp", lambda e: e.dma_start(out=lam[:], in_=dlam), w=["d_lam"])
    P.dma("pool", lambda e: e.dma_start(out=qT[:], in_=dq), w=["d_qT"])
    for m in range(2):
        P.pool(lambda e, m=m: e.memset(kT[m][:], 0.0), w=[("d_kT", h // 2048) for h in range(0, T, 2048)])
    for h in range(0, T, 2048):
        for m in range(2):
            rows = slice(m * 64, (m + 1) * 64)
            P.dma("pool", lambda e, h=h, m=m, rows=rows: e.dma_start(out=kT[m][rows, h:h + 2048], in_=dk[rows, h:h + 2048]),
                  w=[("d_kT", h // 2048)])
    dv_v = dv.rearrange("(n p) d -> p n d", p=128)
    for h in range(0, NKB, 16):
        P.dma("pool", lambda e, h=h: e.dma_start(out=V[:, h:h + 16, :], in_=dv_v[:, h:h + 16, :]), w=[("d_V", h // 16)])
    P.dve(lambda e: e.memset(ones_b[:], 1.0), w=["d_ones_b"])
    P.dve(lambda e: e.memset(ones_f[:], 1.0), w=["d_ones_f"])
    P.dve(lambda e: e.tensor_tensor(out=lt[:, 0:64], in0=lam[:, 0:64], in1=lam[:, 64:128], op=ALU.mult), r=["d_lam"], w=["d_lt"])
    P.dve(lambda e: e.tensor_tensor(out=lt[:, 64:128], in0=lam[:, 128:192], in1=lam[:, 192:256], op=ALU.mult), r=["d_lam"], w=["d_lt"])
    P.dve(lambda e: e.reduce_sum(out=sc[:, 2:3], in_=lt[:, 0:64], axis=AX.X), r=["d_lt"], w=["d_sc"])
    P.dve(lambda e: e.reduce_sum(out=sc[:, 3:4], in_=lt[:, 64:128], axis=AX.X), r=["d_lt"], w=["d_sc"])
    P.act(lambda e: e.activation(out=sc[:, 4:6], in_=sc[:, 2:4], func=AF.Exp), r=["d_sc"], w=["d_sc"])
    P.dve(lambda e: e.tensor_tensor(out=sc[:, 6:7], in0=sc[:, 5:6], in1=sc[:, 4:5], op=ALU.subtract), r=["d_sc"], w=["d_sc"])
    P.dve(lambda e: e.tensor_tensor(out=sc[:, 0:1], in0=sc[:, 6:7], in1=cst[:, 2:3], op=ALU.subtract), r=["d_sc", "d_cst"], w=["d_sc"])
    P.dve(lambda e: e.tensor_scalar(out=sc[:, 7:8], in0=cst[:, 2:3], scalar1=-1.0, scalar2=1.0, op0=ALU.mult, op1=ALU.add), r=["d_cst", "d_sc"], w=["d_sc"])
    P.dve(lambda e: e.tensor_tensor(out=sc[:, 1:2], in0=sc[:, 7:8], in1=cst[:, 3:4], op=ALU.mult), r=["d_sc", "d_cst"], w=["d_sc"])
    for gq in range(4):
        gs = slice(gq * GC, (gq + 1) * GC)
        P.pool("iota", Gi[:], pattern=[[1, GC]], base=-896 + gq * GC, channel_multiplier=-1, w=["d_Gi"])
        P.dve("tensor_copy", out=G[:, gs], in_=Gi[:], r=["d_Gi"], w=["d_G"])
        P.dve("tensor_scalar", out=G[:, gs], in0=G[:, gs], scalar1=cst[:, 1:2], scalar2=None, op0=ALU.add, r=["d_G", "d_cst"], w=["d_G"])
        P.dve("tensor_scalar", out=Gm[:], in0=G[:, gs], scalar1=0.0, scalar2=None, op0=ALU.is_ge, r=["d_G"], w=["d_Gm"])
        P.dve("tensor_scalar", out=G[:, gs], in0=G[:, gs], scalar1=0.0, scalar2=None, op0=ALU.max, r=["d_G", "d_Gm"], w=["d_G"])
        P.act("activation", out=G[:, gs], in_=G[:, gs], func=AF.Exp, scale=cst[:, 0:1], r=["d_G", "d_cst"], w=["d_G"])
        P.dve("tensor_tensor", out=G[:, gs], in0=G[:, gs], in1=Gm[:], op=ALU.mult, r=["d_G", "d_Gm"], w=["d_G"])

    if DEBUG:
        dbgG = C.dout("dbgG", [128, GW])
        dbgsc = C.dout("dbgsc", [128, 8])
        dbgE = C.dout("dbgE", [128, 512])
        dbgP = C.dout("dbgP", [128, 512], BF16)
        dbgt = C.dout("dbgt", [128, 512])
        P.dma("sp", lambda e: e.dma_start(out=dbgG, in_=G[:]), r=["d_G"])
        P.dma("sp", lambda e: e.dma_start(out=dbgsc, in_=sc[:]), r=["d_sc"])
        dbgq = C.dout("dbgq", [128, NQ], BF16)
        dbgk = C.dout("dbgk", [128, T], BF16)
        dbgS = C.dout("dbgS", [128, 512])
        P.dma("sp", lambda e: e.dma_start(out=dbgq, in_=qT[:]), r=["d_qT"])
        P.dma("sp", lambda e: e.dma_start(out=dbgk, in_=kT[0][:]), r=[("d_kT", h // 2048) for h in range(0, T, 2048)])
    bc = 0
    for i in range(NI):
        nkb = 8 * i + 8
        qs = slice(i * 512, (i + 1) * 512)
        for kb in range(nkb):
            b = bc % 2
            bc += 1
            uu0 = 1024 * i - 128 * kb + 896
            for m in range(2):
                rows = slice(m * 64, (m + 1) * 64)
                P.pe(lambda e, m=m, b=b, rows=rows, kb=kb, qs=qs: e.matmul(
                    PS[m * 2 + b][:], lhsT=kT[m][:, kb * 128:(kb + 1) * 128], rhs=qT[:, qs], start=True, stop=True),
                    r=["d_qT", ("d_kT", kb // 16)], w=[("ps", m * 2 + b)])
            for m in range(2):
                P.act(lambda e, m=m, b=b: e.activation(out=E[m][b][:], in_=PS[m * 2 + b][:], func=AF.Exp, scale=0.125),
                      r=[("ps", m * 2 + b)], w=[("d_E", m, b)])
                (P.dve if m == 0 else P.pool)("tensor_tensor", out=Pm[m][b][:], in0=E[m][b][:], in1=G[:, uu0:uu0 + 512], op=ALU.mult,
                                              r=[("d_E", m, b), "d_G"], w=[("d_P", m, b)])
            if DEBUG and i == 0 and kb == 0:
                P.act(lambda e: e.activation(out=t3[:], in_=PS[0][:], func=AF.Copy), r=[("ps", 0)], w=["d_t3"])
                P.dma("sp", lambda e: e.dma_start(out=dbgS, in_=t3[:]), r=["d_t3"])
                P.dma("sp", lambda e: e.dma_start(out=dbgE, in_=E[0][0][:]), r=[("d_E", 0, 0)])
                P.dma("sp", lambda e: e.dma_start(out=dbgP, in_=Pm[0][0][:]), r=[("d_P", 0, 0)])
            for m in range(2):
                P.pe(lambda e, m=m, b=b, kb=kb, nkb=nkb: e.matmul(
                    PS[4 + m][:], lhsT=V[:, kb, :], rhs=Pm[m][b][:], start=(kb == 0), stop=(kb == nkb - 1)),
                    r=[("d_V", kb // 16), ("d_P", m, b)], w=[("ps", 4 + m)])
                P.pe(lambda e, m=m, b=b, kb=kb, nkb=nkb: e.matmul(
                    PS[6 + m][:], lhsT=ones_b[:], rhs=Pm[m][b][:], start=(kb == 0), stop=(kb == nkb - 1)),
                    r=["d_ones_b", ("d_P", m, b)], w=[("ps", 6 + m)])
        P.dve(lambda e: e.reciprocal(out=t1[:], in_=PS[6][:]), r=[("ps", 6)], w=["d_t1"])
        P.dve(lambda e: e.tensor_tensor(out=t1[:], in0=t1[:], in1=PS[4][:], op=ALU.mult), r=["d_t1", ("ps", 4)], w=["d_t1"])
        P.dve(lambda e: e.reciprocal(out=t2[:], in_=PS[7][:]), r=[("ps", 7)], w=["d_t2"])
        P.dve(lambda e: e.tensor_tensor(out=t2[:], in0=t2[:], in1=PS[5][:], op=ALU.mult), r=["d_t2", ("ps", 5)], w=["d_t2"])
        P.dve(lambda e: e.scalar_tensor_tensor(out=t1[:], in0=t2[:], scalar=sc[:, 0:1], in1=t1[:], op0=ALU.mult, op1=ALU.add),
              r=["d_t1", "d_t2", "d_sc"], w=["d_t1"])
        if DEBUG and i == 0:
            P.dma("sp", lambda e: e.dma_start(out=dbgt, in_=t1[:]), r=["d_t1"])
        P.act(lambda e: e.activation(out=t3[:], in_=t1[:], func=AF.Square), r=["d_t1"], w=["d_t3"])
        P.pe(lambda e: e.matmul(PS[6][:], lhsT=ones_f[:], rhs=t3[:], start=True, stop=True), r=["d_ones_f", "d_t3"], w=[("ps", 6)])
        P.dve(lambda e: e.tensor_scalar(out=t2[:], in0=PS[6][:], scalar1=1.0 / 128, scalar2=NORM_EPS, op0=ALU.mult, op1=ALU.add),
              r=[("ps", 6)], w=["d_t2"])
        P.act(lambda e: e.activation(out=t2[:], in_=t2[:], func=AF.Sqrt), r=["d_t2"], w=["d_t2"])
        P.dve(lambda e: e.reciprocal(out=t2[:], in_=t2[:]), r=["d_t2"], w=["d_t2"])
        o = ob[i % 2]
        P.dve(lambda e, o=o: e.scalar_tensor_tensor(out=o[:], in0=t1[:], scalar=sc[:, 1:2], in1=t2[:], op0=ALU.mult, op1=ALU.mult),
              r=["d_t1", "d_t2", "d_sc"], w=[("d_ob", i % 2)])
        P.dma("sp", lambda e, o=o, qs=qs: e.dma_start(out=ya[:, qs], in_=o[:]), r=[("d_ob", i % 2)])


def emit_swa(P, C, PS, T):
    nc = C.nc
    NB = T // 128
    NG = T // 512
    sq = C.din("sq", [64, T])
    sk = C.din("sk", [64, T])
    sv = C.din("sv", [T, 64])
    scst = C.din("scst", [128, 8])
    yb = C.dout("yb", [64, T])
    qT = C.sb("s_qT", [128, T], BF16)
    kT = C.sb("s_kT", [128, T], BF16)
    V = C.sb("s_V", [128, NB, 128], BF16)
    cst = C.sb("s_cst", [128, 8])
    Gi = C.sb("s_Gi", [128, 256], mybir.dt.int32)
    G = C.sb("s_G", [128, 4, 256])
    Gm = C.sb("s_Gm", [128, 256])
    ones_b = C.sb("s_ones_b", [128, 128], BF16)
    E = [C.sb(f"s_E{b}", [128, 1024]) for b in range(2)]
    Pb = [C.sb(f"s_P{b}", [128, 1024], BF16) for b in range(2)]
    t1 = C.sb("s_t1", [64, 512])
    ob = [C.sb(f"s_ob{b}", [64, 512]) for b in range(2)]

    P.dma("sp", lambda e: e.dma_start(out=cst[:], in_=scst), w=["s_cst"])
    P.pool(lambda e: e.memset(qT[:], 0.0), w=["s_qT"])
    P.pool(lambda e: e.memset(kT[:], 0.0), w=["s_kT"])
    P.pool(lambda e: e.memset(V[:], 0.0), w=["s_V"])
    P.dma("pool", lambda e: e.dma_start(out=qT[0:64, :], in_=sq), w=["s_qT"])
    P.dma("pool", lambda e: e.dma_start(out=kT[0:64, :], in_=sk), w=["s_kT"])
    sv_v = sv.rearrange("(n p) d -> p n d", p=128)
    P.dma("pool", lambda e: e.dma_start(out=V[:, :, 0:64], in_=sv_v), w=["s_V"])
    P.dve(lambda e: e.memset(ones_b[:], 1.0), w=["s_ones_b"])
    P.pool(lambda e: e.iota(Gi[:, 0:128], pattern=[[1, 128]], base=128, channel_multiplier=-1), w=["s_Gi"])
    P.pool(lambda e: e.iota(Gi[:, 128:256], pattern=[[1, 128]], base=0, channel_multiplier=-1), w=["s_Gi"])
    g0 = G[:, 0, :]
    P.dve(lambda e: e.tensor_copy(out=g0, in_=Gi[:]), r=["s_Gi"], w=["s_G"])
    P.dve(lambda e: e.tensor_scalar(out=Gm[:, 0:128], in0=G[:, 0, 0:128], scalar1=127.0, scalar2=None, op0=ALU.is_le), r=["s_G"], w=["s_Gm"])
    P.dve(lambda e: e.tensor_scalar(out=Gm[:, 128:256], in0=G[:, 0, 128:256], scalar1=0.0, scalar2=None, op0=ALU.is_ge), r=["s_G"], w=["s_Gm"])
    P.dve(lambda e: e.tensor_scalar(out=g0, in0=g0, scalar1=0.0, scalar2=None, op0=ALU.max), r=["s_G", "s_Gm"], w=["s_G"])
    P.act(lambda e: e.activation(out=g0, in_=g0, func=AF.Exp, scale=cst[:, 0:1]), r=["s_G", "s_cst"], w=["s_G"])
    P.dve(lambda e: e.tensor_tensor(out=g0, in0=g0, in1=Gm[:], op=ALU.mult), r=["s_G", "s_Gm"], w=["s_G"])
    for j in range(1, 4):
        P.dve(lambda e, j=j: e.tensor_copy(out=G[:, j, :], in_=g0), r=["s_G"], w=["s_G"])
    P.act(lambda e: e.activation(out=cst[:, 2:3], in_=cst[:, 1:2], func=AF.Exp), r=["s_cst"], w=["s_cst"])
    Gf = G[:].rearrange("p a b -> p (a b)")
    for gi in range(NG):
        b2 = gi % 2
        pS = [PS[b2 * 2], PS[b2 * 2 + 1]]
        kS = [("ps", b2 * 2), ("ps", b2 * 2 + 1)]
        for bb in range(4):
            n = gi * 4 + bb
            half = bb // 2
            c0 = (bb % 2) * 256
            npv = max(n - 1, 0)
            P.pe(lambda e, n=n, npv=npv, half=half, c0=c0, pS=pS: e.matmul(
                pS[half][:, c0:c0 + 128], lhsT=kT[:, npv * 128:(npv + 1) * 128], rhs=qT[:, n * 128:(n + 1) * 128], start=True, stop=True),
                r=["s_qT", "s_kT"], w=[kS[half]])
            P.pe(lambda e, n=n, half=half, c0=c0, pS=pS: e.matmul(
                pS[half][:, c0 + 128:c0 + 256], lhsT=kT[:, n * 128:(n + 1) * 128], rhs=qT[:, n * 128:(n + 1) * 128], start=True, stop=True),
                r=["s_qT", "s_kT"], w=[kS[half]])
        for half in range(2):
            P.act(lambda e, half=half, b2=b2, pS=pS: e.activation(out=E[b2][:, half * 512:(half + 1) * 512], in_=pS[half][:], func=AF.Exp, scale=0.125),
                  r=[kS[half]], w=[("s_E", b2)])
        P.dve(lambda e, b2=b2: e.tensor_tensor(out=Pb[b2][:], in0=E[b2][:], in1=Gf, op=ALU.mult), r=[("s_E", b2), "s_G"], w=[("s_P", b2)])
        if gi == 0:
            P.dve(lambda e: e.memset(Pb[0][:, 0:128], 0.0), r=[("s_P", 0)], w=[("s_P", 0)])
        po, pd = PS[4 + b2], PS[6 + b2]
        for bb in range(4):
            n = gi * 4 + bb
            npv = max(n - 1, 0)
            cs = slice(bb * 128, (bb + 1) * 128)
            pa = slice(bb * 256, bb * 256 + 128)
            pb_ = slice(bb * 256 + 128, bb * 256 + 256)
            P.pe(lambda e, npv=npv, cs=cs, pa=pa, po=po, b2=b2: e.matmul(po[:, cs], lhsT=V[:, npv, :], rhs=Pb[b2][:, pa], start=True, stop=False),
                 r=["s_V", ("s_P", b2)], w=[("ps", 4 + b2)])
            P.pe(lambda e, n=n, cs=cs, pb_=pb_, po=po, b2=b2: e.matmul(po[:, cs], lhsT=V[:, n, :], rhs=Pb[b2][:, pb_], start=False, stop=True),
                 r=["s_V", ("s_P", b2)], w=[("ps", 4 + b2)])
            P.pe(lambda e, cs=cs, pa=pa, pd=pd, b2=b2: e.matmul(pd[:, cs], lhsT=ones_b[:], rhs=Pb[b2][:, pa], start=True, stop=False),
                 r=["s_ones_b", ("s_P", b2)], w=[("ps", 6 + b2)])
            P.pe(lambda e, cs=cs, pb_=pb_, pd=pd, b2=b2: e.matmul(pd[:, cs], lhsT=ones_b[:], rhs=Pb[b2][:, pb_], start=False, stop=True),
                 r=["s_ones_b", ("s_P", b2)], w=[("ps", 6 + b2)])
        P.dve(lambda e, pd=pd: e.tensor_scalar(out=t1[:], in0=pd[0:64, :], scalar1=cst[0:64, 2:3], scalar2=None, op0=ALU.add),
              r=[("ps", 6 + b2), "s_cst"], w=["s_t1"])
        P.dve(lambda e: e.reciprocal(out=t1[:], in_=t1[:]), r=["s_t1"], w=["s_t1"])
        o = ob[b2]
        P.dve(lambda e, o=o, po=po: e.tensor_tensor(out=o[:], in0=t1[:], in1=po[0:64, :], op=ALU.mult), r=["s_t1", ("ps", 4 + b2)], w=[("s_ob", b2)])
        P.dma("sp", lambda e, o=o, gi=gi: e.dma_start(out=yb[:, gi * 512:(gi + 1) * 512], in_=o[:]), r=[("s_ob", b2)])


def build_mix(T, parts=("diff", "swa", "rwkv")):
    nc = bass.Bass("TRN2", target_bir_lowering=False)
    P = Prog(nc)
    import contextlib
    with contextlib.ExitStack() as es0:
        PS = [es0.enter_context(nc.psum_tensor(f"psb{i}", [128, 512], F32)) for i in range(8)]
        for part in parts:
            with contextlib.ExitStack() as es:
                C = _Ctx(nc, es)
                if part == "diff":
                    emit_diff(P, C, PS, T)
                elif part == "swa":
                    emit_swa(P, C, PS, T)
                elif part == "rwkv":
                    emit_rwkv(P, C, PS, T)
                P.barrier()
        P.emit()
    return nc


def emit_rwkv(P, C, PS, T, SEG=512):
    nc = C.nc
    I32 = mybir.dt.int32
    NSEG = T // SEG
    NSC = SEG // 128
    rin = {nm: C.din(nm, [128, T]) for nm in ("rr", "rk", "rv")}
    rin["rwl"] = C.din("rwl", [96, T])
    rin["ral"] = C.din("ral", [96, T])
    rgl = C.din("rgl", [256, T])
    rmu = C.din("rmu", [128, 8])
    rw2 = C.din("rw2", [96, 128])
    ra2 = C.din("ra2", [96, 128])
    rg2 = C.din("rg2", [256, 128])
    rcst = C.din("rcst", [128, 8])
    rlnw = C.din("rlnw", [128, 128])
    rlnb = C.din("rlnb", [128, 128])
    yc = C.dout("yc", [T, 128])

    def sb(name, shape, dt=F32):
        return C.sb("r_" + name, shape, dt)

    mu = sb("mu", [128, 8]); cst = sb("cst", [128, 8])
    w2 = sb("w2", [128, 128]); a2 = sb("a2", [128, 128]); g2 = sb("g2", [128, 2, 128])
    lnw = sb("lnw", [128, 128]); lnb = sb("lnb", [128, 128])
    di = sb("di", [128, 128], I32); dfl = sb("dfl", [128, 128])
    ident = sb("ident", [128, 128]); BD = sb("BD", [128, 128])
    mask4 = sb("mask4", [128, 512]); Ms = sb("Ms", [128, 128])
    hm = sb("hm", [128, 2]); rmask = sb("rmask", [128, SEG])
    raw = {nm: sb("raw_" + nm, [128, SEG + 1]) for nm in ("rr", "rk", "rv", "rwl", "ral", "g0", "g1")}
    dtmp = sb("dtmp", [128, SEG])
    f = {nm: sb("f_" + nm, [128, SEG]) for nm in ("rr", "rk", "rv", "rwl", "ral", "g0", "g1")}
    ld = sb("ld", [128, SEG]); cum = sb("cum", [128, SEG]); aic = sb("aic", [128, SEG])
    kk = sb("kk", [128, SEG]); bvec = sb("bvec", [128, SEG]); kpr = sb("kpr", [128, SEG])
    sq = sb("sq", [128, SEG])
    W = sb("W", [128, SEG]); Wp = sb("Wp", [128, SEG]); Wi = sb("Wi", [128, SEG]); Wh = sb("Wh", [128, SEG])
    rt = [sb(f"rt{h}", [128, SEG]) for h in range(2)]
    at = [sb(f"at{h}", [128, SEG]) for h in range(2)]
    bt = [sb(f"bt{h}", [128, SEG]) for h in range(2)]
    kt = [sb(f"kt{h}", [128, SEG]) for h in range(2)]
    bh = sb("bh", [128, SEG]); kh = sb("kh", [128, SEG]); pb = sb("pb", [128, SEG])
    gtm = sb("gtm", [128, NSC, 128])
    tm3 = sb("tm3", [128, 4, 128])
    Vz = [sb(f"Vz{c}", [128, 128]) for c in range(2)]
    bon = sb("bon", [128, 2])
    Amat = [sb(f"Amat{h}", [128, 512]) for h in range(2)]
    Pk = [[sb(f"Pk{h}{i}", [128, 256]) for i in range(2)] for h in range(2)]
    TTm = [[sb(f"TT{h}{i}", [128, 128]) for i in range(2)] for h in range(2)]
    ST = sb("ST", [128, 64])
    Xs = [sb(f"Xs{h}", [128, 64]) for h in range(2)]
    Uz = [[sb(f"Uz{h}{c}", [128, 64]) for c in range(2)] for h in range(2)]
    ytm = sb("ytm", [128, 128]); ysq = sb("ysq", [128, 128])
    st4 = sb("st4", [128, 8])
    yo = [sb(f"yo{i}", [128, 128]) for i in range(2)]

    P.dma("sp", out=mu[:], in_=rmu, w=["mu"])
    P.dma("sp", out=cst[:], in_=rcst, w=["cst"])
    P.dve("memset", w2[:], 0.0, w=["w2"]); P.dve("memset", a2[:], 0.0, w=["a2"])
    P.dma("sp", out=w2[0:96, :], in_=rw2, w=["w2"])
    P.dma("sp", out=a2[0:96, :], in_=ra2, w=["a2"])
    P.dma("sp", out=g2[:], in_=rg2.rearrange("(c p) n -> p c n", p=128), w=["g2"])
    P.dma("sp", out=lnw[:], in_=rlnw, w=["lnw"])
    P.dma("sp", out=lnb[:], in_=rlnb, w=["lnb"])
    P.pool("iota", di[:], pattern=[[1, 128]], base=0, channel_multiplier=-1, w=["di"])
    P.dve("tensor_copy", out=dfl[:], in_=di[:], r=["di"], w=["dfl"])
    P.dve("tensor_scalar", out=ident[:], in0=dfl[:], scalar1=0.0, scalar2=None, op0=ALU.is_equal, r=["dfl"], w=["ident"])
    P.dve("memset", BD[:], 0.0, w=["BD"])
    P.dve("memset", BD[0:64, 0:64], 1.0, w=["BD"])
    P.dve("memset", BD[64:128, 64:128], 1.0, w=["BD"])
    for q in range(4):
        P.dve("tensor_scalar", out=mask4[:, q * 128:(q + 1) * 128], in0=dfl[:], scalar1=0.0, scalar2=None,
              op0=(ALU.is_gt if q % 2 == 0 else ALU.is_ge), r=["dfl"], w=["mask4"])
        P.dve("tensor_tensor", out=mask4[:, q * 128:(q + 1) * 128], in0=mask4[:, q * 128:(q + 1) * 128], in1=BD[:], op=ALU.mult,
              r=["mask4", "BD"], w=["mask4"])
    P.dve("tensor_scalar", out=Ms[:], in0=dfl[:], scalar1=0.0, scalar2=None, op0=ALU.is_lt, r=["dfl"], w=["Ms"])
    P.dve("tensor_tensor", out=Ms[:], in0=Ms[:], in1=BD[:], op=ALU.mult, r=["Ms", "BD"], w=["Ms"])
    P.dve("memset", hm[:], 0.0, w=["hm"])
    P.dve("memset", hm[0:64, 0:1], 1.0, w=["hm"])
    P.dve("memset", hm[64:128, 1:2], 1.0, w=["hm"])
    P.dve("memset", rmask[:], 1.0, w=["rmask"])
    P.dve("memset", rmask[:].rearrange("p (c t) -> p c t", t=64)[:, :, 0:1], 0.0, w=["rmask"])
    P.dve("memset", ST[:], 0.0, w=["ST"])
    for h in range(2):
        for c in range(2):
            P.dve("memset", Uz[h][c][:], 0.0, w=[("Uz", h, c)])
    for c in range(2):
        P.dve("memset", Vz[c][:], 0.0, w=[("Vz", c)])
    for nm in raw:
        P.dve("memset", raw[nm][:], 0.0, w=[("raw", nm)])
    for nm in f:
        P.dve("memset", f[nm][:], 0.0, w=[("f", nm)])

    W2 = [W, sb("W_b", [128, SEG])]
    frv2 = [f["rv"], sb("f_rv_b", [128, SEG])]
    gtm2 = [gtm, sb("gtm_b", [128, NSC, 128])]
    rt2 = [rt, [sb(f"rt{h}_b", [128, SEG]) for h in range(2)]]
    at2 = [at, [sb(f"at{h}_b", [128, SEG]) for h in range(2)]]
    bt2 = [bt, [sb(f"bt{h}_b", [128, SEG]) for h in range(2)]]
    kt2 = [kt, [sb(f"kt{h}_b", [128, SEG]) for h in range(2)]]
    bh2 = [bh, sb("bh_b", [128, SEG])]
    kh2 = [kh, sb("kh_b", [128, SEG])]
    pb2 = [pb, sb("pb_b", [128, SEG])]
    tm3_2 = [tm3, sb("tm3_b", [128, 4, 128])]
    Vz2 = [Vz, [sb(f"Vz{c}_b", [128, 128]) for c in range(2)]]
    bon2 = [bon, sb("bon_b", [128, 2])]
    Amat2 = [Amat, [sb(f"Amat{h}_b", [128, 512]) for h in range(2)]]
    Pk2 = [Pk, [[sb(f"Pk{h}{i}_b", [128, 256]) for i in range(2)] for h in range(2)]]
    TTm2 = [TTm, [[sb(f"TT{h}{i}_b", [128, 128]) for i in range(2)] for h in range(2)]]
    for c in range(2):
        P.dve("memset", Vz2[1][c][:], 0.0, w=[("Vz", 1, c)])
    P.dve("memset", frv2[1][:], 0.0, w=[("f", "rv", 1)])

    srcs = {"rr": (rin["rr"], 128), "rk": (rin["rk"], 128), "rv": (rin["rv"], 128), "rwl": (rin["rwl"], 96),
            "ral": (rin["ral"], 96), "g0": (rgl[0:128, :], 128), "g1": (rgl[128:256, :], 128)}
    mucol = {"rr": 0, "rk": 1, "rv": 2, "rwl": 3, "ral": 4, "g0": 5, "g1": 6}
    B7 = PS[7]

    def seg_prep(sg):
        sp = sg % 2
        t0 = sg * SEG
        W = W2[sp]; gtm = gtm2[sp]; rt = rt2[sp]; at = at2[sp]; bt = bt2[sp]; kt = kt2[sp]
        bh = bh2[sp]; kh = kh2[sp]; pb = pb2[sp]
        fl = dict(f); fl["rv"] = frv2[sp]
        fk = lambda nm: ("f", nm, sp) if nm == "rv" else ("f", nm)
        for nm, (src_, rows) in srcs.items():
            if sg == 0:
                P.dma("sp", out=raw[nm][0:rows, 1:SEG + 1], in_=src_[:, 0:SEG], w=[("raw", nm)])
            else:
                P.dma("sp", out=raw[nm][0:rows, :], in_=src_[:, t0 - 1:t0 + SEG], w=[("raw", nm)])
            yield
            P.dve("tensor_tensor", out=dtmp[0:rows, :], in0=raw[nm][0:rows, 0:SEG], in1=raw[nm][0:rows, 1:SEG + 1], op=ALU.subtract,
                  r=[("raw", nm)], w=["dtmp"])
            yield
            P.dve("scalar_tensor_tensor", out=fl[nm][0:rows, :], in0=dtmp[0:rows, :], scalar=mu[0:rows, mucol[nm]:mucol[nm] + 1],
                  in1=raw[nm][0:rows, 1:SEG + 1], op0=ALU.mult, op1=ALU.add, r=["dtmp", "mu", ("raw", nm)], w=[fk(nm)])
            yield
        P.act("activation", out=f["rwl"][0:96, :], in_=f["rwl"][0:96, :], func=AF.Tanh, r=[("f", "rwl")], w=[("f", "rwl")])
        P.pe("matmul", B7[:, 0:SEG], lhsT=w2[:], rhs=f["rwl"][:], start=True, stop=True, r=["w2", ("f", "rwl")], w=[("ps", 7)])
        yield
        P.act("activation", out=ld[:], in_=B7[:, 0:SEG], func=AF.Sigmoid, bias=cst[:, 0:1], r=[("ps", 7), "cst"], w=["ld"])
        P.dve("tensor_scalar", out=ld[:], in0=ld[:], scalar1=-math.exp(-0.5), scalar2=None, op0=ALU.mult, r=["ld"], w=["ld"])
        yield
        P.pe("matmul", B7[:, 0:SEG], lhsT=a2[:], rhs=f["ral"][:], start=True, stop=True, r=["a2", ("f", "ral")], w=[("ps", 7)])
        P.act("activation", out=aic[:], in_=B7[:, 0:SEG], func=AF.Sigmoid, bias=cst[:, 1:2], r=[("ps", 7), "cst"], w=["aic"])
        yield
        for c in range(2):
            nm = f"g{c}"
            P.act("activation", out=f[nm][:], in_=f[nm][:], func=AF.Sigmoid, r=[("f", nm)], w=[("f", nm)])
            yield
        for j in range(NSC):
            js = slice(j * 128, (j + 1) * 128)
            for c in range(2):
                P.pe("matmul", B7[:, js], lhsT=f[f"g{c}"][:, js], rhs=g2[:, c, :], start=(c == 0), stop=(c == 1),
                     r=[("f", f"g{c}"), "g2"], w=[("ps", 7)])
            yield
        P.act("activation", out=gtm[:].rearrange("p a b -> p (a b)"), in_=B7[:, 0:SEG], func=AF.Copy, r=[("ps", 7)], w=[("gtm", sp)])
        yield
        P.dve("tensor_scalar", out=kk[:], in0=f["rk"][:], scalar1=cst[:, 2:3], scalar2=None, op0=ALU.mult, r=[("f", "rk"), "cst"], w=["kk"])
        P.act("activation", out=sq[:], in_=kk[:], func=AF.Square, r=["kk"], w=["sq"])
        yield
        P.pe("matmul", B7[:, 0:SEG], lhsT=BD[:], rhs=sq[:], start=True, stop=True, r=["BD", "sq"], w=[("ps", 7)])
        P.act("activation", out=sq[:], in_=B7[:, 0:SEG], func=AF.Sqrt, r=[("ps", 7)], w=["sq"])
        yield
        P.dve("tensor_scalar", out=sq[:], in0=sq[:], scalar1=1e-12, scalar2=None, op0=ALU.max, r=["sq"], w=["sq"])
        yield
        P.dve("reciprocal", out=sq[:], in_=sq[:], r=["sq"], w=["sq"])
        yield
        P.dve("tensor_tensor", out=kk[:], in0=kk[:], in1=sq[:], op=ALU.mult, r=["kk", "sq"], w=["kk"])
        yield
        P.dve("tensor_scalar", out=kpr[:], in0=aic[:], scalar1=-1.0, scalar2=cst[:, 3:4], op0=ALU.add, op1=ALU.mult, r=["aic", "cst"], w=["kpr"])
        yield
        P.dve("scalar_tensor_tensor", out=kpr[:], in0=kpr[:], scalar=1.0, in1=f["rk"][:], op0=ALU.add, op1=ALU.mult,
              r=["kpr", ("f", "rk")], w=["kpr"])
        yield
        P.dve("tensor_tensor", out=bvec[:], in0=kk[:], in1=aic[:], op=ALU.mult, r=["kk", "aic"], w=["bvec"])
        yield
        P.dve("tensor_tensor_scan", out=cum[:], data0=rmask[:], data1=ld[:], initial=0.0, op0=ALU.mult, op1=ALU.add,
              r=["rmask", "ld"], w=["cum"])
        yield
        P.act("activation", out=W[:], in_=cum[:], func=AF.Exp, r=["cum"], w=[("W", sp)])
        P.act("activation", out=Wi[:], in_=cum[:], func=AF.Exp, scale=-1.0, r=["cum"], w=["Wi"])
        yield
        P.dve("tensor_tensor", out=Wp[:], in0=cum[:], in1=ld[:], op=ALU.subtract, r=["cum", "ld"], w=["Wp"])
        P.act("activation", out=Wp[:], in_=Wp[:], func=AF.Exp, r=["Wp"], w=["Wp"])
        yield
        for c in range(SEG // 64):
            cs = slice(c * 64, (c + 1) * 64)
            P.dve("tensor_scalar", out=Wh[:, cs], in0=cum[:, cs], scalar1=-1.0, scalar2=cum[:, c * 64 + 63:c * 64 + 64],
                  op0=ALU.mult, op1=ALU.add, r=["cum"], w=["Wh"])
            if c % 2 == 1:
                yield
        P.act("activation", out=Wh[:], in_=Wh[:], func=AF.Exp, r=["Wh"], w=["Wh"])
        yield
        for h in range(2):
            hc = hm[:, h:h + 1]
            P.dve("scalar_tensor_tensor", out=rt[h][:], in0=fl["rr"][:], scalar=hc, in1=W[:], op0=ALU.mult, op1=ALU.mult,
                  r=[("f", "rr"), "hm", ("W", sp)], w=[("rt", h, sp)])
            yield
            P.dve("scalar_tensor_tensor", out=at[h][:], in0=kk[:], scalar=hc, in1=Wp[:], op0=ALU.mult, op1=ALU.mult,
                  r=["kk", "hm", "Wp"], w=[("at", h, sp)])
            yield
            P.dve("tensor_scalar", out=at[h][:], in0=at[h][:], scalar1=-1.0, scalar2=None, op0=ALU.mult, r=[("at", h, sp)], w=[("at", h, sp)])
            yield
            P.dve("scalar_tensor_tensor", out=bt[h][:], in0=bvec[:], scalar=hc, in1=Wi[:], op0=ALU.mult, op1=ALU.mult,
                  r=["bvec", "hm", "Wi"], w=[("bt", h, sp)])
            yield
            P.dve("scalar_tensor_tensor", out=kt[h][:], in0=kpr[:], scalar=hc, in1=Wi[:], op0=ALU.mult, op1=ALU.mult,
                  r=["kpr", "hm", "Wi"], w=[("kt", h, sp)])
            yield
        P.dve("tensor_tensor", out=bh[:], in0=bvec[:], in1=Wh[:], op=ALU.mult, r=["bvec", "Wh"], w=[("bh", sp)])
        yield
        P.dve("tensor_tensor", out=kh[:], in0=kpr[:], in1=Wh[:], op=ALU.mult, r=["kpr", "Wh"], w=[("kh", sp)])
        yield
        P.dve("scalar_tensor_tensor", out=pb[:], in0=fl["rr"][:], scalar=cst[:, 4:5], in1=kpr[:], op0=ALU.mult, op1=ALU.mult,
              r=[("f", "rr"), "cst", "kpr"], w=[("pb", sp)])
        yield

    def sc_prep(gidx):
        sg, j = divmod(gidx, NSC)
        sp, q = sg % 2, gidx % 2
        js = slice(j * 128, (j + 1) * 128)
        rt = rt2[sp]; at = at2[sp]; bt = bt2[sp]; kt = kt2[sp]
        tm3 = tm3_2[q]; Vz = Vz2[q]; bon = bon2[q]; Amat = Amat2[q]; Pk = Pk2[q]; TTm = TTm2[q]
        B0 = PS[0]
        for i3, (src_, key) in enumerate(((bh2[sp], ("bh", sp)), (kh2[sp], ("kh", sp)), (frv2[sp], ("f", "rv", sp)), (pb2[sp], ("pb", sp)))):
            P.pe("matmul", B0[:, i3 * 128:(i3 + 1) * 128], lhsT=src_[:, js], rhs=ident[:], start=True, stop=True, r=[key, "ident"], w=[("ps", 0)])
        yield
        P.act("activation", out=tm3[:].rearrange("p a b -> p (a b)"), in_=B0[:], func=AF.Copy, r=[("ps", 0)], w=[("tm3", q)])
        yield
        P.dve("tensor_reduce", out=bon[:], in_=tm3[:, 3, :].rearrange("p (h n) -> p h n", h=2), axis=AX.X, op=ALU.add,
              r=[("tm3", q)], w=[("bon", q)])
        yield
        for c in range(2):
            rs = slice(c * 64, (c + 1) * 64)
            P.dve("tensor_copy", out=Vz[c][rs, :], in_=tm3[rs, 2, :], r=[("tm3", q)], w=[("Vz", q, c)])
            yield
        for h in range(2):
            BA = PS[1 + h]
            P.pe("matmul", BA[:, 0:128], lhsT=bt[h][:, js], rhs=at[h][:, js], start=True, stop=True, r=[("bt", h, sp), ("at", h, sp)], w=[("ps", 1 + h)])
            P.pe("matmul", BA[:, 128:256], lhsT=bt[h][:, js], rhs=rt[h][:, js], start=True, stop=True, r=[("bt", h, sp), ("rt", h, sp)], w=[("ps", 1 + h)])
            P.pe("matmul", BA[:, 256:384], lhsT=kt[h][:, js], rhs=at[h][:, js], start=True, stop=True, r=[("kt", h, sp), ("at", h, sp)], w=[("ps", 1 + h)])
            P.pe("matmul", BA[:, 384:512], lhsT=kt[h][:, js], rhs=rt[h][:, js], start=True, stop=True, r=[("kt", h, sp), ("rt", h, sp)], w=[("ps", 1 + h)])
            yield
            P.dve("tensor_tensor", out=Amat[h][:], in0=BA[:], in1=mask4[:], op=ALU.mult, r=[("ps", 1 + h), "mask4"], w=[("Amat", q, h)])
            yield
            BI = PS[3 + h]
            P.pe("matmul", BI[:, 0:128], lhsT=at[h][:, js], rhs=bt[h][:, js], start=True, stop=True, r=[("at", h, sp), ("bt", h, sp)], w=[("ps", 3 + h)])
            yield
            P.dve("tensor_tensor", out=Pk[h][0][:, 128:256], in0=BI[:, 0:128], in1=Ms[:], op=ALU.mult, r=[("ps", 3 + h), "Ms"], w=[("Pk", q, h, 0)])
            P.act("activation", out=Pk[h][0][:, 0:128], in_=Amat[h][:, 0:128], func=AF.Copy, r=[("Amat", q, h)], w=[("Pk", q, h, 0)])
            yield
            P.dve("tensor_tensor", out=TTm[h][0][:], in0=Amat[h][:, 0:128], in1=ident[:], op=ALU.add, r=[("Amat", q, h), "ident"], w=[("TT", q, h, 0)])
            yield
        for lev in range(5):
            a_, b_ = lev % 2, (lev + 1) % 2
            for h in range(2):
                BI = PS[3 + h]
                cur, nxt = Pk[h][a_], Pk[h][b_]
                P.pe("matmul", BI[:, 0:128], lhsT=cur[:, 128:256], rhs=cur[:, 0:128], start=True, stop=True, r=[("Pk", q, h, a_)], w=[("ps", 3 + h)])
                P.pe("matmul", BI[:, 128:256], lhsT=cur[:, 0:128], rhs=cur[:, 128:256], start=True, stop=True, r=[("Pk", q, h, a_)], w=[("ps", 3 + h)])
                yield
                P.act("activation", out=nxt[:], in_=BI[:, 0:256], func=AF.Copy, r=[("ps", 3 + h)], w=[("Pk", q, h, b_)])
                yield
                P.pe("matmul", BI[:, 256:384], lhsT=nxt[:, 128:256], rhs=TTm[h][a_][:], start=True, stop=True,
                     r=[("Pk", q, h, b_), ("TT", q, h, a_)], w=[("ps", 3 + h)])
                yield
                P.dve("tensor_tensor", out=TTm[h][b_][:], in0=BI[:, 256:384], in1=TTm[h][a_][:], op=ALU.add,
                      r=[("ps", 3 + h), ("TT", q, h, a_)], w=[("TT", q, h, b_)])
                yield

    def sc_seq(gidx):
        sg, j = divmod(gidx, NSC)
        sp, q = sg % 2, gidx % 2
        t0 = sg * SEG
        js = slice(j * 128, (j + 1) * 128)
        W = W2[sp]; gtm = gtm2[sp]; rt = rt2[sp]; at = at2[sp]
        tm3 = tm3_2[q]; Vz = Vz2[q]; bon = bon2[q]; Amat = Amat2[q]; TTm = TTm2[q]
        TTf = [TTm[h][1] for h in range(2)]
        kTT = [("TT", q, h, 1) for h in range(2)]
        BH = [PS[5], PS[6]]
        for c in range(2):
            rs = slice(c * 64, (c + 1) * 64)
            for h in range(2):
                hs = slice(h * 64, (h + 1) * 64)
                B5 = BH[h]
                xc = slice(0, 64)
                uc = slice(64, 128)
                P.pe("matmul", B5[:, xc], lhsT=at[h][:, js], rhs=ST[:], start=True, stop=False, r=[("at", h, sp), "ST"], w=[("ps", 5 + h)])
                P.pe("matmul", B5[:, xc], lhsT=Amat[h][:, 256:384], rhs=Vz[c][:, hs], start=False, stop=True,
                     r=[("Amat", q, h), ("Vz", q, c)], w=[("ps", 5 + h)])
                yield
                P.act("activation", out=Xs[h][:], in_=B5[:, xc], func=AF.Copy, r=[("ps", 5 + h)], w=[("Xs", h)])
                yield
                P.pe("matmul", B5[:, uc], lhsT=TTf[h][:], rhs=Xs[h][:], start=True, stop=True, r=[kTT[h], ("Xs", h)], w=[("ps", 5 + h)])
                yield
                P.dve("tensor_copy", out=Uz[h][c][rs, :], in_=B5[rs, uc], r=[("ps", 5 + h)], w=[("Uz", h, c)])
                yield
            for h in range(2):
                hs = slice(h * 64, (h + 1) * 64)
                B6 = BH[h]
                yc_ = slice(128, 192)
                sc_ = slice(192, 256)
                P.pe("matmul", B6[:, yc_], lhsT=rt[h][:, js], rhs=ST[:], start=True, stop=False, r=[("rt", h, sp), "ST"], w=[("ps", 5 + h)])
                P.pe("matmul", B6[:, yc_], lhsT=Amat[h][:, 128:256], rhs=Uz[h][c][:], start=False, stop=False,
                     r=[("Amat", q, h), ("Uz", h, c)], w=[("ps", 5 + h)])
                P.pe("matmul", B6[:, yc_], lhsT=Amat[h][:, 384:512], rhs=Vz[c][:, hs], start=False, stop=True,
                     r=[("Amat", q, h), ("Vz", q, c)], w=[("ps", 5 + h)])
                P.pe("matmul", B6[:, sc_], lhsT=tm3[:, 0, :], rhs=Uz[h][c][:], start=True, stop=False, r=[("tm3", q), ("Uz", h, c)], w=[("ps", 5 + h)])
                P.pe("matmul", B6[:, sc_], lhsT=tm3[:, 1, :], rhs=Vz[c][:, hs], start=False, stop=True, r=[("tm3", q), ("Vz", q, c)], w=[("ps", 5 + h)])
                yield
            for h in range(2):
                hs = slice(h * 64, (h + 1) * 64)
                B6 = BH[h]
                sc_ = slice(192, 256)
                wc = j * 128 + c * 64 + 63
                P.dve("scalar_tensor_tensor", out=ST[hs, :], in0=ST[hs, :], scalar=W[hs, wc:wc + 1], in1=B6[hs, sc_],
                      op0=ALU.mult, op1=ALU.add, r=["ST", ("W", sp), ("ps", 5 + h)], w=["ST"])
                yield
                P.act("activation", out=ytm[rs, hs], in_=B6[rs, slice(128, 192)], func=AF.Copy, r=[("ps", 5 + h)], w=["ytm"])
                yield
        y3 = ytm[:].rearrange("p (h n) -> p h n", h=2)
        P.dve("tensor_reduce", out=st4[:, 0:2], in_=y3, axis=AX.X, op=ALU.add, r=["ytm"], w=["st4"])
        P.act("activation", out=ysq[:], in_=ytm[:], func=AF.Square, r=["ytm"], w=["ysq"])
        yield
        P.dve("tensor_reduce", out=st4[:, 2:4], in_=ysq[:].rearrange("p (h n) -> p h n", h=2), axis=AX.X, op=ALU.add, r=["ysq", "st4"], w=["st4"])
        yield
        P.dve("tensor_scalar", out=st4[:, 0:4], in0=st4[:, 0:4], scalar1=1.0 / 64, scalar2=None, op0=ALU.mult, r=["st4"], w=["st4"])
        yield
        P.dve("tensor_tensor", out=st4[:, 4:6], in0=st4[:, 0:2], in1=st4[:, 0:2], op=ALU.mult, r=["st4"], w=["st4"])
        yield
        P.dve("tensor_tensor", out=st4[:, 4:6], in0=st4[:, 2:4], in1=st4[:, 4:6], op=ALU.subtract, r=["st4"], w=["st4"])
        yield
        P.dve("tensor_scalar", out=st4[:, 4:6], in0=st4[:, 4:6], scalar1=64e-5, scalar2=None, op0=ALU.add, r=["st4"], w=["st4"])
        yield
        P.act("activation", out=st4[:, 4:6], in_=st4[:, 4:6], func=AF.Sqrt, r=["st4"], w=["st4"])
        yield
        P.dve("reciprocal", out=st4[:, 6:8], in_=st4[:, 4:6], r=["st4"], w=["st4"])
        yield
        o = yo[gidx % 2]
        ko = ("yo", gidx % 2)
        for h in range(2):
            hs = slice(h * 64, (h + 1) * 64)
            P.dve("tensor_scalar", out=o[:, hs], in0=ytm[:, hs], scalar1=st4[:, h:h + 1], scalar2=st4[:, 6 + h:7 + h],
                  op0=ALU.subtract, op1=ALU.mult, r=["ytm", "st4"], w=[ko])
            yield
        P.dve("tensor_tensor", out=o[:], in0=o[:], in1=lnw[:], op=ALU.mult, r=[ko, "lnw"], w=[ko])
        yield
        P.dve("tensor_tensor", out=o[:], in0=o[:], in1=lnb[:], op=ALU.add, r=[ko, "lnb"], w=[ko])
        yield
        for h in range(2):
            hs = slice(h * 64, (h + 1) * 64)
            P.dve("scalar_tensor_tensor", out=o[:, hs], in0=tm3[:, 2, hs], scalar=bon[:, h:h + 1], in1=o[:, hs],
                  op0=ALU.mult, op1=ALU.add, r=[("tm3", q), ("bon", q), ko], w=[ko])
            yield
        P.dve("tensor_tensor", out=o[:], in0=o[:], in1=gtm[:, j, :], op=ALU.mult, r=[ko, ("gtm", sp)], w=[ko])
        P.dma("sp", out=yc[t0 + j * 128:t0 + (j + 1) * 128, :], in_=o[:], r=[ko])
        yield

    def interleave(gens):
        gens = list(gens)
        while gens:
            for g_ in list(gens):
                try:
                    next(g_)
                except StopIteration:
                    gens.remove(g_)

    NG = NSEG * NSC
    interleave([seg_prep(0)])
    interleave([sc_prep(0)])
    for gidx in range(NG):
        tasks = [sc_seq(gidx)]
        if gidx + 1 < NG:
            tasks.append(sc_prep(gidx + 1))
        sg, j = divmod(gidx, NSC)
        if j == 0 and sg + 1 < NSEG:
            tasks.append(seg_prep(sg + 1))
        interleave(tasks)


def build_mix(T, parts=("diff", "swa", "rwkv")):
    nc = bass.Bass("TRN2", target_bir_lowering=False)
    P = Prog(nc)
    import contextlib
    with contextlib.ExitStack() as es0:
        PS = [es0.enter_context(nc.psum_tensor(f"psb{i}", [128, 512], F32)) for i in range(8)]
        for part in parts:
            with contextlib.ExitStack() as es:
                C = _Ctx(nc, es)
                if part == "diff":
                    emit_diff(P, C, PS, T)
                elif part == "swa":
                    emit_swa(P, C, PS, T)
                elif part == "rwkv":
                    emit_rwkv(P, C, PS, T)
                P.barrier()
        P.emit()
    return nc


def emit_rwkv(P, C, PS, T, SEG=512):
    nc = C.nc
    I32 = mybir.dt.int32
    NSEG = T // SEG
    NSC = SEG // 128
    rin = {nm: C.din(nm, [128, T]) for nm in ("rr", "rk", "rv")}
    rin["rwl"] = C.din("rwl", [96, T])
    rin["ral"] = C.din("ral", [96, T])
    rgl = C.din("rgl", [256, T])
    rmu = C.din("rmu", [128, 8])
    rw2 = C.din("rw2", [96, 128])
    ra2 = C.din("ra2", [96, 128])
    rg2 = C.din("rg2", [256, 128])
    rcst = C.din("rcst", [128, 8])
    rlnw = C.din("rlnw", [128, 128])
    rlnb = C.din("rlnb", [128, 128])
    yc = C.dout("yc", [T, 128])

    def sb(name, shape, dt=F32):
        return C.sb("r_" + name, shape, dt)

    mu = sb("mu", [128, 8]); cst = sb("cst", [128, 8])
    w2 = sb("w2", [128, 128]); a2 = sb("a2", [128, 128]); g2 = sb("g2", [128, 2, 128])
    lnw = sb("lnw", [128, 128]); lnb = sb("lnb", [128, 128])
    di = sb("di", [128, 128], I32); dfl = sb("dfl", [128, 128])
    ident = sb("ident", [128, 128]); BD = sb("BD", [128, 128])
    mask4 = sb("mask4", [128, 512]); Ms = sb("Ms", [128, 128])
    hm = sb("hm", [128, 2]); rmask = sb("rmask", [128, SEG])
    raw = {nm: sb("raw_" + nm, [128, SEG + 1]) for nm in ("rr", "rk", "rv", "rwl", "ral", "g0", "g1")}
    dtmp = sb("dtmp", [128, SEG])
    f = {nm: sb("f_" + nm, [128, SEG]) for nm in ("rr", "rk", "rv", "rwl", "ral", "g0", "g1")}
    ld = sb("ld", [128, SEG]); cum = sb("cum", [128, SEG]); aic = sb("aic", [128, SEG])
    kk = sb("kk", [128, SEG]); bvec = sb("bvec", [128, SEG]); kpr = sb("kpr", [128, SEG])
    sq = sb("sq", [128, SEG])
    W = sb("W", [128, SEG]); Wp = sb("Wp", [128, SEG]); Wi = sb("Wi", [128, SEG]); Wh = sb("Wh", [128, SEG])
    rt = [sb(f"rt{h}", [128, SEG]) for h in range(2)]
    at = [sb(f"at{h}", [128, SEG]) for h in range(2)]
    bt = [sb(f"bt{h}", [128, SEG]) for h in range(2)]
    kt = [sb(f"kt{h}", [128, SEG]) for h in range(2)]
    bh = sb("bh", [128, SEG]); kh = sb("kh", [128, SEG]); pb = sb("pb", [128, SEG])
    gtm = sb("gtm", [128, NSC, 128])
    tm3 = sb("tm3", [128, 4, 128])
    Vz = [sb(f"Vz{c}", [128, 128]) for c in range(2)]
    bon = sb("bon", [128, 2])
    Amat = [sb(f"Amat{h}", [128, 512]) for h in range(2)]
    Pk = [[sb(f"Pk{h}{i}", [128, 256]) for i in range(2)] for h in range(2)]
    TTm = [[sb(f"TT{h}{i}", [128, 128]) for i in range(2)] for h in range(2)]
    ST = sb("ST", [128, 64])
    Xs = [sb(f"Xs{h}", [128, 64]) for h in range(2)]
    Uz = [[sb(f"Uz{h}{c}", [128, 64]) for c in range(2)] for h in range(2)]
    ytm = sb("ytm", [128, 128]); ysq = sb("ysq", [128, 128])
    st4 = sb("st4", [128, 8])
    yo = [sb(f"yo{i}", [128, 128]) for i in range(2)]

    P.dma("sp", out=mu[:], in_=rmu, w=["mu"])
    P.dma("sp", out=cst[:], in_=rcst, w=["cst"])
    P.dve("memset", w2[:], 0.0, w=["w2"]); P.dve("memset", a2[:], 0.0, w=["a2"])
    P.dma("sp", out=w2[0:96, :], in_=rw2, w=["w2"])
    P.dma("sp", out=a2[0:96, :], in_=ra2, w=["a2"])
    P.dma("sp", out=g2[:], in_=rg2.rearrange("(c p) n -> p c n", p=128), w=["g2"])
    P.dma("sp", out=lnw[:], in_=rlnw, w=["lnw"])
    P.dma("sp", out=lnb[:], in_=rlnb, w=["lnb"])
    P.pool("iota", di[:], pattern=[[1, 128]], base=0, channel_multiplier=-1, w=["di"])
    P.dve("tensor_copy", out=dfl[:], in_=di[:], r=["di"], w=["dfl"])
    P.dve("tensor_scalar", out=ident[:], in0=dfl[:], scalar1=0.0, scalar2=None, op0=ALU.is_equal, r=["dfl"], w=["ident"])
    P.dve("memset", BD[:], 0.0, w=["BD"])
    P.dve("memset", BD[0:64, 0:64], 1.0, w=["BD"])
    P.dve("memset", BD[64:128, 64:128], 1.0, w=["BD"])
    for q in range(4):
        P.dve("tensor_scalar", out=mask4[:, q * 128:(q + 1) * 128], in0=dfl[:], scalar1=0.0, scalar2=None,
              op0=(ALU.is_gt if q % 2 == 0 else ALU.is_ge), r=["dfl"], w=["mask4"])
        P.dve("tensor_tensor", out=mask4[:, q * 128:(q + 1) * 128], in0=mask4[:, q * 128:(q + 1) * 128], in1=BD[:], op=ALU.mult,
              r=["mask4", "BD"], w=["mask4"])
    P.dve("tensor_scalar", out=Ms[:], in0=dfl[:], scalar1=0.0, scalar2=None, op0=ALU.is_lt, r=["dfl"], w=["Ms"])
    P.dve("tensor_tensor", out=Ms[:], in0=Ms[:], in1=BD[:], op=ALU.mult, r=["Ms", "BD"], w=["Ms"])
    P.dve("memset", hm[:], 0.0, w=["hm"])
    P.dve("memset", hm[0:64, 0:1], 1.0, w=["hm"])
    P.dve("memset", hm[64:128, 1:2], 1.0, w=["hm"])
    P.dve("memset", rmask[:], 1.0, w=["rmask"])
    P.dve("memset", rmask[:].rearrange("p (c t) -> p c t", t=64)[:, :, 0:1], 0.0, w=["rmask"])
    P.dve("memset", ST[:], 0.0, w=["ST"])
    for h in range(2):
        for c in range(2):
            P.dve("memset", Uz[h][c][:], 0.0, w=[("Uz", h, c)])
    for c in range(2):
        P.dve("memset", Vz[c][:], 0.0, w=[("Vz", c)])
    for nm in raw:
        P.dve("memset", raw[nm][:], 0.0, w=[("raw", nm)])
    for nm in f:
        P.dve("memset", f[nm][:], 0.0, w=[("f", nm)])

    srcs = {"rr": (rin["rr"], 128), "rk": (rin["rk"], 128), "rv": (rin["rv"], 128), "rwl": (rin["rwl"], 96),
            "ral": (rin["ral"], 96), "g0": (rgl[0:128, :], 128), "g1": (rgl[128:256, :], 128)}
    mucol = {"rr": 0, "rk": 1, "rv": 2, "rwl": 3, "ral": 4, "g0": 5, "g1": 6}
    B7 = PS[7]
    for sg in range(NSEG):
        t0 = sg * SEG
        for nm, (src, rows) in srcs.items():
            if sg == 0:
                P.dma("sp", out=raw[nm][0:rows, 1:SEG + 1], in_=src[:, 0:SEG], w=[("raw", nm)])
            else:
                P.dma("sp", out=raw[nm][0:rows, :], in_=src[:, t0 - 1:t0 + SEG], w=[("raw", nm)])
            P.dve("tensor_tensor", out=dtmp[0:rows, :], in0=raw[nm][0:rows, 0:SEG], in1=raw[nm][0:rows, 1:SEG + 1], op=ALU.subtract,
                  r=[("raw", nm)], w=["dtmp"])
            P.dve("scalar_tensor_tensor", out=f[nm][0:rows, :], in0=dtmp[0:rows, :], scalar=mu[0:rows, mucol[nm]:mucol[nm] + 1],
                  in1=raw[nm][0:rows, 1:SEG + 1], op0=ALU.mult, op1=ALU.add, r=["dtmp", "mu", ("raw", nm)], w=[("f", nm)])
        P.act("activation", out=f["rwl"][0:96, :], in_=f["rwl"][0:96, :], func=AF.Tanh, r=[("f", "rwl")], w=[("f", "rwl")])
        P.pe("matmul", B7[:, 0:SEG], lhsT=w2[:], rhs=f["rwl"][:], start=True, stop=True, r=["w2", ("f", "rwl")], w=[("ps", 7)])
        P.act("activation", out=ld[:], in_=B7[:, 0:SEG], func=AF.Sigmoid, bias=cst[:, 0:1], r=[("ps", 7), "cst"], w=["ld"])
        P.dve("tensor_scalar", out=ld[:], in0=ld[:], scalar1=-math.exp(-0.5), scalar2=None, op0=ALU.mult, r=["ld"], w=["ld"])
        P.pe("matmul", B7[:, 0:SEG], lhsT=a2[:], rhs=f["ral"][:], start=True, stop=True, r=["a2", ("f", "ral")], w=[("ps", 7)])
        P.act("activation", out=aic[:], in_=B7[:, 0:SEG], func=AF.Sigmoid, bias=cst[:, 1:2], r=[("ps", 7), "cst"], w=["aic"])
        for c in range(2):
            nm = f"g{c}"
            P.act("activation", out=f[nm][:], in_=f[nm][:], func=AF.Sigmoid, r=[("f", nm)], w=[("f", nm)])
        for j in range(NSC):
            js = slice(j * 128, (j + 1) * 128)
            for c in range(2):
                P.pe("matmul", B7[:, js], lhsT=f[f"g{c}"][:, js], rhs=g2[:, c, :], start=(c == 0), stop=(c == 1),
                     r=[("f", f"g{c}"), "g2"], w=[("ps", 7)])
        P.act("activation", out=gtm[:].rearrange("p a b -> p (a b)"), in_=B7[:, 0:SEG], func=AF.Copy, r=[("ps", 7)], w=["gtm"])
        P.dve("tensor_scalar", out=kk[:], in0=f["rk"][:], scalar1=cst[:, 2:3], scalar2=None, op0=ALU.mult, r=[("f", "rk"), "cst"], w=["kk"])
        P.act("activation", out=sq[:], in_=kk[:], func=AF.Square, r=["kk"], w=["sq"])
        P.pe("matmul", B7[:, 0:SEG], lhsT=BD[:], rhs=sq[:], start=True, stop=True, r=["BD", "sq"], w=[("ps", 7)])
        P.act("activation", out=sq[:], in_=B7[:, 0:SEG], func=AF.Sqrt, r=[("ps", 7)], w=["sq"])
        P.dve("tensor_scalar", out=sq[:], in0=sq[:], scalar1=1e-12, scalar2=None, op0=ALU.max, r=["sq"], w=["sq"])
        P.dve("reciprocal", out=sq[:], in_=sq[:], r=["sq"], w=["sq"])
        P.dve("tensor_tensor", out=kk[:], in0=kk[:], in1=sq[:], op=ALU.mult, r=["kk", "sq"], w=["kk"])
        P.dve("tensor_scalar", out=kpr[:], in0=aic[:], scalar1=-1.0, scalar2=cst[:, 3:4], op0=ALU.add, op1=ALU.mult, r=["aic", "cst"], w=["kpr"])
        P.dve("scalar_tensor_tensor", out=kpr[:], in0=kpr[:], scalar=1.0, in1=f["rk"][:], op0=ALU.add, op1=ALU.mult,
              r=["kpr", ("f", "rk")], w=["kpr"])
        P.dve("tensor_tensor", out=bvec[:], in0=kk[:], in1=aic[:], op=ALU.mult, r=["kk", "aic"], w=["bvec"])
        P.dve("tensor_tensor_scan", out=cum[:], data0=rmask[:], data1=ld[:], initial=0.0, op0=ALU.mult, op1=ALU.add,
              r=["rmask", "ld"], w=["cum"])
        P.act("activation", out=W[:], in_=cum[:], func=AF.Exp, r=["cum"], w=["W"])
        P.act("activation", out=Wi[:], in_=cum[:], func=AF.Exp, scale=-1.0, r=["cum"], w=["Wi"])
        P.dve("tensor_tensor", out=Wp[:], in0=cum[:], in1=ld[:], op=ALU.subtract, r=["cum", "ld"], w=["Wp"])
        P.act("activation", out=Wp[:], in_=Wp[:], func=AF.Exp, r=["Wp"], w=["Wp"])
        for c in range(SEG // 64):
            cs = slice(c * 64, (c + 1) * 64)
            P.dve("tensor_scalar", out=Wh[:, cs], in0=cum[:, cs], scalar1=-1.0, scalar2=cum[:, c * 64 + 63:c * 64 + 64],
                  op0=ALU.mult, op1=ALU.add, r=["cum"], w=["Wh"])
        P.act("activation", out=Wh[:], in_=Wh[:], func=AF.Exp, r=["Wh"], w=["Wh"])
        for h in range(2):
            hc = hm[:, h:h + 1]
            P.dve("scalar_tensor_tensor", out=rt[h][:], in0=f["rr"][:], scalar=hc, in1=W[:], op0=ALU.mult, op1=ALU.mult,
                  r=[("f", "rr"), "hm", "W"], w=[("rt", h)])
            P.dve("scalar_tensor_tensor", out=at[h][:], in0=kk[:], scalar=hc, in1=Wp[:], op0=ALU.mult, op1=ALU.mult,
                  r=["kk", "hm", "Wp"], w=[("at", h)])
            P.dve("tensor_scalar", out=at[h][:], in0=at[h][:], scalar1=-1.0, scalar2=None, op0=ALU.mult, r=[("at", h)], w=[("at", h)])
            P.dve("scalar_tensor_tensor", out=bt[h][:], in0=bvec[:], scalar=hc, in1=Wi[:], op0=ALU.mult, op1=ALU.mult,
                  r=["bvec", "hm", "Wi"], w=[("bt", h)])
            P.dve("scalar_tensor_tensor", out=kt[h][:], in0=kpr[:], scalar=hc, in1=Wi[:], op0=ALU.mult, op1=ALU.mult,
                  r=["kpr", "hm", "Wi"], w=[("kt", h)])
        P.dve("tensor_tensor", out=bh[:], in0=bvec[:], in1=Wh[:], op=ALU.mult, r=["bvec", "Wh"], w=["bh"])
        P.dve("tensor_tensor", out=kh[:], in0=kpr[:], in1=Wh[:], op=ALU.mult, r=["kpr", "Wh"], w=["kh"])
        P.dve("scalar_tensor_tensor", out=pb[:], in0=f["rr"][:], scalar=cst[:, 4:5], in1=kpr[:], op0=ALU.mult, op1=ALU.mult,
              r=[("f", "rr"), "cst", "kpr"], w=["pb"])

        if RW_STAGE == 1:
            P.dma("sp", out=yc[t0:t0 + 128, :], in_=pb[:, 0:128], r=["pb"])
            continue
        for j in range(NSC):
            js = slice(j * 128, (j + 1) * 128)
            B0 = PS[0]
            for i3, src in enumerate((bh, kh, f["rv"])):
                key = ["bh", "kh", ("f", "rv")][i3]
                P.pe("matmul", B0[:, i3 * 128:(i3 + 1) * 128], lhsT=src[:, js], rhs=ident[:], start=True, stop=True, r=[key, "ident"], w=[("ps", 0)])
            P.pe("matmul", B0[:, 384:512], lhsT=pb[:, js], rhs=ident[:], start=True, stop=True, r=["pb", "ident"], w=[("ps", 0)])
            P.act("activation", out=tm3[:].rearrange("p a b -> p (a b)"), in_=B0[:], func=AF.Copy, r=[("ps", 0)], w=["tm3"])
            if RW_STAGE == 20:
                P.dma("sp", out=yc[t0 + j * 128:t0 + (j + 1) * 128, :], in_=tm3[:, 2, :], r=["tm3"])
                continue
            P.dve("tensor_reduce", out=bon[:], in_=tm3[:, 3, :].rearrange("p (h n) -> p h n", h=2), axis=AX.X, op=ALU.add,
                  r=["tm3"], w=["bon"])
            if RW_STAGE == 21:
                P.dma("sp", out=yc[t0 + j * 128:t0 + (j + 1) * 128, :], in_=tm3[:, 2, :], r=["tm3", "bon"])
                continue
            if RW_STAGE == 2:
                P.dma("sp", out=yc[t0 + j * 128:t0 + (j + 1) * 128, :], in_=tm3[:, 2, :], r=["tm3"])
                continue
            for c in range(2):
                rs = slice(c * 64, (c + 1) * 64)
                P.dve("tensor_copy", out=Vz[c][rs, :], in_=tm3[rs, 2, :], r=["tm3"], w=[("Vz", c)])
            for h in range(2):
                BA = PS[1 + h]
                P.pe("matmul", BA[:, 0:128], lhsT=bt[h][:, js], rhs=at[h][:, js], start=True, stop=True, r=[("bt", h), ("at", h)], w=[("ps", 1 + h)])
                P.pe("matmul", BA[:, 128:256], lhsT=bt[h][:, js], rhs=rt[h][:, js], start=True, stop=True, r=[("bt", h), ("rt", h)], w=[("ps", 1 + h)])
                P.pe("matmul", BA[:, 256:384], lhsT=kt[h][:, js], rhs=at[h][:, js], start=True, stop=True, r=[("kt", h), ("at", h)], w=[("ps", 1 + h)])
                P.pe("matmul", BA[:, 384:512], lhsT=kt[h][:, js], rhs=rt[h][:, js], start=True, stop=True, r=[("kt", h), ("rt", h)], w=[("ps", 1 + h)])
                P.dve("tensor_tensor", out=Amat[h][:], in0=BA[:], in1=mask4[:], op=ALU.mult, r=[("ps", 1 + h), "mask4"], w=[("Amat", h)])
                BI = PS[3 + h]
                P.pe("matmul", BI[:, 0:128], lhsT=at[h][:, js], rhs=bt[h][:, js], start=True, stop=True, r=[("at", h), ("bt", h)], w=[("ps", 3 + h)])
                P.dve("tensor_tensor", out=Pk[h][0][:, 128:256], in0=BI[:, 0:128], in1=Ms[:], op=ALU.mult, r=[("ps", 3 + h), "Ms"], w=[("Pk", h, 0)])
                P.act("activation", out=Pk[h][0][:, 0:128], in_=Amat[h][:, 0:128], func=AF.Copy, r=[("Amat", h)], w=[("Pk", h, 0)])
                P.dve("tensor_tensor", out=TTm[h][0][:], in0=Amat[h][:, 0:128], in1=ident[:], op=ALU.add, r=[("Amat", h), "ident"], w=[("TT", h, 0)])
            for lev in range(5):
                a_, b_ = lev % 2, (lev + 1) % 2
                for h in range(2):
                    BI = PS[3 + h]
                    cur, nxt = Pk[h][a_], Pk[h][b_]
                    P.pe("matmul", BI[:, 0:128], lhsT=cur[:, 128:256], rhs=cur[:, 0:128], start=True, stop=True, r=[("Pk", h, a_)], w=[("ps", 3 + h)])
                    P.pe("matmul", BI[:, 128:256], lhsT=cur[:, 0:128], rhs=cur[:, 128:256], start=True, stop=True, r=[("Pk", h, a_)], w=[("ps", 3 + h)])
                    P.act("activation", out=nxt[:], in_=BI[:, 0:256], func=AF.Copy, r=[("ps", 3 + h)], w=[("Pk", h, b_)])
                    P.pe("matmul", BI[:, 256:384], lhsT=nxt[:, 128:256], rhs=TTm[h][a_][:], start=True, stop=True,
                         r=[("Pk", h, b_), ("TT", h, a_)], w=[("ps", 3 + h)])
                    P.dve("tensor_tensor", out=TTm[h][b_][:], in0=BI[:, 256:384], in1=TTm[h][a_][:], op=ALU.add,
                          r=[("ps", 3 + h), ("TT", h, a_)], w=[("TT", h, b_)])
            if RW_STAGE == 3:
                P.dma("sp", out=yc[t0 + j * 128:t0 + (j + 1) * 128, :], in_=TTm[0][1][:], r=[("TT", 0, 1)])
                continue
            TTf = [TTm[h][1] for h in range(2)]
            kTT = [("TT", h, 1) for h in range(2)]
            BH = [PS[5], PS[6]]
            for c in range(2):
                rs = slice(c * 64, (c + 1) * 64)
                for h in range(2):
                    hs = slice(h * 64, (h + 1) * 64)
                    B5 = BH[h]
                    xc = slice(0, 64)
                    uc = slice(64, 128)
                    P.pe("matmul", B5[:, xc], lhsT=at[h][:, js], rhs=ST[:], start=True, stop=False, r=[("at", h), "ST"], w=[("ps", 5 + h)])
                    P.pe("matmul", B5[:, xc], lhsT=Amat[h][:, 256:384], rhs=Vz[c][:, hs], start=False, stop=True,
                         r=[("Amat", h), ("Vz", c)], w=[("ps", 5 + h)])
                    P.act("activation", out=Xs[h][:], in_=B5[:, xc], func=AF.Copy, r=[("ps", 5 + h)], w=[("Xs", h)])
                    P.pe("matmul", B5[:, uc], lhsT=TTf[h][:], rhs=Xs[h][:], start=True, stop=True, r=[kTT[h], ("Xs", h)], w=[("ps", 5 + h)])
                    P.dve("tensor_copy", out=Uz[h][c][rs, :], in_=B5[rs, uc], r=[("ps", 5 + h)], w=[("Uz", h, c)])
                for h in range(2):
                    hs = slice(h * 64, (h + 1) * 64)
                    B6 = BH[h]
                    yc_ = slice(128, 192)
                    sc_ = slice(192, 256)
                    P.pe("matmul", B6[:, yc_], lhsT=rt[h][:, js], rhs=ST[:], start=True, stop=False, r=[("rt", h), "ST"], w=[("ps", 5 + h)])
                    P.pe("matmul", B6[:, yc_], lhsT=Amat[h][:, 128:256], rhs=Uz[h][c][:], start=False, stop=False,
                         r=[("Amat", h), ("Uz", h, c)], w=[("ps", 5 + h)])
                    P.pe("matmul", B6[:, yc_], lhsT=Amat[h][:, 384:512], rhs=Vz[c][:, hs], start=False, stop=True,
                         r=[("Amat", h), ("Vz", c)], w=[("ps", 5 + h)])
                    P.act("activation", out=ytm[rs, hs], in_=B6[rs, yc_], func=AF.Copy, r=[("ps", 5 + h)], w=["ytm"])
                    P.pe("matmul", B6[:, sc_], lhsT=tm3[:, 0, :], rhs=Uz[h][c][:], start=True, stop=False, r=["tm3", ("Uz", h, c)], w=[("ps", 5 + h)])
                    P.pe("matmul", B6[:, sc_], lhsT=tm3[:, 1, :], rhs=Vz[c][:, hs], start=False, stop=True, r=["tm3", ("Vz", c)], w=[("ps", 5 + h)])
                for h in range(2):
                    hs = slice(h * 64, (h + 1) * 64)
                    B6 = BH[h]
                    sc_ = slice(192, 256)
                    wc = j * 128 + c * 64 + 63
                    P.dve("scalar_tensor_tensor", out=ST[hs, :], in0=ST[hs, :], scalar=W[hs, wc:wc + 1], in1=B6[hs, sc_],
                          op0=ALU.mult, op1=ALU.add, r=["ST", "W", ("ps", 5 + h)], w=["ST"])
            y3 = ytm[:].rearrange("p (h n) -> p h n", h=2)
            P.dve("tensor_reduce", out=st4[:, 0:2], in_=y3, axis=AX.X, op=ALU.add, r=["ytm"], w=["st4"])
            P.act("activation", out=ysq[:], in_=ytm[:], func=AF.Square, r=["ytm"], w=["ysq"])
            P.dve("tensor_reduce", out=st4[:, 2:4], in_=ysq[:].rearrange("p (h n) -> p h n", h=2), axis=AX.X, op=ALU.add, r=["ysq", "st4"], w=["st4"])
            P.dve("tensor_scalar", out=st4[:, 0:4], in0=st4[:, 0:4], scalar1=1.0 / 64, scalar2=None, op0=ALU.mult, r=["st4"], w=["st4"])
            P.dve("tensor_tensor", out=st4[:, 4:6], in0=st4[:, 0:2], in1=st4[:, 0:2], op=ALU.mult, r=["st4"], w=["st4"])
            P.dve("tensor_tensor", out=st4[:, 4:6], in0=st4[:, 2:4], in1=st4[:, 4:6], op=ALU.subtract, r=["st4"], w=["st4"])
            P.dve("tensor_scalar", out=st4[:, 4:6], in0=st4[:, 4:6], scalar1=64e-5, scalar2=None, op0=ALU.add, r=["st4"], w=["st4"])
            P.act("activation", out=st4[:, 4:6], in_=st4[:, 4:6], func=AF.Sqrt, r=["st4"], w=["st4"])
            P.dve("reciprocal", out=st4[:, 6:8], in_=st4[:, 4:6], r=["st4"], w=["st4"])
            o = yo[j % 2]
            ko = ("yo", j % 2)
            for h in range(2):
                hs = slice(h * 64, (h + 1) * 64)
                P.dve("tensor_scalar", out=o[:, hs], in0=ytm[:, hs], scalar1=st4[:, h:h + 1], scalar2=st4[:, 6 + h:7 + h],
                      op0=ALU.subtract, op1=ALU.mult, r=["ytm", "st4"], w=[ko])
            P.dve("tensor_tensor", out=o[:], in0=o[:], in1=lnw[:], op=ALU.mult, r=[ko, "lnw"], w=[ko])
            P.dve("tensor_tensor", out=o[:], in0=o[:], in1=lnb[:], op=ALU.add, r=[ko, "lnb"], w=[ko])
            for h in range(2):
                hs = slice(h * 64, (h + 1) * 64)
                P.dve("scalar_tensor_tensor", out=o[:, hs], in0=tm3[:, 2, hs], scalar=bon[:, h:h + 1], in1=o[:, hs],
                      op0=ALU.mult, op1=ALU.add, r=["tm3", "bon", ko], w=[ko])
            P.dve("tensor_tensor", out=o[:], in0=o[:], in1=gtm[:, j, :], op=ALU.mult, r=[ko, "gtm"], w=[ko])
            P.dma("sp", out=yc[t0 + j * 128:t0 + (j + 1) * 128, :], in_=o[:], r=[ko])


def build_ffn(T, final=False, TP=512):
    D, FH = D_MODEL, FFN_HIDDEN
    KC = D // 128
    NJ = FH // 128
    GJ = 4
    nc = bass.Bass("TRN2", target_bir_lowering=False)
    P = Prog(nc)
    import contextlib
    with contextlib.ExitStack() as es:
        C = _Ctx(nc, es)
        mixT = C.din("mixT", [D, T]); xT = C.din("xT", [D, T])
        w_out = C.din("w_out", [D, D]); gF = C.din("g_ffn", [128, KC]); gL = C.din("g_fin", [128, KC])
        wgu = C.din("w_gu", [D, 2 * FH]); wdn = C.din("w_dn", [FH, D])
        oT = C.dout("oT", [D, T])
        PS = [es.enter_context(nc.psum_tensor(f"psc{i}", [128, 512], F32)) for i in range(8)]
        x_sb = C.sb("x_sb", [128, KC, TP])
        mh = C.sb("mh", [128, KC, TP], BF16)
        gf = C.sb("gf", [128, KC]); gl = C.sb("gl", [128, KC])
        ones = C.sb("ones", [128, 128])
        sq = [C.sb(f"sq{i}", [128, TP]) for i in range(2)]
        rstd = C.sb("rstd", [128, TP])
        wo = [C.sb(f"wo{i}", [128, KC, 512], BF16) for i in range(2)]
        wg = [C.sb(f"wg{i}", [128, KC, 512], BF16) for i in range(2)]
        wu = [C.sb(f"wu{i}", [128, KC, 512], BF16) for i in range(2)]
        wd = [C.sb(f"wd{i}", [128, GJ, D], BF16) for i in range(2)]
        sg = [C.sb(f"sg{i}", [128, TP]) for i in range(2)]
        act = [C.sb(f"act{i}", [128, GJ, TP], BF16) for i in range(2)]
        ob = [C.sb(f"obf{i}", [128, TP]) for i in range(2)]
        wo_v = w_out.rearrange("(k p) c -> p k c", p=128)
        wgu_v = wgu.rearrange("(k p) c -> p k c", p=128)
        wdn_v = wdn.rearrange("(j p) c -> p j c", p=128)
        xT_v = xT.rearrange("(k p) t -> p k t", p=128)
        mixT_v = mixT.rearrange("(k p) t -> p k t", p=128)
        oT_v = oT.rearrange("(k p) t -> p k t", p=128)

        P.dma("sp", out=gf[:], in_=gF, w=["gf"])
        P.dma("sp", out=gl[:], in_=gL, w=["gl"])
        P.dve("memset", ones[:], 1.0, w=["ones"])

        def rmsnorm(gt, gkey, outfn):
            for c in range(KC):
                s = sq[c % 2]
                P.act("activation", out=s[:], in_=x_sb[:, c, :], func=AF.Square, r=[("x", c)], w=[("sq", c % 2)])
                P.pe("matmul", PS[7][:, 0:TP], lhsT=ones[:], rhs=s[:], start=(c == 0), stop=(c == KC - 1),
                     r=["ones", ("sq", c % 2)], w=[("ps", 7)])
            P.dve("tensor_scalar", out=rstd[:], in0=PS[7][:, 0:TP], scalar1=1.0 / D, scalar2=NORM_EPS, op0=ALU.mult, op1=ALU.add,
                  r=[("ps", 7)], w=["rstd"])
            P.act("activation", out=rstd[:], in_=rstd[:], func=AF.Sqrt, r=["rstd"], w=["rstd"])
            P.dve("reciprocal", out=rstd[:], in_=rstd[:], r=["rstd"], w=["rstd"])
            for c in range(KC):
                outfn(c, gt, gkey)

        pcnt = [0]
        wcnt = {"wo": 0, "wgu": 0, "wd": 0}
        for tp in range(T // TP):
            ts = slice(tp * TP, (tp + 1) * TP)
            for c0 in range(0, KC, 4):
                P.dma("sp", out=x_sb[:, c0:c0 + 4, :], in_=xT_v[:, c0:c0 + 4, ts], w=[("x", c) for c in range(c0, c0 + 4)])
                P.dma("pool", out=mh[:, c0:c0 + 4, :], in_=mixT_v[:, c0:c0 + 4, ts], w=[("mh", c) for c in range(c0, c0 + 4)])
            for pi in range(D // 512):
                b = wcnt["wo"] % 2
                wcnt["wo"] += 1
                P.dma("pool", out=wo[b][:], in_=wo_v[:, :, pi * 512:(pi + 1) * 512], w=[("wo", b)])
                for mi in range(4):
                    m = pi * 4 + mi
                    pb_ = pcnt[0] % 4
                    pcnt[0] += 1
                    for k in range(KC):
                        P.pe("matmul", PS[pb_][:, 0:TP], lhsT=wo[b][:, k, mi * 128:(mi + 1) * 128], rhs=mh[:, k, :],
                             start=(k == 0), stop=(k == KC - 1), r=[("wo", b), ("mh", k)], w=[("ps", pb_)])
                    P.dve("tensor_tensor", out=x_sb[:, m, :], in0=x_sb[:, m, :], in1=PS[pb_][:, 0:TP], op=ALU.add,
                          r=[("x", m), ("ps", pb_)], w=[("x", m)])
            rmsnorm(gf, "gf", lambda c, gt, gkey: P.dve(
                "scalar_tensor_tensor", out=mh[:, c, :], in0=x_sb[:, c, :], scalar=gt[:, c:c + 1], in1=rstd[:],
                op0=ALU.mult, op1=ALU.mult, r=[("x", c), gkey, "rstd"], w=[("mh", c)]))
            for g in range(NJ // GJ):
                b = wcnt["wgu"] % 2
                wcnt["wgu"] += 1
                P.dma("pool", out=wg[b][:], in_=wgu_v[:, :, g * 512:(g + 1) * 512], w=[("wg", b)])
                P.dma("pool", out=wu[b][:], in_=wgu_v[:, :, FH + g * 512:FH + (g + 1) * 512], w=[("wu", b)])
                bd = wcnt["wd"] % 2
                wcnt["wd"] += 1
                P.dma("pool", out=wd[bd][:], in_=wdn_v[:, g * GJ:(g + 1) * GJ, :], w=[("wd", bd)])
                ab = g % 2
                for jj in range(GJ):
                    pg = pcnt[0] % 4
                    pu = 4 + pcnt[0] % 3
                    pcnt[0] += 1
                    for k in range(KC):
                        P.pe("matmul", PS[pg][:, 0:TP], lhsT=wg[b][:, k, jj * 128:(jj + 1) * 128], rhs=mh[:, k, :],
                             start=(k == 0), stop=(k == KC - 1), r=[("wg", b), ("mh", k)], w=[("ps", pg)])
                    for k in range(KC):
                        P.pe("matmul", PS[pu][:, 0:TP], lhsT=wu[b][:, k, jj * 128:(jj + 1) * 128], rhs=mh[:, k, :],
                             start=(k == 0), stop=(k == KC - 1), r=[("wu", b), ("mh", k)], w=[("ps", pu)])
                    s = sg[jj % 2]
                    P.act("activation", out=s[:], in_=PS[pg][:, 0:TP], func=AF.Silu, r=[("ps", pg)], w=[("sg", jj % 2)])
                    P.dve("tensor_tensor", out=act[ab][:, jj, :], in0=s[:], in1=PS[pu][:, 0:TP], op=ALU.mult,
                          r=[("sg", jj % 2), ("ps", pu)], w=[("act", ab, jj)])
                for m in range(KC):
                    pb_ = pcnt[0] % 4
                    pcnt[0] += 1
                    for jj in range(GJ):
                        P.pe("matmul", PS[pb_][:, 0:TP], lhsT=wd[bd][:, jj, m * 128:(m + 1) * 128], rhs=act[ab][:, jj, :],
                             start=(jj == 0), stop=(jj == GJ - 1), r=[("wd", bd), ("act", ab, jj)], w=[("ps", pb_)])
                    P.dve("tensor_tensor", out=x_sb[:, m, :], in0=x_sb[:, m, :], in1=PS[pb_][:, 0:TP], op=ALU.add,
                          r=[("x", m), ("ps", pb_)], w=[("x", m)])
            if final:
                def outfn(c, gt, gkey):
                    o = ob[c % 2]
                    P.dve("scalar_tensor_tensor", out=o[:], in0=x_sb[:, c, :], scalar=gt[:, c:c + 1], in1=rstd[:],
                          op0=ALU.mult, op1=ALU.mult, r=[("x", c), gkey, "rstd"], w=[("ob", c % 2)])
                    P.dma("sp", out=oT_v[:, c, ts], in_=o[:], r=[("ob", c % 2)])
                rmsnorm(gl, "gl", outfn)
            else:
                for c0 in range(0, KC, 4):
                    P.dma("sp", out=oT_v[:, c0:c0 + 4, ts], in_=x_sb[:, c0:c0 + 4, :], r=[("x", c) for c in range(c0, c0 + 4)])
        P.emit()
    return nc


def _alibi_slopes():
    idx = np.arange(1, 13, dtype=np.float64)
    m = np.exp2(-8.0 * idx / 12)
    di = np.arange(2, 12, 3)
    si = np.setdiff1d(np.arange(12), di)
    return m[di].astype(np.float32), m[si].astype(np.float32)


_PROGS = {}


def _prog(name, fn):
    if name not in _PROGS:
        _PROGS[name] = fn()
    return _PROGS[name]


def _run(nc, in_maps):
    res = run_bass_kernel_spmd(nc, in_maps, core_ids=list(range(len(in_maps))))
    return res.results


def kernel(x, attn_norm_g, w_in, diff_lambda, diff_subln_g, swa_sinks, rwkv_mu,
           rwkv_w0, rwkv_w2, rwkv_a0, rwkv_a2, rwkv_g2, rwkv_k_k, rwkv_k_a, rwkv_r_k,
           rwkv_ln_w, rwkv_ln_b, w_out, ffn_norm_g, w_gate_up, w_down, final_norm_g):
    f32 = np.float32
    A = lambda z: np.ascontiguousarray(np.asarray(z, dtype=f32))
    T = SEQ
    TC = T // NCORES
    D = D_MODEL
    x = np.asarray(x, dtype=f32).reshape(T, D)
    dsl, ssl = _alibi_slopes()
    gcol = lambda g: A(np.asarray(g, dtype=f32).reshape(D // 128, 128).T)
    xTfull = A(x.T)
    TA, TCc = T // NA, T // NC
    ncA = _prog("A", lambda: build_proj(TA, D, IN_COLS))
    ncB = _prog("B", lambda: build_mix(T))
    NI = T // 1024
    for l in range(DEPTH):
        wl_ = A(w_in[l])
        ga = gcol(attn_norm_g[l])
        rA = _run(ncA, [dict(xT=A(xTfull[:, c * TA:(c + 1) * TA]), g=ga, w=wl_) for c in range(NA)])
        projT = np.concatenate([rA[c]["yT"] for c in range(NA)], axis=1)
        del rA
        linit = 0.8 - 0.6 * math.exp(-0.3 * l)
        RW = 2304
        mu = np.asarray(rwkv_mu[l], dtype=f32)
        in_maps = []
        qcols_all = []
        for c in range(NCORES):
            h, p = c % 4, c // 4
            qcols = np.concatenate([np.arange((2 * i + p) * 512, (2 * i + p + 1) * 512) for i in range(NI)])
            qcols_all.append(qcols)
            dcst = np.zeros((128, 8), f32)
            dcst[:, 0] = -dsl[h]; dcst[:, 1] = 512 * p; dcst[:, 2] = linit; dcst[:, 3] = np.asarray(diff_subln_g[l], dtype=f32)
            scst = np.zeros((128, 8), f32)
            scst[:, 0] = -ssl[c]; scst[:, 1] = np.asarray(swa_sinks[l], dtype=f32)[c]
            kvh = c // 4
            ch = slice(c * 128, (c + 1) * 128)
            rmu = np.zeros((128, 8), f32)
            rmu[:, 0] = mu[0:1024][ch]; rmu[:, 1] = mu[1024:2048][ch]; rmu[:, 2] = mu[2048:3072][ch]
            rmu[:96, 3] = mu[3072:3168]; rmu[:96, 4] = mu[3168:3264]; rmu[:, 5] = mu[3264:3392]; rmu[:, 6] = mu[3392:3520]
            rcst = np.zeros((128, 8), f32)
            rcst[:, 0] = np.asarray(rwkv_w0[l], dtype=f32)[ch]; rcst[:, 1] = np.asarray(rwkv_a0[l], dtype=f32)[ch]
            rcst[:, 2] = np.asarray(rwkv_k_k[l], dtype=f32)[ch]; rcst[:, 3] = np.asarray(rwkv_k_a[l], dtype=f32)[ch]
            rcst[:, 4] = np.asarray(rwkv_r_k[l], dtype=f32).reshape(1024)[ch]
            in_maps.append(dict(
                dq=A(projT[h * 128:(h + 1) * 128][:, qcols]),
                dk=A(projT[512 + h * 128:512 + (h + 1) * 128]),
                dv=A(projT[1024 + h * 128:1024 + (h + 1) * 128].T),
                dcst=dcst, dlam=A(np.broadcast_to(np.asarray(diff_lambda[l], dtype=f32).reshape(1, 256), (128, 256))),
                sq=A(projT[1536 + c * 64:1536 + (c + 1) * 64]),
                sk=A(projT[2048 + kvh * 64:2048 + (kvh + 1) * 64]),
                sv=A(projT[2176 + kvh * 64:2176 + (kvh + 1) * 64].T),
                scst=scst,
                rr=A(projT[RW + c * 128:RW + (c + 1) * 128]),
                rk=A(projT[RW + 1024 + c * 128:RW + 1024 + (c + 1) * 128]),
                rv=A(projT[RW + 2048 + c * 128:RW + 2048 + (c + 1) * 128]),
                rwl=A(projT[RW + 3072:RW + 3168]), ral=A(projT[RW + 3168:RW + 3264]), rgl=A(projT[RW + 3264:RW + 3520]),
                rmu=rmu, rw2=A(np.asarray(rwkv_w2[l])[:, ch]), ra2=A(np.asarray(rwkv_a2[l])[:, ch]), rg2=A(np.asarray(rwkv_g2[l])[:, ch]),
                rcst=rcst,
                rlnw=A(np.broadcast_to(np.asarray(rwkv_ln_w[l], dtype=f32)[ch], (128, 128))),
                rlnb=A(np.broadcast_to(np.asarray(rwkv_ln_b[l], dtype=f32)[ch], (128, 128)))))
        del projT
        rB = _run(ncB, in_maps)
        del in_maps
        mixT = np.empty((D, T), f32)
        for c in range(NCORES):
            h = c % 4
            mixT[h * 128:(h + 1) * 128][:, qcols_all[c]] = rB[c]["ya"]
            mixT[512 + c * 64:512 + (c + 1) * 64] = rB[c]["yb"]
            mixT[1024 + c * 128:1024 + (c + 1) * 128] = rB[c]["yc"].T
        del rB
        final = (l == DEPTH - 1)
        ncC = _prog("Cf" if final else "C", lambda: build_ffn(TCc, final=final))
        wo_, wgu_, wdn_ = A(w_out[l]), A(w_gate_up[l]), A(w_down[l])
        gf_, gl_ = gcol(ffn_norm_g[l]), gcol(final_norm_g)
        rC = _run(ncC, [dict(mixT=A(mixT[:, c * TCc:(c + 1) * TCc]), xT=A(xTfull[:, c * TCc:(c + 1) * TCc]), w_out=wo_, g_ffn=gf_,
                             g_fin=gl_, w_gu=wgu_, w_dn=wdn_) for c in range(NC)])
        xTfull = np.concatenate([rC[c]["oT"] for c in range(NC)], axis=1)
        del rC, mixT
    out = A(xTfull.T).reshape(1, T, D)
    return out
```

```python
import math
import numpy as np
import concourse.bass as bass
import concourse.mybir as mybir
from concourse.bass_utils import run_bass_kernel_spmd

F32 = mybir.dt.float32
BF16 = mybir.dt.bfloat16
AF = mybir.ActivationFunctionType
ALU = mybir.AluOpType
AX = mybir.AxisListType

NCORES = 8
NA = 8
NC = 8
D_MODEL = 2048
SEQ = 8192
DEPTH = 4
IN_COLS = 5824
FFN_HIDDEN = 5632
NORM_EPS = 1e-5
DEBUG = False
RW_STAGE = 0
USE_F32R = False


class _Op:
    __slots__ = ("eng", "fn", "deps", "dma", "sem", "val", "signal", "idx")


class Prog:
    ENGS = ("pe", "act", "dve", "pool", "sp")
    N_DMA_SEMS = 6
    SEM_LIMIT = 30000

    def __init__(self, nc):
        self.nc = nc
        self.ops = []
        self.last_w = {}
        self.readers = {}
        self.dma_rr = {e: 0 for e in self.ENGS}
        self.dma_last = {}

    def add(self, eng, fn, reads=(), writes=(), dma=False, args=(), kwargs=None):
        if isinstance(fn, str):
            name, a, kw = fn, tuple(args), dict(kwargs or {})
            fn = lambda e, name=name, a=a, kw=kw: getattr(e, name)(*a, **kw)
        op = _Op()
        op.eng, op.fn, op.dma = eng, fn, dma
        op.idx = len(self.ops)
        deps = set()
        for r in reads:
            lw = self.last_w.get(r)
            if lw is not None:
                deps.add(lw)
        for w in writes:
            lw = self.last_w.get(w)
            if lw is not None:
                deps.add(lw)
            deps.update(self.readers.get(w, ()))
        for r in reads:
            self.readers.setdefault(r, []).append(op.idx)
        for w in writes:
            self.last_w[w] = op.idx
            self.readers[w] = []
        if dma:
            slot = (eng, self.dma_rr[eng] % self.N_DMA_SEMS)
            self.dma_rr[eng] += 1
            prev = self.dma_last.get(slot)
            if prev is not None:
                deps.add(prev)
            self.dma_last[slot] = op.idx
            op.sem = slot
        deps.discard(op.idx)
        op.deps = deps
        op.signal = False
        self.ops.append(op)
        return op.idx

    def pe(self, fn, *a, r=(), w=(), **kw):
        return self.add("pe", fn, r, w, args=a, kwargs=kw)

    def act(self, fn, *a, r=(), w=(), **kw):
        return self.add("act", fn, r, w, args=a, kwargs=kw)

    def dve(self, fn, *a, r=(), w=(), **kw):
        return self.add("dve", fn, r, w, args=a, kwargs=kw)

    def pool(self, fn, *a, r=(), w=(), **kw):
        return self.add("pool", fn, r, w, args=a, kwargs=kw)

    def dma(self, eng, fn=None, r=(), w=(), **kw):
        if fn is None:
            fn = "dma_start"
        return self.add(eng, fn, r, w, dma=True, kwargs=kw)

    def barrier(self):
        last = {}
        for op in self.ops:
            if op.fn is None:
                continue
            if op.dma:
                last[("dma", id(op.sem) if not isinstance(op.sem, tuple) else op.sem)] = op.idx
            else:
                last[op.eng] = op.idx
        deps = set(last.values())
        for e in self.ENGS:
            op = _Op()
            op.eng, op.fn, op.dma = e, None, False
            op.idx = len(self.ops)
            op.deps = set(deps)
            op.signal = False
            self.ops.append(op)
        self.last_w = {}
        self.readers = {}

    def emit(self):
        nc = self.nc
        ops = self.ops
        for op in ops:
            for d in op.deps:
                dop = ops[d]
                if dop.dma:
                    continue
                if dop.eng == "pe" and op.eng == "pe" and not op.dma:
                    continue
                dop.signal = True
        last_of = {}
        for op in ops:
            if not op.dma and op.fn is not None:
                last_of[op.eng] = op
        for op in last_of.values():
            op.signal = True
        import contextlib
        stack = contextlib.ExitStack()
        eng_sems = {e: [stack.enter_context(nc.semaphore(f"s_{e}_0"))] for e in self.ENGS}
        eng_cnt = {e: 0 for e in self.ENGS}
        dma_sems = {}
        dma_cnt = {}
        for op in ops:
            if op.dma:
                slot = op.sem
                if slot not in dma_sems:
                    dma_sems[slot] = stack.enter_context(nc.semaphore(f"d_{slot[0]}_{slot[1]}"))
                    dma_cnt[slot] = 0
                dma_cnt[slot] += 16
                op.sem = dma_sems[slot]
                op.val = dma_cnt[slot]
            elif op.signal:
                if eng_cnt[op.eng] >= self.SEM_LIMIT:
                    eng_sems[op.eng].append(
                        stack.enter_context(nc.semaphore(f"s_{op.eng}_{len(eng_sems[op.eng])}")))
                    eng_cnt[op.eng] = 0
                eng_cnt[op.eng] += 1
                op.sem = eng_sems[op.eng][-1]
                op.val = eng_cnt[op.eng]
        streams = {e: [] for e in self.ENGS}
        for op in ops:
            streams[op.eng].append(op)
        final_waits = {}
        for op in ops:
            if op.dma or op.signal:
                key = id(op.sem)
                if key not in final_waits or final_waits[key][1] < op.val:
                    final_waits[key] = (op.sem, op.val)

        def run_stream(ename, eng):
            waited = {}
            for op in streams[ename]:
                need = {}
                for d in op.deps:
                    dop = ops[d]
                    if (not dop.dma) and dop.eng == "pe" and ename == "pe" and not op.dma:
                        continue
                    k = id(dop.sem)
                    if k not in need or need[k][1] < dop.val:
                        need[k] = (dop.sem, dop.val)
                for k, (sem, val) in need.items():
                    if waited.get(k, 0) >= val:
                        continue
                    eng.wait_ge(sem, val)
                    waited[k] = val
                if op.fn is None:
                    continue
                ins = op.fn(eng)
                if op.dma:
                    ins.then_inc(op.sem, 16)
                elif op.signal:
                    ins.then_inc(op.sem, 1)
            if ename == "sp":
                for k, (sem, val) in final_waits.items():
                    if waited.get(k, 0) >= val:
                        continue
                    eng.wait_ge(sem, val)

        all_sems = [s for lst in eng_sems.values() for s in lst] + list(dma_sems.values())
        with stack:
            for s in all_sems:
                nc.gpsimd.sem_clear(s)
            nc.all_engine_barrier()
            with nc.Block() as block:
                @block.tensor
                def _(e):
                    run_stream("pe", e)

                @block.scalar
                def _(e):
                    run_stream("act", e)

                @block.vector
                def _(e):
                    run_stream("dve", e)

                @block.gpsimd
                def _(e):
                    run_stream("pool", e)

                @block.sync
                def _(e):
                    run_stream("sp", e)
            nc.all_engine_barrier()
            for s in all_sems:
                nc.gpsimd.sem_clear(s)


def build_proj(T, D, NOUT, PANEL=512, out_dtype=F32, TP=1024):
    nc = bass.Bass("TRN2", target_bir_lowering=False)
    KC = D // 128
    NT = TP // 512
    xT = nc.dram_tensor("xT", [D, T], F32, kind="ExternalInput").ap()
    g = nc.dram_tensor("g", [128, KC], F32, kind="ExternalInput").ap()
    w = nc.dram_tensor("w", [D, NOUT], F32, kind="ExternalInput").ap()
    yT = nc.dram_tensor("yT", [NOUT, T], out_dtype, kind="ExternalOutput").ap()
    P = Prog(nc)
    import contextlib
    with contextlib.ExitStack() as es:
        def sb(name, shape, dt):
            return es.enter_context(nc.sbuf_tensor(name, shape, dt))

        def ps(name, shape, dt=F32):
            return es.enter_context(nc.psum_tensor(name, shape, dt))

        x_sb = sb("x_sb", [128, KC, TP], F32)
        h_sb = sb("h_sb", [128, KC, TP], BF16)
        g_sb = sb("g_sb", [128, KC], F32)
        sq = [sb(f"sq{i}", [128, TP], F32) for i in range(2)]
        ones = sb("ones", [128, 128], F32)
        rstd = sb("rstd", [128, TP], F32)
        wp = [sb(f"wp{i}", [128, KC, PANEL], BF16) for i in range(2)]
        ob = [sb(f"ob{i}", [128, 512], out_dtype) for i in range(4)]
        pss = [ps(f"pss{i}", [128, 512]) for i in range(NT)]
        pacc = [ps(f"pacc{i}", [128, 512]) for i in range(4)]
        w_v = w.rearrange("(k p) c -> p k c", p=128)
        xT_v = xT.rearrange("(k p) t -> p k t", p=128)

        P.dma("sp", out=g_sb[:], in_=g, w=["g"])
        P.dve("memset", ones[:], 1.0, w=["ones"])
        cnt = 0
        wcnt = 0
        for tp in range(T // TP):
            tb = tp * TP
            for c0 in range(0, KC, 4):
                P.dma("sp", out=x_sb[:, c0:c0 + 4, :], in_=xT_v[:, c0:c0 + 4, tb:tb + TP], w=[("x", c) for c in range(c0, c0 + 4)])
            for c in range(KC):
                s = sq[c % 2]
                P.act("activation", out=s[:], in_=x_sb[:, c, :], func=AF.Square, r=[("x", c)], w=[("sq", c % 2)])
                for n in range(NT):
                    P.pe("matmul", pss[n][:], lhsT=ones[:], rhs=s[:, n * 512:(n + 1) * 512], start=(c == 0), stop=(c == KC - 1),
                         r=["ones", ("sq", c % 2)], w=[("pss", n)])
            for n in range(NT):
                sl = slice(n * 512, (n + 1) * 512)
                P.dve("tensor_scalar", out=rstd[:, sl], in0=pss[n][:], scalar1=1.0 / D, scalar2=NORM_EPS, op0=ALU.mult, op1=ALU.add,
                      r=[("pss", n)], w=[("rstd", n)])
                P.act("activation", out=rstd[:, sl], in_=rstd[:, sl], func=AF.Sqrt, r=[("rstd", n)], w=[("rstd", n)])
                P.dve("reciprocal", out=rstd[:, sl], in_=rstd[:, sl], r=[("rstd", n)], w=[("rstd", n)])
            for c in range(KC):
                for n in range(NT):
                    sl = slice(n * 512, (n + 1) * 512)
                    P.dve("scalar_tensor_tensor", out=h_sb[:, c, sl], in0=x_sb[:, c, sl], scalar=g_sb[:, c:c + 1], in1=rstd[:, sl],
                          op0=ALU.mult, op1=ALU.mult, r=[("x", c), "g", ("rstd", n)], w=[("h", c, n)])
            npan = (NOUT + PANEL - 1) // PANEL
            for pi in range(npan):
                c0 = pi * PANEL
                pw = min(PANEL, NOUT - c0)
                wb = wcnt % 2
                wcnt += 1
                wt = wp[wb]
                P.dma("pool", out=wt[:, :, :pw], in_=w_v[:, :, c0:c0 + pw], w=[("wp", wb)])
                for m0 in range(0, pw, 128):
                    mw = min(128, pw - m0)
                    for n in range(NT):
                        sl = slice(n * 512, (n + 1) * 512)
                        pa = pacc[cnt % 4]
                        o = ob[cnt % 4]
                        for k in range(KC):
                            P.pe("matmul", pa[:mw, :], lhsT=wt[:, k, m0:m0 + mw], rhs=h_sb[:, k, sl], start=(k == 0), stop=(k == KC - 1),
                                 r=[("wp", wb), ("h", k, n)], w=[("pacc", cnt % 4)])
                        if cnt % 2 == 0:
                            P.act("activation", out=o[:mw, :], in_=pa[:mw, :], func=AF.Copy, r=[("pacc", cnt % 4)], w=[("ob", cnt % 4)])
                        else:
                            P.dve("tensor_copy", out=o[:mw, :], in_=pa[:mw, :], r=[("pacc", cnt % 4)], w=[("ob", cnt % 4)])
                        r0 = c0 + m0
                        P.dma("sp", out=yT[r0:r0 + mw, tb + n * 512:tb + (n + 1) * 512], in_=o[:mw, :], r=[("ob", cnt % 4)])
                        cnt += 1
        P.emit()
    return nc


def _R(ap):
    return ap.bitcast(mybir.dt.float32r) if USE_F32R else ap


class _Ctx:
    def __init__(self, nc, es):
        self.nc, self.es = nc, es

    def sb(self, name, shape, dt=F32):
        return self.es.enter_context(self.nc.sbuf_tensor(name, shape, dt))

    def din(self, name, shape, dt=F32):
        return self.nc.dram_tensor(name, shape, dt, kind="ExternalInput").ap()

    def dout(self, name, shape, dt=F32):
        return self.nc.dram_tensor(name, shape, dt, kind="ExternalOutput").ap()


def emit_diff(P, C, PS, T):
    nc = C.nc
    NI = T // 1024
    NQ = NI * 512
    NKB = T // 128
    GW = (NI - 1) * 1024 + 896 + 512
    dq = C.din("dq", [128, NQ])
    dk = C.din("dk", [128, T])
    dv = C.din("dv", [T, 128])
    dcst = C.din("dcst", [128, 8])
    dlam = C.din("dlam", [128, 256])
    ya = C.dout("ya", [128, NQ])

    qT = C.sb("d_qT", [128, NQ], BF16)
    kT = [C.sb(f"d_kT{m}", [128, T], BF16) for m in range(2)]
    V = C.sb("d_V", [128, NKB, 128], BF16)
    cst = C.sb("d_cst", [128, 8])
    lam = C.sb("d_lam", [128, 256])
    lt = C.sb("d_lt", [128, 128])
    sc = C.sb("d_sc", [128, 8])
    GC = GW // 4
    Gi = C.sb("d_Gi", [128, GC], mybir.dt.int32)
    G = C.sb("d_G", [128, GW])
    Gm = C.sb("d_Gm", [128, GC])
    ones_b = C.sb("d_ones_b", [128, 128], BF16)
    ones_f = C.sb("d_ones_f", [128, 128])
    E = [[C.sb(f"d_E{m}{b}", [128, 512]) for b in range(2)] for m in range(2)]
    Pm = [[C.sb(f"d_P{m}{b}", [128, 512], BF16) for b in range(2)] for m in range(2)]
    t1 = C.sb("d_t1", [128, 512])
    t2 = C.sb("d_t2", [128, 512])
    t3 = C.sb("d_t3", [128, 512])
    ob = [C.sb(f"d_ob{b}", [128, 512]) for b in range(2)]

    P.dma("sp", lambda e: e.dma_start(out=cst[:], in_=dcst), w=["d_cst"])
    P.dma("sp", lambda e: e.dma_start(out=lam[:], in_=dlam), w=["d_lam"])
    P.dma("pool", lambda e: e.dma_start(out=qT[:], in_=dq), w=["d_qT"])
    for m in range(2):
        P.pool(lambda e, m=m: e.memset(kT[m][:], 0.0), w=[("d_kT", h // 2048) for h in range(0, T, 2048)])
    for h in range(0, T, 2048):
        for m in range(2):
            rows = slice(m * 64, (m + 1) * 64)
            P.dma("pool", lambda e, h=h, m=m, rows=rows: e.dma_start(out=kT[m][rows, h:h + 2048], in_=dk[rows, h:h + 2048]),
                  w=[("d_kT", h // 2048)])
    dv_v = dv.rearrange("(n p) d -> p n d", p=128)
    for h in range(0, NKB, 16):
        P.dma("pool", lambda e, h=h: e.dma_start(out=V[:, h:h + 16, :], in_=dv_v[:, h:h + 16, :]), w=[("d_V", h // 16)])
    P.dve(lambda e: e.memset(ones_b[:], 1.0), w=["d_ones_b"])
    P.dve(lambda e: e.memset(ones_f[:], 1.0), w=["d_ones_f"])
    P.dve(lambda e: e.tensor_tensor(out=lt[:, 0:64], in0=lam[:, 0:64], in1=lam[:, 64:128], op=ALU.mult), r=["d_lam"], w=["d_lt"])
    P.dve(lambda e: e.tensor_tensor(out=lt[:, 64:128], in0=lam[:, 128:192], in1=lam[:, 192:256], op=ALU.mult), r=["d_lam"], w=["d_lt"])
    P.dve(lambda e: e.reduce_sum(out=sc[:, 2:3], in_=lt[:, 0:64], axis=AX.X), r=["d_lt"], w=["d_sc"])
    P.dve(lambda e: e.reduce_sum(out=sc[:, 3:4], in_=lt[:, 64:128], axis=AX.X), r=["d_lt"], w=["d_sc"])
    P.act(lambda e: e.activation(out=sc[:, 4:6], in_=sc[:, 2:4], func=AF.Exp), r=["d_sc"], w=["d_sc"])
    P.dve(lambda e: e.tensor_tensor(out=sc[:, 6:7], in0=sc[:, 5:6], in1=sc[:, 4:5], op=ALU.subtract), r=["d_sc"], w=["d_sc"])
    P.dve(lambda e: e.tensor_tensor(out=sc[:, 0:1], in0=sc[:, 6:7], in1=cst[:, 2:3], op=ALU.subtract), r=["d_sc", "d_cst"], w=["d_sc"])
    P.dve(lambda e: e.tensor_scalar(out=sc[:, 7:8], in0=cst[:, 2:3], scalar1=-1.0, scalar2=1.0, op0=ALU.mult, op1=ALU.add), r=["d_cst", "d_sc"], w=["d_sc"])
    P.dve(lambda e: e.tensor_tensor(out=sc[:, 1:2], in0=sc[:, 7:8], in1=cst[:, 3:4], op=ALU.mult), r=["d_sc", "d_cst"], w=["d_sc"])
    for gq in range(4):
        gs = slice(gq * GC, (gq + 1) * GC)
        P.pool("iota", Gi[:], pattern=[[1, GC]], base=-896 + gq * GC, channel_multiplier=-1, w=["d_Gi"])
        P.dve("tensor_copy", out=G[:, gs], in_=Gi[:], r=["d_Gi"], w=["d_G"])
        P.dve("tensor_scalar", out=G[:, gs], in0=G[:, gs], scalar1=cst[:, 1:2], scalar2=None, op0=ALU.add, r=["d_G", "d_cst"], w=["d_G"])
        P.dve("tensor_scalar", out=Gm[:], in0=G[:, gs], scalar1=0.0, scalar2=None, op0=ALU.is_ge, r=["d_G"], w=["d_Gm"])
        P.dve("tensor_scalar", out=G[:, gs], in0=G[:, gs], scalar1=0.0, scalar2=None, op0=ALU.max, r=["d_G", "d_Gm"], w=["d_G"])
        P.act("activation", out=G[:, gs], in_=G[:, gs], func=AF.Exp, scale=cst[:, 0:1], r=["d_G", "d_cst"], w=["d_G"])
        P.dve("tensor_tensor", out=G[:, gs], in0=G[:, gs], in1=Gm[:], op=ALU.mult, r=["d_G", "d_Gm"], w=["d_G"])

    if DEBUG:
        dbgG = C.dout("dbgG", [128, GW])
        dbgsc = C.dout("dbgsc", [128, 8])
        dbgE = C.dout("dbgE", [128, 512])
        dbgP = C.dout("dbgP", [128, 512], BF16)
        dbgt = C.dout("dbgt", [128, 512])
        P.dma("sp", lambda e: e.dma_start(out=dbgG, in_=G[:]), r=["d_G"])
        P.dma("sp", lambda e: e.dma_start(out=dbgsc, in_=sc[:]), r=["d_sc"])
        dbgq = C.dout("dbgq", [128, NQ], BF16)
        dbgk = C.dout("dbgk", [128, T], BF16)
        dbgS = C.dout("dbgS", [128, 512])
        P.dma("sp", lambda e: e.dma_start(out=dbgq, in_=qT[:]), r=["d_qT"])
        P.dma("sp", lambda e: e.dma_start(out=dbgk, in_=kT[0][:]), r=[("d_kT", h // 2048) for h in range(0, T, 2048)])
    bc = 0
    for i in range(NI):
        nkb = 8 * i + 8
        qs = slice(i * 512, (i + 1) * 512)
        for kb in range(nkb):
            b = bc % 2
            bc += 1
            uu0 = 1024 * i - 128 * kb + 896
            for m in range(2):
                rows = slice(m * 64, (m + 1) * 64)
                P.pe(lambda e, m=m, b=b, rows=rows, kb=kb, qs=qs: e.matmul(
                    PS[m * 2 + b][:], lhsT=kT[m][:, kb * 128:(kb + 1) * 128], rhs=qT[:, qs], start=True, stop=True),
                    r=["d_qT", ("d_kT", kb // 16)], w=[("ps", m * 2 + b)])
            for m in range(2):
                P.act(lambda e, m=m, b=b: e.activation(out=E[m][b][:], in_=PS[m * 2 + b][:], func=AF.Exp, scale=0.125),
                      r=[("ps", m * 2 + b)], w=[("d_E", m, b)])
                P.dve("tensor_tensor", out=Pm[m][b][:], in0=E[m][b][:], in1=G[:, uu0:uu0 + 512], op=ALU.mult,
                                              r=[("d_E", m, b), "d_G"], w=[("d_P", m, b)])
            if DEBUG and i == 0 and kb == 0:
                P.act(lambda e: e.activation(out=t3[:], in_=PS[0][:], func=AF.Copy), r=[("ps", 0)], w=["d_t3"])
                P.dma("sp", lambda e: e.dma_start(out=dbgS, in_=t3[:]), r=["d_t3"])
                P.dma("sp", lambda e: e.dma_start(out=dbgE, in_=E[0][0][:]), r=[("d_E", 0, 0)])
                P.dma("sp", lambda e: e.dma_start(out=dbgP, in_=Pm[0][0][:]), r=[("d_P", 0, 0)])
            for m in range(2):
                P.pe(lambda e, m=m, b=b, kb=kb, nkb=nkb: e.matmul(
                    PS[4 + m][:], lhsT=V[:, kb, :], rhs=Pm[m][b][:], start=(kb == 0), stop=(kb == nkb - 1)),
                    r=[("d_V", kb // 16), ("d_P", m, b)], w=[("ps", 4 + m)])
                P.pe(lambda e, m=m, b=b, kb=kb, nkb=nkb: e.matmul(
                    PS[6 + m][:], lhsT=ones_b[:], rhs=Pm[m][b][:], start=(kb == 0), stop=(kb == nkb - 1)),
                    r=["d_ones_b", ("d_P", m, b)], w=[("ps", 6 + m)])
        P.dve(lambda e: e.reciprocal(out=t1[:], in_=PS[6][:]), r=[("ps", 6)], w=["d_t1"])
        P.dve(lambda e: e.tensor_tensor(out=t1[:], in0=t1[:], in1=PS[4][:], op=ALU.mult), r=["d_t1", ("ps", 4)], w=["d_t1"])
        P.dve(lambda e: e.reciprocal(out=t2[:], in_=PS[7][:]), r=[("ps", 7)], w=["d_t2"])
        P.dve(lambda e: e.tensor_tensor(out=t2[:], in0=t2[:], in1=PS[5][:], op=ALU.mult), r=["d_t2", ("ps", 5)], w=["d_t2"])
        P.dve(lambda e: e.scalar_tensor_tensor(out=t1[:], in0=t2[:], scalar=sc[:, 0:1], in1=t1[:], op0=ALU.mult, op1=ALU.add),
              r=["d_t1", "d_t2", "d_sc"], w=["d_t1"])
        if DEBUG and i == 0:
            P.dma("sp", lambda e: e.dma_start(out=dbgt, in_=t1[:]), r=["d_t1"])
        P.act(lambda e: e.activation(out=t3[:], in_=t1[:], func=AF.Square), r=["d_t1"], w=["d_t3"])
        P.pe(lambda e: e.matmul(PS[6][:], lhsT=ones_f[:], rhs=t3[:], start=True, stop=True), r=["d_ones_f", "d_t3"], w=[("ps", 6)])
        P.dve(lambda e: e.tensor_scalar(out=t2[:], in0=PS[6][:], scalar1=1.0 / 128, scalar2=NORM_EPS, op0=ALU.mult, op1=ALU.add),
              r=[("ps", 6)], w=["d_t2"])
        P.act(lambda e: e.activation(out=t2[:], in_=t2[:], func=AF.Sqrt), r=["d_t2"], w=["d_t2"])
        P.dve(lambda e: e.reciprocal(out=t2[:], in_=t2[:]), r=["d_t2"], w=["d_t2"])
        o = ob[i % 2]
        P.dve(lambda e, o=o: e.scalar_tensor_tensor(out=o[:], in0=t1[:], scalar=sc[:, 1:2], in1=t2[:], op0=ALU.mult, op1=ALU.mult),
              r=["d_t1", "d_t2", "d_sc"], w=[("d_ob", i % 2)])
        P.dma("sp", lambda e, o=o, qs=qs: e.dma_start(out=ya[:, qs], in_=o[:]), r=[("d_ob", i % 2)])


def emit_swa(P, C, PS, T):
    nc = C.nc
    NB = T // 128
    NG = T // 512
    sq = C.din("sq", [64, T])
    sk = C.din("sk", [64, T])
    sv = C.din("sv", [T, 64])
    scst = C.din("scst", [128, 8])
    yb = C.dout("yb", [64, T])
    qT = C.sb("s_qT", [128, T], BF16)
    kT = C.sb("s_kT", [128, T], BF16)
    V = C.sb("s_V", [128, NB, 128], BF16)
    cst = C.sb("s_cst", [128, 8])
    Gi = C.sb("s_Gi", [128, 256], mybir.dt.int32)
    G = C.sb("s_G", [128, 4, 256])
    Gm = C.sb("s_Gm", [128, 256])
    ones_b = C.sb("s_ones_b", [128, 128], BF16)
    E = [C.sb(f"s_E{b}", [128, 1024]) for b in range(2)]
    Pb = [C.sb(f"s_P{b}", [128, 1024], BF16) for b in range(2)]
    t1 = C.sb("s_t1", [64, 512])
    ob = [C.sb(f"s_ob{b}", [64, 512]) for b in range(2)]

    P.dma("sp", lambda e: e.dma_start(out=cst[:], in_=scst), w=["s_cst"])
    P.pool(lambda e: e.memset(qT[:], 0.0), w=["s_qT"])
    P.pool(lambda e: e.memset(kT[:], 0.0), w=["s_kT"])
    P.pool(lambda e: e.memset(V[:], 0.0), w=["s_V"])
    P.dma("pool", lambda e: e.dma_start(out=qT[0:64, :], in_=sq), w=["s_qT"])
    P.dma("pool", lambda e: e.dma_start(out=kT[0:64, :], in_=sk), w=["s_kT"])
    sv_v = sv.rearrange("(n p) d -> p n d", p=128)
    P.dma("pool", lambda e: e.dma_start(out=V[:, :, 0:64], in_=sv_v), w=["s_V"])
    P.dve(lambda e: e.memset(ones_b[:], 1.0), w=["s_ones_b"])
    P.pool(lambda e: e.iota(Gi[:, 0:128], pattern=[[1, 128]], base=128, channel_multiplier=-1), w=["s_Gi"])
    P.pool(lambda e: e.iota(Gi[:, 128:256], pattern=[[1, 128]], base=0, channel_multiplier=-1), w=["s_Gi"])
    g0 = G[:, 0, :]
    P.dve(lambda e: e.tensor_copy(out=g0, in_=Gi[:]), r=["s_Gi"], w=["s_G"])
    P.dve(lambda e: e.tensor_scalar(out=Gm[:, 0:128], in0=G[:, 0, 0:128], scalar1=127.0, scalar2=None, op0=ALU.is_le), r=["s_G"], w=["s_Gm"])
    P.dve(lambda e: e.tensor_scalar(out=Gm[:, 128:256], in0=G[:, 0, 128:256], scalar1=0.0, scalar2=None, op0=ALU.is_ge), r=["s_G"], w=["s_Gm"])
    P.dve(lambda e: e.tensor_scalar(out=g0, in0=g0, scalar1=0.0, scalar2=None, op0=ALU.max), r=["s_G", "s_Gm"], w=["s_G"])
    P.act(lambda e: e.activation(out=g0, in_=g0, func=AF.Exp, scale=cst[:, 0:1]), r=["s_G", "s_cst"], w=["s_G"])
    P.dve(lambda e: e.tensor_tensor(out=g0, in0=g0, in1=Gm[:], op=ALU.mult), r=["s_G", "s_Gm"], w=["s_G"])
    for j in range(1, 4):
        P.dve(lambda e, j=j: e.tensor_copy(out=G[:, j, :], in_=g0), r=["s_G"], w=["s_G"])
    P.act(lambda e: e.activation(out=cst[:, 2:3], in_=cst[:, 1:2], func=AF.Exp), r=["s_cst"], w=["s_cst"])
    Gf = G[:].rearrange("p a b -> p (a b)")
    for gi in range(NG):
        b2 = gi % 2
        pS = [PS[b2 * 2], PS[b2 * 2 + 1]]
        kS = [("ps", b2 * 2), ("ps", b2 * 2 + 1)]
        for bb in range(4):
            n = gi * 4 + bb
            half = bb // 2
            c0 = (bb % 2) * 256
            npv = max(n - 1, 0)
            P.pe(lambda e, n=n, npv=npv, half=half, c0=c0, pS=pS: e.matmul(
                pS[half][:, c0:c0 + 128], lhsT=kT[:, npv * 128:(npv + 1) * 128], rhs=qT[:, n * 128:(n + 1) * 128], start=True, stop=True),
                r=["s_qT", "s_kT"], w=[kS[half]])
            P.pe(lambda e, n=n, half=half, c0=c0, pS=pS: e.matmul(
                pS[half][:, c0 + 128:c0 + 256], lhsT=kT[:, n * 128:(n + 1) * 128], rhs=qT[:, n * 128:(n + 1) * 128], start=True, stop=True),
                r=["s_qT", "s_kT"], w=[kS[half]])
        for half in range(2):
            P.act(lambda e, half=half, b2=b2, pS=pS: e.activation(out=E[b2][:, half * 512:(half + 1) * 512], in_=pS[half][:], func=AF.Exp, scale=0.125),
                  r=[kS[half]], w=[("s_E", b2)])
        P.dve(lambda e, b2=b2: e.tensor_tensor(out=Pb[b2][:], in0=E[b2][:], in1=Gf, op=ALU.mult), r=[("s_E", b2), "s_G"], w=[("s_P", b2)])
        if gi == 0:
            P.dve(lambda e: e.memset(Pb[0][:, 0:128], 0.0), r=[("s_P", 0)], w=[("s_P", 0)])
        po, pd = PS[4 + b2], PS[6 + b2]
        for bb in range(4):
            n = gi * 4 + bb
            npv = max(n - 1, 0)
            cs = slice(bb * 128, (bb + 1) * 128)
            pa = slice(bb * 256, bb * 256 + 128)
            pb_ = slice(bb * 256 + 128, bb * 256 + 256)
            P.pe(lambda e, npv=npv, cs=cs, pa=pa, po=po, b2=b2: e.matmul(po[:, cs], lhsT=V[:, npv, :], rhs=Pb[b2][:, pa], start=True, stop=False),
                 r=["s_V", ("s_P", b2)], w=[("ps", 4 + b2)])
            P.pe(lambda e, n=n, cs=cs, pb_=pb_, po=po, b2=b2: e.matmul(po[:, cs], lhsT=V[:, n, :], rhs=Pb[b2][:, pb_], start=False, stop=True),
                 r=["s_V", ("s_P", b2)], w=[("ps", 4 + b2)])
            P.pe(lambda e, cs=cs, pa=pa, pd=pd, b2=b2: e.matmul(pd[:, cs], lhsT=ones_b[:], rhs=Pb[b2][:, pa], start=True, stop=False),
                 r=["s_ones_b", ("s_P", b2)], w=[("ps", 6 + b2)])
            P.pe(lambda e, cs=cs, pb_=pb_, pd=pd, b2=b2: e.matmul(pd[:, cs], lhsT=ones_b[:], rhs=Pb[b2][:, pb_], start=False, stop=True),
                 r=["s_ones_b", ("s_P", b2)], w=[("ps", 6 + b2)])
        P.dve(lambda e, pd=pd: e.tensor_scalar(out=t1[:], in0=pd[0:64, :], scalar1=cst[0:64, 2:3], scalar2=None, op0=ALU.add),
              r=[("ps", 6 + b2), "s_cst"], w=["s_t1"])
        P.dve(lambda e: e.reciprocal(out=t1[:], in_=t1[:]), r=["s_t1"], w=["s_t1"])
        o = ob[b2]
        P.dve(lambda e, o=o, po=po: e.tensor_tensor(out=o[:], in0=t1[:], in1=po[0:64, :], op=ALU.mult), r=["s_t1", ("ps", 4 + b2)], w=[("s_ob", b2)])
        P.dma("sp", lambda e, o=o, gi=gi: e.dma_start(out=yb[:, gi * 512:(gi + 1) * 512], in_=o[:]), r=[("s_ob", b2)])


def build_mix(T, parts=("diff", "swa", "rwkv")):
    nc = bass.Bass("TRN2", target_bir_lowering=False)
    P = Prog(nc)
    import contextlib
    with contextlib.ExitStack() as es0:
        PS = [es0.enter_context(nc.psum_tensor(f"psb{i}", [128, 512], F32)) for i in range(8)]
        for part in parts:
            with contextlib.ExitStack() as es:
                C = _Ctx(nc, es)
                if part == "diff":
                    emit_diff(P, C, PS, T)
                elif part == "swa":
                    emit_swa(P, C, PS, T)
                elif part == "rwkv":
                    emit_rwkv(P, C, PS, T)
                P.barrier()
        P.emit()
    return nc


def emit_rwkv(P, C, PS, T, SEG=512):
    nc = C.nc
    I32 = mybir.dt.int32
    NSEG = T // SEG
    NSC = SEG // 128
    rin = {nm: C.din(nm, [128, T]) for nm in ("rr", "rk", "rv")}
    rin["rwl"] = C.din("rwl", [96, T])
    rin["ral"] = C.din("ral", [96, T])
    rgl = C.din("rgl", [256, T])
    rmu = C.din("rmu", [128, 8])
    rw2 = C.din("rw2", [96, 128])
    ra2 = C.din("ra2", [96, 128])
    rg2 = C.din("rg2", [256, 128])
    rcst = C.din("rcst", [128, 8])
    rlnw = C.din("rlnw", [128, 128])
    rlnb = C.din("rlnb", [128, 128])
    yc = C.dout("yc", [T, 128])

    def sb(name, shape, dt=F32):
        return C.sb("r_" + name, shape, dt)

    mu = sb("mu", [128, 8]); cst = sb("cst", [128, 8])
    w2 = sb("w2", [128, 128]); a2 = sb("a2", [128, 128]); g2 = sb("g2", [128, 2, 128])
    lnw = sb("lnw", [128, 128]); lnb = sb("lnb", [128, 128])
    di = sb("di", [128, 128], I32); dfl = sb("dfl", [128, 128])
    ident = sb("ident", [128, 128]); BD = sb("BD", [128, 128])
    mask4 = sb("mask4", [128, 512]); Ms = sb("Ms", [128, 128])
    hm = sb("hm", [128, 2]); rmask = sb("rmask", [128, SEG])
    raw = {nm: sb("raw_" + nm, [128, SEG + 1]) for nm in ("rr", "rk", "rv", "rwl", "ral", "g0", "g1")}
    dtmp = sb("dtmp", [128, SEG])
    f = {nm: sb("f_" + nm, [128, SEG]) for nm in ("rr", "rk", "rv", "rwl", "ral", "g0", "g1")}
    ld = sb("ld", [128, SEG]); cum = sb("cum", [128, SEG]); aic = sb("aic", [128, SEG])
    kk = sb("kk", [128, SEG]); bvec = sb("bvec", [128, SEG]); kpr = sb("kpr", [128, SEG])
    sq = sb("sq", [128, SEG])
    W = sb("W", [128, SEG]); Wp = sb("Wp", [128, SEG]); Wi = sb("Wi", [128, SEG]); Wh = sb("Wh", [128, SEG])
    rt = [sb(f"rt{h}", [128, SEG]) for h in range(2)]
    at = [sb(f"at{h}", [128, SEG]) for h in range(2)]
    bt = [sb(f"bt{h}", [128, SEG]) for h in range(2)]
    kt = [sb(f"kt{h}", [128, SEG]) for h in range(2)]
    bh = sb("bh", [128, SEG]); kh = sb("kh", [128, SEG]); pb = sb("pb", [128, SEG])
    gtm = sb("gtm", [128, NSC, 128])
    tm3 = sb("tm3", [128, 4, 128])
    Vz = [sb(f"Vz{c}", [128, 128]) for c in range(2)]
    bon = sb("bon", [128, 2])
    Amat = [sb(f"Amat{h}", [128, 512]) for h in range(2)]
    Pk = [[sb(f"Pk{h}{i}", [128, 256]) for i in range(2)] for h in range(2)]
    TTm = [[sb(f"TT{h}{i}", [128, 128]) for i in range(2)] for h in range(2)]
    ST = sb("ST", [128, 64])
    Xs = [sb(f"Xs{h}", [128, 64]) for h in range(2)]
    Uz = [[sb(f"Uz{h}{c}", [128, 64]) for c in range(2)] for h in range(2)]
    ytm = sb("ytm", [128, 128]); ysq = sb("ysq", [128, 128])
    st4 = sb("st4", [128, 8])
    yo = [sb(f"yo{i}", [128, 128]) for i in range(2)]

    P.dma("sp", out=mu[:], in_=rmu, w=["mu"])
    P.dma("sp", out=cst[:], in_=rcst, w=["cst"])
    P.dve("memset", w2[:], 0.0, w=["w2"]); P.dve("memset", a2[:], 0.0, w=["a2"])
    P.dma("sp", out=w2[0:96, :], in_=rw2, w=["w2"])
    P.dma("sp", out=a2[0:96, :], in_=ra2, w=["a2"])
    P.dma("sp", out=g2[:], in_=rg2.rearrange("(c p) n -> p c n", p=128), w=["g2"])
    P.dma("sp", out=lnw[:], in_=rlnw, w=["lnw"])
    P.dma("sp", out=lnb[:], in_=rlnb, w=["lnb"])
    P.pool("iota", di[:], pattern=[[1, 128]], base=0, channel_multiplier=-1, w=["di"])
    P.dve("tensor_copy", out=dfl[:], in_=di[:], r=["di"], w=["dfl"])
    P.dve("tensor_scalar", out=ident[:], in0=dfl[:], scalar1=0.0, scalar2=None, op0=ALU.is_equal, r=["dfl"], w=["ident"])
    P.dve("memset", BD[:], 0.0, w=["BD"])
    P.dve("memset", BD[0:64, 0:64], 1.0, w=["BD"])
    P.dve("memset", BD[64:128, 64:128], 1.0, w=["BD"])
    for q in range(4):
        P.dve("tensor_scalar", out=mask4[:, q * 128:(q + 1) * 128], in0=dfl[:], scalar1=0.0, scalar2=None,
              op0=(ALU.is_gt if q % 2 == 0 else ALU.is_ge), r=["dfl"], w=["mask4"])
        P.dve("tensor_tensor", out=mask4[:, q * 128:(q + 1) * 128], in0=mask4[:, q * 128:(q + 1) * 128], in1=BD[:], op=ALU.mult,
              r=["mask4", "BD"], w=["mask4"])
    P.dve("tensor_scalar", out=Ms[:], in0=dfl[:], scalar1=0.0, scalar2=None, op0=ALU.is_lt, r=["dfl"], w=["Ms"])
    P.dve("tensor_tensor", out=Ms[:], in0=Ms[:], in1=BD[:], op=ALU.mult, r=["Ms", "BD"], w=["Ms"])
    P.dve("memset", hm[:], 0.0, w=["hm"])
    P.dve("memset", hm[0:64, 0:1], 1.0, w=["hm"])
    P.dve("memset", hm[64:128, 1:2], 1.0, w=["hm"])
    P.dve("memset", rmask[:], 1.0, w=["rmask"])
    P.dve("memset", rmask[:].rearrange("p (c t) -> p c t", t=64)[:, :, 0:1], 0.0, w=["rmask"])
    P.dve("memset", ST[:], 0.0, w=["ST"])
    for h in range(2):
        for c in range(2):
            P.dve("memset", Uz[h][c][:], 0.0, w=[("Uz", h, c)])
    for c in range(2):
        P.dve("memset", Vz[c][:], 0.0, w=[("Vz", c)])
    for nm in raw:
        P.dve("memset", raw[nm][:], 0.0, w=[("raw", nm)])
    for nm in f:
        P.dve("memset", f[nm][:], 0.0, w=[("f", nm)])

    W2 = [W, sb("W_b", [128, SEG])]
    frv2 = [f["rv"], sb("f_rv_b", [128, SEG])]
    gtm2 = [gtm, sb("gtm_b", [128, NSC, 128])]
    rt2 = [rt, [sb(f"rt{h}_b", [128, SEG]) for h in range(2)]]
    at2 = [at, [sb(f"at{h}_b", [128, SEG]) for h in range(2)]]
    bt2 = [bt, [sb(f"bt{h}_b", [128, SEG]) for h in range(2)]]
    kt2 = [kt, [sb(f"kt{h}_b", [128, SEG]) for h in range(2)]]
    bh2 = [bh, sb("bh_b", [128, SEG])]
    kh2 = [kh, sb("kh_b", [128, SEG])]
    pb2 = [pb, sb("pb_b", [128, SEG])]
    tm3_2 = [tm3, sb("tm3_b", [128, 4, 128])]
    Vz2 = [Vz, [sb(f"Vz{c}_b", [128, 128]) for c in range(2)]]
    bon2 = [bon, sb("bon_b", [128, 2])]
    Amat2 = [Amat, [sb(f"Amat{h}_b", [128, 512]) for h in range(2)]]
    Pk2 = [Pk, [[sb(f"Pk{h}{i}_b", [128, 256]) for i in range(2)] for h in range(2)]]
    TTm2 = [TTm, [[sb(f"TT{h}{i}_b", [128, 128]) for i in range(2)] for h in range(2)]]
    for c in range(2):
        P.dve("memset", Vz2[1][c][:], 0.0, w=[("Vz", 1, c)])
    P.dve("memset", frv2[1][:], 0.0, w=[("f", "rv", 1)])

    srcs = {"rr": (rin["rr"], 128), "rk": (rin["rk"], 128), "rv": (rin["rv"], 128), "rwl": (rin["rwl"], 96),
            "ral": (rin["ral"], 96), "g0": (rgl[0:128, :], 128), "g1": (rgl[128:256, :], 128)}
    mucol = {"rr": 0, "rk": 1, "rv": 2, "rwl": 3, "ral": 4, "g0": 5, "g1": 6}
    B7 = PS[7]

    def seg_prep(sg):
        sp = sg % 2
        t0 = sg * SEG
        W = W2[sp]; gtm = gtm2[sp]; rt = rt2[sp]; at = at2[sp]; bt = bt2[sp]; kt = kt2[sp]
        bh = bh2[sp]; kh = kh2[sp]; pb = pb2[sp]
        fl = dict(f); fl["rv"] = frv2[sp]
        fk = lambda nm: ("f", nm, sp) if nm == "rv" else ("f", nm)
        for nm, (src_, rows) in srcs.items():
            if sg == 0:
                P.dma("sp", out=raw[nm][0:rows, 1:SEG + 1], in_=src_[:, 0:SEG], w=[("raw", nm)])
            else:
                P.dma("sp", out=raw[nm][0:rows, :], in_=src_[:, t0 - 1:t0 + SEG], w=[("raw", nm)])
            yield
            P.dve("tensor_tensor", out=dtmp[0:rows, :], in0=raw[nm][0:rows, 0:SEG], in1=raw[nm][0:rows, 1:SEG + 1], op=ALU.subtract,
                  r=[("raw", nm)], w=["dtmp"])
            yield
            P.dve("scalar_tensor_tensor", out=fl[nm][0:rows, :], in0=dtmp[0:rows, :], scalar=mu[0:rows, mucol[nm]:mucol[nm] + 1],
                  in1=raw[nm][0:rows, 1:SEG + 1], op0=ALU.mult, op1=ALU.add, r=["dtmp", "mu", ("raw", nm)], w=[fk(nm)])
            yield
        P.act("activation", out=f["rwl"][0:96, :], in_=f["rwl"][0:96, :], func=AF.Tanh, r=[("f", "rwl")], w=[("f", "rwl")])
        P.pe("matmul", B7[:, 0:SEG], lhsT=w2[:], rhs=f["rwl"][:], start=True, stop=True, r=["w2", ("f", "rwl")], w=[("ps", 7)])
        yield
        P.act("activation", out=ld[:], in_=B7[:, 0:SEG], func=AF.Sigmoid, bias=cst[:, 0:1], r=[("ps", 7), "cst"], w=["ld"])
        P.dve("tensor_scalar", out=ld[:], in0=ld[:], scalar1=-math.exp(-0.5), scalar2=None, op0=ALU.mult, r=["ld"], w=["ld"])
        yield
        P.pe("matmul", B7[:, 0:SEG], lhsT=a2[:], rhs=f["ral"][:], start=True, stop=True, r=["a2", ("f", "ral")], w=[("ps", 7)])
        P.act("activation", out=aic[:], in_=B7[:, 0:SEG], func=AF.Sigmoid, bias=cst[:, 1:2], r=[("ps", 7), "cst"], w=["aic"])
        yield
        for c in range(2):
            nm = f"g{c}"
            P.act("activation", out=f[nm][:], in_=f[nm][:], func=AF.Sigmoid, r=[("f", nm)], w=[("f", nm)])
            yield
        for j in range(NSC):
            js = slice(j * 128, (j + 1) * 128)
            for c in range(2):
                P.pe("matmul", B7[:, js], lhsT=f[f"g{c}"][:, js], rhs=g2[:, c, :], start=(c == 0), stop=(c == 1),
                     r=[("f", f"g{c}"), "g2"], w=[("ps", 7)])
            yield
        P.act("activation", out=gtm[:].rearrange("p a b -> p (a b)"), in_=B7[:, 0:SEG], func=AF.Copy, r=[("ps", 7)], w=[("gtm", sp)])
        yield
        P.dve("tensor_scalar", out=kk[:], in0=f["rk"][:], scalar1=cst[:, 2:3], scalar2=None, op0=ALU.mult, r=[("f", "rk"), "cst"], w=["kk"])
        P.act("activation", out=sq[:], in_=kk[:], func=AF.Square, r=["kk"], w=["sq"])
        yield
        P.pe("matmul", B7[:, 0:SEG], lhsT=BD[:], rhs=sq[:], start=True, stop=True, r=["BD", "sq"], w=[("ps", 7)])
        P.act("activation", out=sq[:], in_=B7[:, 0:SEG], func=AF.Sqrt, r=[("ps", 7)], w=["sq"])
        yield
        P.dve("tensor_scalar", out=sq[:], in0=sq[:], scalar1=1e-12, scalar2=None, op0=ALU.max, r=["sq"], w=["sq"])
        yield
        P.dve("reciprocal", out=sq[:], in_=sq[:], r=["sq"], w=["sq"])
        yield
        P.dve("tensor_tensor", out=kk[:], in0=kk[:], in1=sq[:], op=ALU.mult, r=["kk", "sq"], w=["kk"])
        yield
        P.dve("tensor_scalar", out=kpr[:], in0=aic[:], scalar1=-1.0, scalar2=cst[:, 3:4], op0=ALU.add, op1=ALU.mult, r=["aic", "cst"], w=["kpr"])
        yield
        P.dve("scalar_tensor_tensor", out=kpr[:], in0=kpr[:], scalar=1.0, in1=f["rk"][:], op0=ALU.add, op1=ALU.mult,
              r=["kpr", ("f", "rk")], w=["kpr"])
        yield
        P.dve("tensor_tensor", out=bvec[:], in0=kk[:], in1=aic[:], op=ALU.mult, r=["kk", "aic"], w=["bvec"])
        yield
        P.dve("tensor_tensor_scan", out=cum[:], data0=rmask[:], data1=ld[:], initial=0.0, op0=ALU.mult, op1=ALU.add,
              r=["rmask", "ld"], w=["cum"])
        yield
        P.act("activation", out=W[:], in_=cum[:], func=AF.Exp, r=["cum"], w=[("W", sp)])
        P.act("activation", out=Wi[:], in_=cum[:], func=AF.Exp, scale=-1.0, r=["cum"], w=["Wi"])
        yield
        P.dve("tensor_tensor", out=Wp[:], in0=cum[:], in1=ld[:], op=ALU.subtract, r=["cum", "ld"], w=["Wp"])
        P.act("activation", out=Wp[:], in_=Wp[:], func=AF.Exp, r=["Wp"], w=["Wp"])
        yield
        for c in range(SEG // 64):
            cs = slice(c * 64, (c + 1) * 64)
            P.dve("tensor_scalar", out=Wh[:, cs], in0=cum[:, cs], scalar1=-1.0, scalar2=cum[:, c * 64 + 63:c * 64 + 64],
                  op0=ALU.mult, op1=ALU.add, r=["cum"], w=["Wh"])
            if c % 2 == 1:
                yield
        P.act("activation", out=Wh[:], in_=Wh[:], func=AF.Exp, r=["Wh"], w=["Wh"])
        yield
        for h in range(2):
            hc = hm[:, h:h + 1]
            P.dve("scalar_tensor_tensor", out=rt[h][:], in0=fl["rr"][:], scalar=hc, in1=W[:], op0=ALU.mult, op1=ALU.mult,
                  r=[("f", "rr"), "hm", ("W", sp)], w=[("rt", h, sp)])
            yield
            P.dve("scalar_tensor_tensor", out=at[h][:], in0=kk[:], scalar=hc, in1=Wp[:], op0=ALU.mult, op1=ALU.mult,
                  r=["kk", "hm", "Wp"], w=[("at", h, sp)])
            yield
            P.dve("tensor_scalar", out=at[h][:], in0=at[h][:], scalar1=-1.0, scalar2=None, op0=ALU.mult, r=[("at", h, sp)], w=[("at", h, sp)])
            yield
            P.dve("scalar_tensor_tensor", out=bt[h][:], in0=bvec[:], scalar=hc, in1=Wi[:], op0=ALU.mult, op1=ALU.mult,
                  r=["bvec", "hm", "Wi"], w=[("bt", h, sp)])
            yield
            P.dve("scalar_tensor_tensor", out=kt[h][:], in0=kpr[:], scalar=hc, in1=Wi[:], op0=ALU.mult, op1=ALU.mult,
                  r=["kpr", "hm", "Wi"], w=[("kt", h, sp)])
            yield
        P.dve("tensor_tensor", out=bh[:], in0=bvec[:], in1=Wh[:], op=ALU.mult, r=["bvec", "Wh"], w=[("bh", sp)])
        yield
        P.dve("tensor_tensor", out=kh[:], in0=kpr[:], in1=Wh[:], op=ALU.mult, r=["kpr", "Wh"], w=[("kh", sp)])
        yield
        P.dve("scalar_tensor_tensor", out=pb[:], in0=fl["rr"][:], scalar=cst[:, 4:5], in1=kpr[:], op0=ALU.mult, op1=ALU.mult,
              r=[("f", "rr"), "cst", "kpr"], w=[("pb", sp)])
        yield

    def sc_prep(gidx):
        sg, j = divmod(gidx, NSC)
        sp, q = sg % 2, gidx % 2
        js = slice(j * 128, (j + 1) * 128)
        rt = rt2[sp]; at = at2[sp]; bt = bt2[sp]; kt = kt2[sp]
        tm3 = tm3_2[q]; Vz = Vz2[q]; bon = bon2[q]; Amat = Amat2[q]; Pk = Pk2[q]; TTm = TTm2[q]
        B0 = PS[0]
        for i3, (src_, key) in enumerate(((bh2[sp], ("bh", sp)), (kh2[sp], ("kh", sp)), (frv2[sp], ("f", "rv", sp)), (pb2[sp], ("pb", sp)))):
            P.pe("matmul", B0[:, i3 * 128:(i3 + 1) * 128], lhsT=_R(src_[:, js]), rhs=_R(ident[:]), start=True, stop=True, r=[key, "ident"], w=[("ps", 0)])
        yield
        P.act("activation", out=tm3[:].rearrange("p a b -> p (a b)"), in_=B0[:], func=AF.Copy, r=[("ps", 0)], w=[("tm3", q)])
        yield
        P.dve("tensor_reduce", out=bon[:], in_=tm3[:, 3, :].rearrange("p (h n) -> p h n", h=2), axis=AX.X, op=ALU.add,
              r=[("tm3", q)], w=[("bon", q)])
        yield
        for c in range(2):
            rs = slice(c * 64, (c + 1) * 64)
            P.dve("tensor_copy", out=Vz[c][rs, :], in_=tm3[rs, 2, :], r=[("tm3", q)], w=[("Vz", q, c)])
            yield
        for h in range(2):
            BA = PS[1 + h]
            P.pe("matmul", BA[:, 0:128], lhsT=_R(bt[h][:, js]), rhs=_R(at[h][:, js]), start=True, stop=True, r=[("bt", h, sp), ("at", h, sp)], w=[("ps", 1 + h)])
            P.pe("matmul", BA[:, 128:256], lhsT=_R(bt[h][:, js]), rhs=_R(rt[h][:, js]), start=True, stop=True, r=[("bt", h, sp), ("rt", h, sp)], w=[("ps", 1 + h)])
            P.pe("matmul", BA[:, 256:384], lhsT=_R(kt[h][:, js]), rhs=_R(at[h][:, js]), start=True, stop=True, r=[("kt", h, sp), ("at", h, sp)], w=[("ps", 1 + h)])
            P.pe("matmul", BA[:, 384:512], lhsT=_R(kt[h][:, js]), rhs=_R(rt[h][:, js]), start=True, stop=True, r=[("kt", h, sp), ("rt", h, sp)], w=[("ps", 1 + h)])
            yield
            P.dve("tensor_tensor", out=Amat[h][:], in0=BA[:], in1=mask4[:], op=ALU.mult, r=[("ps", 1 + h), "mask4"], w=[("Amat", q, h)])
            yield
            BI = PS[3 + h]
            P.pe("matmul", BI[:, 0:128], lhsT=_R(at[h][:, js]), rhs=_R(bt[h][:, js]), start=True, stop=True, r=[("at", h, sp), ("bt", h, sp)], w=[("ps", 3 + h)])
            yield
            P.dve("tensor_tensor", out=Pk[h][0][:, 128:256], in0=BI[:, 0:128], in1=Ms[:], op=ALU.mult, r=[("ps", 3 + h), "Ms"], w=[("Pk", q, h, 0)])
            P.act("activation", out=Pk[h][0][:, 0:128], in_=Amat[h][:, 0:128], func=AF.Copy, r=[("Amat", q, h)], w=[("Pk", q, h, 0)])
            yield
            P.dve("tensor_tensor", out=TTm[h][0][:], in0=Amat[h][:, 0:128], in1=ident[:], op=ALU.add, r=[("Amat", q, h), "ident"], w=[("TT", q, h, 0)])
            yield
        for lev in range(5):
            a_, b_ = lev % 2, (lev + 1) % 2
            for h in range(2):
                BI = PS[3 + h]
                cur, nxt = Pk[h][a_], Pk[h][b_]
                P.pe("matmul", BI[:, 0:128], lhsT=_R(cur[:, 128:256]), rhs=_R(cur[:, 0:128]), start=True, stop=True, r=[("Pk", q, h, a_)], w=[("ps", 3 + h)])
                P.pe("matmul", BI[:, 128:256], lhsT=_R(cur[:, 0:128]), rhs=_R(cur[:, 128:256]), start=True, stop=True, r=[("Pk", q, h, a_)], w=[("ps", 3 + h)])
                yield
                P.act("activation", out=nxt[:], in_=BI[:, 0:256], func=AF.Copy, r=[("ps", 3 + h)], w=[("Pk", q, h, b_)])
                yield
                P.pe("matmul", BI[:, 256:384], lhsT=_R(nxt[:, 128:256]), rhs=_R(TTm[h][a_][:]), start=True, stop=True,
                     r=[("Pk", q, h, b_), ("TT", q, h, a_)], w=[("ps", 3 + h)])
                yield
                P.dve("tensor_tensor", out=TTm[h][b_][:], in0=BI[:, 256:384], in1=TTm[h][a_][:], op=ALU.add,
                      r=[("ps", 3 + h), ("TT", q, h, a_)], w=[("TT", q, h, b_)])
                yield

    def sc_seq(gidx):
        sg, j = divmod(gidx, NSC)
        sp, q = sg % 2, gidx % 2
        t0 = sg * SEG
        js = slice(j * 128, (j + 1) * 128)
        W = W2[sp]; gtm = gtm2[sp]; rt = rt2[sp]; at = at2[sp]
        tm3 = tm3_2[q]; Vz = Vz2[q]; bon = bon2[q]; Amat = Amat2[q]; TTm = TTm2[q]
        TTf = [TTm[h][1] for h in range(2)]
        kTT = [("TT", q, h, 1) for h in range(2)]
        BH = [PS[5], PS[6]]
        for c in range(2):
            rs = slice(c * 64, (c + 1) * 64)
            for h in range(2):
                hs = slice(h * 64, (h + 1) * 64)
                B5 = BH[h]
                xc = slice(0, 64)
                uc = slice(64, 128)
                P.pe("matmul", B5[:, xc], lhsT=_R(at[h][:, js]), rhs=_R(ST[:]), start=True, stop=False, r=[("at", h, sp), "ST"], w=[("ps", 5 + h)])
                P.pe("matmul", B5[:, xc], lhsT=_R(Amat[h][:, 256:384]), rhs=_R(Vz[c][:, hs]), start=False, stop=True,
                     r=[("Amat", q, h), ("Vz", q, c)], w=[("ps", 5 + h)])
                yield
                P.act("activation", out=Xs[h][:], in_=B5[:, xc], func=AF.Copy, r=[("ps", 5 + h)], w=[("Xs", h)])
                yield
                P.pe("matmul", B5[:, uc], lhsT=_R(TTf[h][:]), rhs=_R(Xs[h][:]), start=True, stop=True, r=[kTT[h], ("Xs", h)], w=[("ps", 5 + h)])
                yield
                P.dve("tensor_copy", out=Uz[h][c][rs, :], in_=B5[rs, uc], r=[("ps", 5 + h)], w=[("Uz", h, c)])
                yield
            for h in range(2):
                hs = slice(h * 64, (h + 1) * 64)
                B6 = BH[h]
                yc_ = slice(128, 192)
                sc_ = slice(192, 256)
                P.pe("matmul", B6[:, yc_], lhsT=_R(rt[h][:, js]), rhs=_R(ST[:]), start=True, stop=False, r=[("rt", h, sp), "ST"], w=[("ps", 5 + h)])
                P.pe("matmul", B6[:, yc_], lhsT=_R(Amat[h][:, 128:256]), rhs=_R(Uz[h][c][:]), start=False, stop=False,
                     r=[("Amat", q, h), ("Uz", h, c)], w=[("ps", 5 + h)])
                P.pe("matmul", B6[:, yc_], lhsT=_R(Amat[h][:, 384:512]), rhs=_R(Vz[c][:, hs]), start=False, stop=True,
                     r=[("Amat", q, h), ("Vz", q, c)], w=[("ps", 5 + h)])
                P.pe("matmul", B6[:, sc_], lhsT=_R(tm3[:, 0, :]), rhs=_R(Uz[h][c][:]), start=True, stop=False, r=[("tm3", q), ("Uz", h, c)], w=[("ps", 5 + h)])
                P.pe("matmul", B6[:, sc_], lhsT=_R(tm3[:, 1, :]), rhs=_R(Vz[c][:, hs]), start=False, stop=True, r=[("tm3", q), ("Vz", q, c)], w=[("ps", 5 + h)])
                yield
            for h in range(2):
                hs = slice(h * 64, (h + 1) * 64)
                B6 = BH[h]
                sc_ = slice(192, 256)
                wc = j * 128 + c * 64 + 63
                P.dve("scalar_tensor_tensor", out=ST[hs, :], in0=ST[hs, :], scalar=W[hs, wc:wc + 1], in1=B6[hs, sc_],
                      op0=ALU.mult, op1=ALU.add, r=["ST", ("W", sp), ("ps", 5 + h)], w=["ST"])
                yield
                P.act("activation", out=ytm[rs, hs], in_=B6[rs, slice(128, 192)], func=AF.Copy, r=[("ps", 5 + h)], w=["ytm"])
                yield
        y3 = ytm[:].rearrange("p (h n) -> p h n", h=2)
        P.dve("tensor_reduce", out=st4[:, 0:2], in_=y3, axis=AX.X, op=ALU.add, r=["ytm"], w=["st4"])
        P.act("activation", out=ysq[:], in_=ytm[:], func=AF.Square, r=["ytm"], w=["ysq"])
        yield
        P.dve("tensor_reduce", out=st4[:, 2:4], in_=ysq[:].rearrange("p (h n) -> p h n", h=2), axis=AX.X, op=ALU.add, r=["ysq", "st4"], w=["st4"])
        yield
        P.dve("tensor_scalar", out=st4[:, 0:4], in0=st4[:, 0:4], scalar1=1.0 / 64, scalar2=None, op0=ALU.mult, r=["st4"], w=["st4"])
        yield
        P.dve("tensor_tensor", out=st4[:, 4:6], in0=st4[:, 0:2], in1=st4[:, 0:2], op=ALU.mult, r=["st4"], w=["st4"])
        yield
        P.dve("tensor_tensor", out=st4[:, 4:6], in0=st4[:, 2:4], in1=st4[:, 4:6], op=ALU.subtract, r=["st4"], w=["st4"])
        yield
        P.dve("tensor_scalar", out=st4[:, 4:6], in0=st4[:, 4:6], scalar1=64e-5, scalar2=None, op0=ALU.add, r=["st4"], w=["st4"])
        yield
        P.act("activation", out=st4[:, 4:6], in_=st4[:, 4:6], func=AF.Sqrt, r=["st4"], w=["st4"])
        yield
        P.dve("reciprocal", out=st4[:, 6:8], in_=st4[:, 4:6], r=["st4"], w=["st4"])
        yield
        o = yo[gidx % 2]
        ko = ("yo", gidx % 2)
        for h in range(2):
            hs = slice(h * 64, (h + 1) * 64)
            P.dve("tensor_scalar", out=o[:, hs], in0=ytm[:, hs], scalar1=st4[:, h:h + 1], scalar2=st4[:, 6 + h:7 + h],
                  op0=ALU.subtract, op1=ALU.mult, r=["ytm", "st4"], w=[ko])
            yield
        P.dve("tensor_tensor", out=o[:], in0=o[:], in1=lnw[:], op=ALU.mult, r=[ko, "lnw"], w=[ko])
        yield
        P.dve("tensor_tensor", out=o[:], in0=o[:], in1=lnb[:], op=ALU.add, r=[ko, "lnb"], w=[ko])
        yield
        for h in range(2):
            hs = slice(h * 64, (h + 1) * 64)
            P.dve("scalar_tensor_tensor", out=o[:, hs], in0=tm3[:, 2, hs], scalar=bon[:, h:h + 1], in1=o[:, hs],
                  op0=ALU.mult, op1=ALU.add, r=[("tm3", q), ("bon", q), ko], w=[ko])
            yield
        P.dve("tensor_tensor", out=o[:], in0=o[:], in1=gtm[:, j, :], op=ALU.mult, r=[ko, ("gtm", sp)], w=[ko])
        P.dma("sp", out=yc[t0 + j * 128:t0 + (j + 1) * 128, :], in_=o[:], r=[ko])
        yield

    def interleave(gens):
        gens = list(gens)
        while gens:
            for g_ in list(gens):
                try:
                    next(g_)
                except StopIteration:
                    gens.remove(g_)

    NG = NSEG * NSC
    interleave([seg_prep(0)])
    interleave([sc_prep(0)])
    for gidx in range(NG):
        tasks = [sc_seq(gidx)]
        if gidx + 1 < NG:
            tasks.append(sc_prep(gidx + 1))
        sg, j = divmod(gidx, NSC)
        if j == 0 and sg + 1 < NSEG:
            tasks.append(seg_prep(sg + 1))
        interleave(tasks)


def build_mix(T, parts=("diff", "swa", "rwkv")):
    nc = bass.Bass("TRN2", target_bir_lowering=False)
    P = Prog(nc)
    import contextlib
    with contextlib.ExitStack() as es0:
        PS = [es0.enter_context(nc.psum_tensor(f"psb{i}", [128, 512], F32)) for i in range(8)]
        for part in parts:
            with contextlib.ExitStack() as es:
                C = _Ctx(nc, es)
                if part == "diff":
                    emit_diff(P, C, PS, T)
                elif part == "swa":
                    emit_swa(P, C, PS, T)
                elif part == "rwkv":
                    emit_rwkv(P, C, PS, T)
                P.barrier()
        P.emit()
    return nc


def emit_rwkv(P, C, PS, T, SEG=512):
    nc = C.nc
    I32 = mybir.dt.int32
    NSEG = T // SEG
    NSC = SEG // 128
    rin = {nm: C.din(nm, [128, T]) for nm in ("rr", "rk", "rv")}
    rin["rwl"] = C.din("rwl", [96, T])
    rin["ral"] = C.din("ral", [96, T])
    rgl = C.din("rgl", [256, T])
    rmu = C.din("rmu", [128, 8])
    rw2 = C.din("rw2", [96, 128])
    ra2 = C.din("ra2", [96, 128])
    rg2 = C.din("rg2", [256, 128])
    rcst = C.din("rcst", [128, 8])
    rlnw = C.din("rlnw", [128, 128])
    rlnb = C.din("rlnb", [128, 128])
    yc = C.dout("yc", [T, 128])

    def sb(name, shape, dt=F32):
        return C.sb("r_" + name, shape, dt)

    mu = sb("mu", [128, 8]); cst = sb("cst", [128, 8])
    w2 = sb("w2", [128, 128]); a2 = sb("a2", [128, 128]); g2 = sb("g2", [128, 2, 128])
    lnw = sb("lnw", [128, 128]); lnb = sb("lnb", [128, 128])
    di = sb("di", [128, 128], I32); dfl = sb("dfl", [128, 128])
    ident = sb("ident", [128, 128]); BD = sb("BD", [128, 128])
    mask4 = sb("mask4", [128, 512]); Ms = sb("Ms", [128, 128])
    hm = sb("hm", [128, 2]); rmask = sb("rmask", [128, SEG])
    raw = {nm: sb("raw_" + nm, [128, SEG + 1]) for nm in ("rr", "rk", "rv", "rwl", "ral", "g0", "g1")}
    dtmp = sb("dtmp", [128, SEG])
    f = {nm: sb("f_" + nm, [128, SEG]) for nm in ("rr", "rk", "rv", "rwl", "ral", "g0", "g1")}
    ld = sb("ld", [128, SEG]); cum = sb("cum", [128, SEG]); aic = sb("aic", [128, SEG])
    kk = sb("kk", [128, SEG]); bvec = sb("bvec", [128, SEG]); kpr = sb("kpr", [128, SEG])
    sq = sb("sq", [128, SEG])
    W = sb("W", [128, SEG]); Wp = sb("Wp", [128, SEG]); Wi = sb("Wi", [128, SEG]); Wh = sb("Wh", [128, SEG])
    rt = [sb(f"rt{h}", [128, SEG]) for h in range(2)]
    at = [sb(f"at{h}", [128, SEG]) for h in range(2)]
    bt = [sb(f"bt{h}", [128, SEG]) for h in range(2)]
    kt = [sb(f"kt{h}", [128, SEG]) for h in range(2)]
    bh = sb("bh", [128, SEG]); kh = sb("kh", [128, SEG]); pb = sb("pb", [128, SEG])
    gtm = sb("gtm", [128, NSC, 128])
    tm3 = sb("tm3", [128, 4, 128])
    Vz = [sb(f"Vz{c}", [128, 128]) for c in range(2)]
    bon = sb("bon", [128, 2])
    Amat = [sb(f"Amat{h}", [128, 512]) for h in range(2)]
    Pk = [[sb(f"Pk{h}{i}", [128, 256]) for i in range(2)] for h in range(2)]
    TTm = [[sb(f"TT{h}{i}", [128, 128]) for i in range(2)] for h in range(2)]
    ST = sb("ST", [128, 64])
    Xs = [sb(f"Xs{h}", [128, 64]) for h in range(2)]
    Uz = [[sb(f"Uz{h}{c}", [128, 64]) for c in range(2)] for h in range(2)]
    ytm = sb("ytm", [128, 128]); ysq = sb("ysq", [128, 128])
    st4 = sb("st4", [128, 8])
    yo = [sb(f"yo{i}", [128, 128]) for i in range(2)]

    P.dma("sp", out=mu[:], in_=rmu, w=["mu"])
    P.dma("sp", out=cst[:], in_=rcst, w=["cst"])
    P.dve("memset", w2[:], 0.0, w=["w2"]); P.dve("memset", a2[:], 0.0, w=["a2"])
    P.dma("sp", out=w2[0:96, :], in_=rw2, w=["w2"])
    P.dma("sp", out=a2[0:96, :], in_=ra2, w=["a2"])
    P.dma("sp", out=g2[:], in_=rg2.rearrange("(c p) n -> p c n", p=128), w=["g2"])
    P.dma("sp", out=lnw[:], in_=rlnw, w=["lnw"])
    P.dma("sp", out=lnb[:], in_=rlnb, w=["lnb"])
    P.pool("iota", di[:], pattern=[[1, 128]], base=0, channel_multiplier=-1, w=["di"])
    P.dve("tensor_copy", out=dfl[:], in_=di[:], r=["di"], w=["dfl"])
    P.dve("tensor_scalar", out=ident[:], in0=dfl[:], scalar1=0.0, scalar2=None, op0=ALU.is_equal, r=["dfl"], w=["ident"])
    P.dve("memset", BD[:], 0.0, w=["BD"])
    P.dve("memset", BD[0:64, 0:64], 1.0, w=["BD"])
    P.dve("memset", BD[64:128, 64:128], 1.0, w=["BD"])
    for q in range(4):
        P.dve("tensor_scalar", out=mask4[:, q * 128:(q + 1) * 128], in0=dfl[:], scalar1=0.0, scalar2=None,
              op0=(ALU.is_gt if q % 2 == 0 else ALU.is_ge), r=["dfl"], w=["mask4"])
        P.dve("tensor_tensor", out=mask4[:, q * 128:(q + 1) * 128], in0=mask4[:, q * 128:(q + 1) * 128], in1=BD[:], op=ALU.mult,
              r=["mask4", "BD"], w=["mask4"])
    P.dve("tensor_scalar", out=Ms[:], in0=dfl[:], scalar1=0.0, scalar2=None, op0=ALU.is_lt, r=["dfl"], w=["Ms"])
    P.dve("tensor_tensor", out=Ms[:], in0=Ms[:], in1=BD[:], op=ALU.mult, r=["Ms", "BD"], w=["Ms"])
    P.dve("memset", hm[:], 0.0, w=["hm"])
    P.dve("memset", hm[0:64, 0:1], 1.0, w=["hm"])
    P.dve("memset", hm[64:128, 1:2], 1.0, w=["hm"])
    P.dve("memset", rmask[:], 1.0, w=["rmask"])
    P.dve("memset", rmask[:].rearrange("p (c t) -> p c t", t=64)[:, :, 0:1], 0.0, w=["rmask"])
    P.dve("memset", ST[:], 0.0, w=["ST"])
    for h in range(2):
        for c in range(2):
            P.dve("memset", Uz[h][c][:], 0.0, w=[("Uz", h, c)])
    for c in range(2):
        P.dve("memset", Vz[c][:], 0.0, w=[("Vz", c)])
    for nm in raw:
        P.dve("memset", raw[nm][:], 0.0, w=[("raw", nm)])
    for nm in f:
        P.dve("memset", f[nm][:], 0.0, w=[("f", nm)])

    srcs = {"rr": (rin["rr"], 128), "rk": (rin["rk"], 128), "rv": (rin["rv"], 128), "rwl": (rin["rwl"], 96),
            "ral": (rin["ral"], 96), "g0": (rgl[0:128, :], 128), "g1": (rgl[128:256, :], 128)}
    mucol = {"rr": 0, "rk": 1, "rv": 2, "rwl": 3, "ral": 4, "g0": 5, "g1": 6}
    B7 = PS[7]
    for sg in range(NSEG):
        t0 = sg * SEG
        for nm, (src, rows) in srcs.items():
            if sg == 0:
                P.dma("sp", out=raw[nm][0:rows, 1:SEG + 1], in_=src[:, 0:SEG], w=[("raw", nm)])
            else:
                P.dma("sp", out=raw[nm][0:rows, :], in_=src[:, t0 - 1:t0 + SEG], w=[("raw", nm)])
            P.dve("tensor_tensor", out=dtmp[0:rows, :], in0=raw[nm][0:rows, 0:SEG], in1=raw[nm][0:rows, 1:SEG + 1], op=ALU.subtract,
                  r=[("raw", nm)], w=["dtmp"])
            P.dve("scalar_tensor_tensor", out=f[nm][0:rows, :], in0=dtmp[0:rows, :], scalar=mu[0:rows, mucol[nm]:mucol[nm] + 1],
                  in1=raw[nm][0:rows, 1:SEG + 1], op0=ALU.mult, op1=ALU.add, r=["dtmp", "mu", ("raw", nm)], w=[("f", nm)])
        P.act("activation", out=f["rwl"][0:96, :], in_=f["rwl"][0:96, :], func=AF.Tanh, r=[("f", "rwl")], w=[("f", "rwl")])
        P.pe("matmul", B7[:, 0:SEG], lhsT=w2[:], rhs=f["rwl"][:], start=True, stop=True, r=["w2", ("f", "rwl")], w=[("ps", 7)])
        P.act("activation", out=ld[:], in_=B7[:, 0:SEG], func=AF.Sigmoid, bias=cst[:, 0:1], r=[("ps", 7), "cst"], w=["ld"])
        P.dve("tensor_scalar", out=ld[:], in0=ld[:], scalar1=-math.exp(-0.5), scalar2=None, op0=ALU.mult, r=["ld"], w=["ld"])
        P.pe("matmul", B7[:, 0:SEG], lhsT=a2[:], rhs=f["ral"][:], start=True, stop=True, r=["a2", ("f", "ral")], w=[("ps", 7)])
        P.act("activation", out=aic[:], in_=B7[:, 0:SEG], func=AF.Sigmoid, bias=cst[:, 1:2], r=[("ps", 7), "cst"], w=["aic"])
        for c in range(2):
            nm = f"g{c}"
            P.act("activation", out=f[nm][:], in_=f[nm][:], func=AF.Sigmoid, r=[("f", nm)], w=[("f", nm)])
        for j in range(NSC):
            js = slice(j * 128, (j + 1) * 128)
            for c in range(2):
                P.pe("matmul", B7[:, js], lhsT=f[f"g{c}"][:, js], rhs=g2[:, c, :], start=(c == 0), stop=(c == 1),
                     r=[("f", f"g{c}"), "g2"], w=[("ps", 7)])
        P.act("activation", out=gtm[:].rearrange("p a b -> p (a b)"), in_=B7[:, 0:SEG], func=AF.Copy, r=[("ps", 7)], w=["gtm"])
        P.dve("tensor_scalar", out=kk[:], in0=f["rk"][:], scalar1=cst[:, 2:3], scalar2=None, op0=ALU.mult, r=[("f", "rk"), "cst"], w=["kk"])
        P.act("activation", out=sq[:], in_=kk[:], func=AF.Square, r=["kk"], w=["sq"])
        P.pe("matmul", B7[:, 0:SEG], lhsT=BD[:], rhs=sq[:], start=True, stop=True, r=["BD", "sq"], w=[("ps", 7)])
        P.act("activation", out=sq[:], in_=B7[:, 0:SEG], func=AF.Sqrt, r=[("ps", 7)], w=["sq"])
        P.dve("tensor_scalar", out=sq[:], in0=sq[:], scalar1=1e-12, scalar2=None, op0=ALU.max, r=["sq"], w=["sq"])
        P.dve("reciprocal", out=sq[:], in_=sq[:], r=["sq"], w=["sq"])
        P.dve("tensor_tensor", out=kk[:], in0=kk[:], in1=sq[:], op=ALU.mult, r=["kk", "sq"], w=["kk"])
        P.dve("tensor_scalar", out=kpr[:], in0=aic[:], scalar1=-1.0, scalar2=cst[:, 3:4], op0=ALU.add, op1=ALU.mult, r=["aic", "cst"], w=["kpr"])
        P.dve("scalar_tensor_tensor", out=kpr[:], in0=kpr[:], scalar=1.0, in1=f["rk"][:], op0=ALU.add, op1=ALU.mult,
              r=["kpr", ("f", "rk")], w=["kpr"])
        P.dve("tensor_tensor", out=bvec[:], in0=kk[:], in1=aic[:], op=ALU.mult, r=["kk", "aic"], w=["bvec"])
        P.dve("tensor_tensor_scan", out=cum[:], data0=rmask[:], data1=ld[:], initial=0.0, op0=ALU.mult, op1=ALU.add,
              r=["rmask", "ld"], w=["cum"])
        P.act("activation", out=W[:], in_=cum[:], func=AF.Exp, r=["cum"], w=["W"])
        P.act("activation", out=Wi[:], in_=cum[:], func=AF.Exp, scale=-1.0, r=["cum"], w=["Wi"])
        P.dve("tensor_tensor", out=Wp[:], in0=cum[:], in1=ld[:], op=ALU.subtract, r=["cum", "ld"], w=["Wp"])
        P.act("activation", out=Wp[:], in_=Wp[:], func=AF.Exp, r=["Wp"], w=["Wp"])
        for c in range(SEG // 64):
            cs = slice(c * 64, (c + 1) * 64)
            P.dve("tensor_scalar", out=Wh[:, cs], in0=cum[:, cs], scalar1=-1.0, scalar2=cum[:, c * 64 + 63:c * 64 + 64],
                  op0=ALU.mult, op1=ALU.add, r=["cum"], w=["Wh"])
        P.act("activation", out=Wh[:], in_=Wh[:], func=AF.Exp, r=["Wh"], w=["Wh"])
        for h in range(2):
            hc = hm[:, h:h + 1]
            P.dve("scalar_tensor_tensor", out=rt[h][:], in0=f["rr"][:], scalar=hc, in1=W[:], op0=ALU.mult, op1=ALU.mult,
                  r=[("f", "rr"), "hm", "W"], w=[("rt", h)])
            P.dve("scalar_tensor_tensor", out=at[h][:], in0=kk[:], scalar=hc, in1=Wp[:], op0=ALU.mult, op1=ALU.mult,
                  r=["kk", "hm", "Wp"], w=[("at", h)])
            P.dve("tensor_scalar", out=at[h][:], in0=at[h][:], scalar1=-1.0, scalar2=None, op0=ALU.mult, r=[("at", h)], w=[("at", h)])
            P.dve("scalar_tensor_tensor", out=bt[h][:], in0=bvec[:], scalar=hc, in1=Wi[:], op0=ALU.mult, op1=ALU.mult,
                  r=["bvec", "hm", "Wi"], w=[("bt", h)])
            P.dve("scalar_tensor_tensor", out=kt[h][:], in0=kpr[:], scalar=hc, in1=Wi[:], op0=ALU.mult, op1=ALU.mult,
                  r=["kpr", "hm", "Wi"], w=[("kt", h)])
        P.dve("tensor_tensor", out=bh[:], in0=bvec[:], in1=Wh[:], op=ALU.mult, r=["bvec", "Wh"], w=["bh"])
        P.dve("tensor_tensor", out=kh[:], in0=kpr[:], in1=Wh[:], op=ALU.mult, r=["kpr", "Wh"], w=["kh"])
        P.dve("scalar_tensor_tensor", out=pb[:], in0=f["rr"][:], scalar=cst[:, 4:5], in1=kpr[:], op0=ALU.mult, op1=ALU.mult,
              r=[("f", "rr"), "cst", "kpr"], w=["pb"])

        if RW_STAGE == 1:
            P.dma("sp", out=yc[t0:t0 + 128, :], in_=pb[:, 0:128], r=["pb"])
            continue
        for j in range(NSC):
            js = slice(j * 128, (j + 1) * 128)
            B0 = PS[0]
            for i3, src in enumerate((bh, kh, f["rv"])):
                key = ["bh", "kh", ("f", "rv")][i3]
                P.pe("matmul", B0[:, i3 * 128:(i3 + 1) * 128], lhsT=src[:, js], rhs=ident[:], start=True, stop=True, r=[key, "ident"], w=[("ps", 0)])
            P.pe("matmul", B0[:, 384:512], lhsT=pb[:, js], rhs=ident[:], start=True, stop=True, r=["pb", "ident"], w=[("ps", 0)])
            P.act("activation", out=tm3[:].rearrange("p a b -> p (a b)"), in_=B0[:], func=AF.Copy, r=[("ps", 0)], w=["tm3"])
            if RW_STAGE == 20:
                P.dma("sp", out=yc[t0 + j * 128:t0 + (j + 1) * 128, :], in_=tm3[:, 2, :], r=["tm3"])
                continue
            P.dve("tensor_reduce", out=bon[:], in_=tm3[:, 3, :].rearrange("p (h n) -> p h n", h=2), axis=AX.X, op=ALU.add,
                  r=["tm3"], w=["bon"])
            if RW_STAGE == 21:
                P.dma("sp", out=yc[t0 + j * 128:t0 + (j + 1) * 128, :], in_=tm3[:, 2, :], r=["tm3", "bon"])
                continue
            if RW_STAGE == 2:
                P.dma("sp", out=yc[t0 + j * 128:t0 + (j + 1) * 128, :], in_=tm3[:, 2, :], r=["tm3"])
                continue
            for c in range(2):
                rs = slice(c * 64, (c + 1) * 64)
                P.dve("tensor_copy", out=Vz[c][rs, :], in_=tm3[rs, 2, :], r=["tm3"], w=[("Vz", c)])
            for h in range(2):
                BA = PS[1 + h]
                P.pe("matmul", BA[:, 0:128], lhsT=bt[h][:, js], rhs=at[h][:, js], start=True, stop=True, r=[("bt", h), ("at", h)], w=[("ps", 1 + h)])
                P.pe("matmul", BA[:, 128:256], lhsT=bt[h][:, js], rhs=rt[h][:, js], start=True, stop=True, r=[("bt", h), ("rt", h)], w=[("ps", 1 + h)])
                P.pe("matmul", BA[:, 256:384], lhsT=kt[h][:, js], rhs=at[h][:, js], start=True, stop=True, r=[("kt", h), ("at", h)], w=[("ps", 1 + h)])
                P.pe("matmul", BA[:, 384:512], lhsT=kt[h][:, js], rhs=rt[h][:, js], start=True, stop=True, r=[("kt", h), ("rt", h)], w=[("ps", 1 + h)])
                P.dve("tensor_tensor", out=Amat[h][:], in0=BA[:], in1=mask4[:], op=ALU.mult, r=[("ps", 1 + h), "mask4"], w=[("Amat", h)])
                BI = PS[3 + h]
                P.pe("matmul", BI[:, 0:128], lhsT=at[h][:, js], rhs=bt[h][:, js], start=True, stop=True, r=[("at", h), ("bt", h)], w=[("ps", 3 + h)])
                P.dve("tensor_tensor", out=Pk[h][0][:, 128:256], in0=BI[:, 0:128], in1=Ms[:], op=ALU.mult, r=[("ps", 3 + h), "Ms"], w=[("Pk", h, 0)])
                P.act("activation", out=Pk[h][0][:, 0:128], in_=Amat[h][:, 0:128], func=AF.Copy, r=[("Amat", h)], w=[("Pk", h, 0)])
                P.dve("tensor_tensor", out=TTm[h][0][:], in0=Amat[h][:, 0:128], in1=ident[:], op=ALU.add, r=[("Amat", h), "ident"], w=[("TT", h, 0)])
            for lev in range(5):
                a_, b_ = lev % 2, (lev + 1) % 2
                for h in range(2):
                    BI = PS[3 + h]
                    cur, nxt = Pk[h][a_], Pk[h][b_]
                    P.pe("matmul", BI[:, 0:128], lhsT=cur[:, 128:256], rhs=cur[:, 0:128], start=True, stop=True, r=[("Pk", h, a_)], w=[("ps", 3 + h)])
                    P.pe("matmul", BI[:, 128:256], lhsT=cur[:, 0:128], rhs=cur[:, 128:256], start=True, stop=True, r=[("Pk", h, a_)], w=[("ps", 3 + h)])
                    P.act("activation", out=nxt[:], in_=BI[:, 0:256], func=AF.Copy, r=[("ps", 3 + h)], w=[("Pk", h, b_)])
                    P.pe("matmul", BI[:, 256:384], lhsT=nxt[:, 128:256], rhs=TTm[h][a_][:], start=True, stop=True,
                         r=[("Pk", h, b_), ("TT", h, a_)], w=[("ps", 3 + h)])
                    P.dve("tensor_tensor", out=TTm[h][b_][:], in0=BI[:, 256:384], in1=TTm[h][a_][:], op=ALU.add,
                          r=[("ps", 3 + h), ("TT", h, a_)], w=[("TT", h, b_)])
            if RW_STAGE == 3:
                P.dma("sp", out=yc[t0 + j * 128:t0 + (j + 1) * 128, :], in_=TTm[0][1][:], r=[("TT", 0, 1)])
                continue
            TTf = [TTm[h][1] for h in range(2)]
            kTT = [("TT", h, 1) for h in range(2)]
            BH = [PS[5], PS[6]]
            for c in range(2):
                rs = slice(c * 64, (c + 1) * 64)
                for h in range(2):
                    hs = slice(h * 64, (h + 1) * 64)
                    B5 = BH[h]
                    xc = slice(0, 64)
                    uc = slice(64, 128)
                    P.pe("matmul", B5[:, xc], lhsT=at[h][:, js], rhs=ST[:], start=True, stop=False, r=[("at", h), "ST"], w=[("ps", 5 + h)])
                    P.pe("matmul", B5[:, xc], lhsT=Amat[h][:, 256:384], rhs=Vz[c][:, hs], start=False, stop=True,
                         r=[("Amat", h), ("Vz", c)], w=[("ps", 5 + h)])
                    P.act("activation", out=Xs[h][:], in_=B5[:, xc], func=AF.Copy, r=[("ps", 5 + h)], w=[("Xs", h)])
                    P.pe("matmul", B5[:, uc], lhsT=TTf[h][:], rhs=Xs[h][:], start=True, stop=True, r=[kTT[h], ("Xs", h)], w=[("ps", 5 + h)])
                    P.dve("tensor_copy", out=Uz[h][c][rs, :], in_=B5[rs, uc], r=[("ps", 5 + h)], w=[("Uz", h, c)])
                for h in range(2):
                    hs = slice(h * 64, (h + 1) * 64)
                    B6 = BH[h]
                    yc_ = slice(128, 192)
                    sc_ = slice(192, 256)
                    P.pe("matmul", B6[:, yc_], lhsT=rt[h][:, js], rhs=ST[:], start=True, stop=False, r=[("rt", h), "ST"], w=[("ps", 5 + h)])
                    P.pe("matmul", B6[:, yc_], lhsT=Amat[h][:, 128:256], rhs=Uz[h][c][:], start=False, stop=False,
                         r=[("Amat", h), ("Uz", h, c)], w=[("ps", 5 + h)])
                    P.pe("matmul", B6[:, yc_], lhsT=Amat[h][:, 384:512], rhs=Vz[c][:, hs], start=False, stop=True,
                         r=[("Amat", h), ("Vz", c)], w=[("ps", 5 + h)])
                    P.act("activation", out=ytm[rs, hs], in_=B6[rs, yc_], func=AF.Copy, r=[("ps", 5 + h)], w=["ytm"])
                    P.pe("matmul", B6[:, sc_], lhsT=tm3[:, 0, :], rhs=Uz[h][c][:], start=True, stop=False, r=["tm3", ("Uz", h, c)], w=[("ps", 5 + h)])
                    P.pe("matmul", B6[:, sc_], lhsT=tm3[:, 1, :], rhs=Vz[c][:, hs], start=False, stop=True, r=["tm3", ("Vz", c)], w=[("ps", 5 + h)])
                for h in range(2):
                    hs = slice(h * 64, (h + 1) * 64)
                    B6 = BH[h]
                    sc_ = slice(192, 256)
                    wc = j * 128 + c * 64 + 63
                    P.dve("scalar_tensor_tensor", out=ST[hs, :], in0=ST[hs, :], scalar=W[hs, wc:wc + 1], in1=B6[hs, sc_],
                          op0=ALU.mult, op1=ALU.add, r=["ST", "W", ("ps", 5 + h)], w=["ST"])
            y3 = ytm[:].rearrange("p (h n) -> p h n", h=2)
            P.dve("tensor_reduce", out=st4[:, 0:2], in_=y3, axis=AX.X, op=ALU.add, r=["ytm"], w=["st4"])
            P.act("activation", out=ysq[:], in_=ytm[:], func=AF.Square, r=["ytm"], w=["ysq"])
            P.dve("tensor_reduce", out=st4[:, 2:4], in_=ysq[:].rearrange("p (h n) -> p h n", h=2), axis=AX.X, op=ALU.add, r=["ysq", "st4"], w=["st4"])
            P.dve("tensor_scalar", out=st4[:, 0:4], in0=st4[:, 0:4], scalar1=1.0 / 64, scalar2=None, op0=ALU.mult, r=["st4"], w=["st4"])
            P.dve("tensor_tensor", out=st4[:, 4:6], in0=st4[:, 0:2], in1=st4[:, 0:2], op=ALU.mult, r=["st4"], w=["st4"])
            P.dve("tensor_tensor", out=st4[:, 4:6], in0=st4[:, 2:4], in1=st4[:, 4:6], op=ALU.subtract, r=["st4"], w=["st4"])
            P.dve("tensor_scalar", out=st4[:, 4:6], in0=st4[:, 4:6], scalar1=64e-5, scalar2=None, op0=ALU.add, r=["st4"], w=["st4"])
            P.act("activation", out=st4[:, 4:6], in_=st4[:, 4:6], func=AF.Sqrt, r=["st4"], w=["st4"])
            P.dve("reciprocal", out=st4[:, 6:8], in_=st4[:, 4:6], r=["st4"], w=["st4"])
            o = yo[j % 2]
            ko = ("yo", j % 2)
            for h in range(2):
                hs = slice(h * 64, (h + 1) * 64)
                P.dve("tensor_scalar", out=o[:, hs], in0=ytm[:, hs], scalar1=st4[:, h:h + 1], scalar2=st4[:, 6 + h:7 + h],
                      op0=ALU.subtract, op1=ALU.mult, r=["ytm", "st4"], w=[ko])
            P.dve("tensor_tensor", out=o[:], in0=o[:], in1=lnw[:], op=ALU.mult, r=[ko, "lnw"], w=[ko])
            P.dve("tensor_tensor", out=o[:], in0=o[:], in1=lnb[:], op=ALU.add, r=[ko, "lnb"], w=[ko])
            for h in range(2):
                hs = slice(h * 64, (h + 1) * 64)
                P.dve("scalar_tensor_tensor", out=o[:, hs], in0=tm3[:, 2, hs], scalar=bon[:, h:h + 1], in1=o[:, hs],
                      op0=ALU.mult, op1=ALU.add, r=["tm3", "bon", ko], w=[ko])
            P.dve("tensor_tensor", out=o[:], in0=o[:], in1=gtm[:, j, :], op=ALU.mult, r=[ko, "gtm"], w=[ko])
            P.dma("sp", out=yc[t0 + j * 128:t0 + (j + 1) * 128, :], in_=o[:], r=[ko])


def build_ffn(T, final=False, TP=512):
    D, FH = D_MODEL, FFN_HIDDEN
    KC = D // 128
    NJ = FH // 128
    GJ = 4
    nc = bass.Bass("TRN2", target_bir_lowering=False)
    P = Prog(nc)
    import contextlib
    with contextlib.ExitStack() as es:
        C = _Ctx(nc, es)
        mixT = C.din("mixT", [D, T]); xT = C.din("xT", [D, T])
        w_out = C.din("w_out", [D, D]); gF = C.din("g_ffn", [128, KC]); gL = C.din("g_fin", [128, KC])
        wgu = C.din("w_gu", [D, 2 * FH]); wdn = C.din("w_dn", [FH, D])
        oT = C.dout("oT", [D, T])
        PS = [es.enter_context(nc.psum_tensor(f"psc{i}", [128, 512], F32)) for i in range(8)]
        x_sb = C.sb("x_sb", [128, KC, TP])
        mh = C.sb("mh", [128, KC, TP], BF16)
        gf = C.sb("gf", [128, KC]); gl = C.sb("gl", [128, KC])
        ones = C.sb("ones", [128, 128])
        sq = [C.sb(f"sq{i}", [128, TP]) for i in range(2)]
        rstd = C.sb("rstd", [128, TP])
        wo = [C.sb(f"wo{i}", [128, KC, 512], BF16) for i in range(2)]
        wg = [C.sb(f"wg{i}", [128, KC, 512], BF16) for i in range(2)]
        wu = [C.sb(f"wu{i}", [128, KC, 512], BF16) for i in range(2)]
        wd = [C.sb(f"wd{i}", [128, GJ, D], BF16) for i in range(2)]
        sg = [C.sb(f"sg{i}", [128, TP]) for i in range(2)]
        act = [C.sb(f"act{i}", [128, GJ, TP], BF16) for i in range(2)]
        ob = [C.sb(f"obf{i}", [128, TP]) for i in range(2)]
        wo_v = w_out.rearrange("(k p) c -> p k c", p=128)
        wgu_v = wgu.rearrange("(k p) c -> p k c", p=128)
        wdn_v = wdn.rearrange("(j p) c -> p j c", p=128)
        xT_v = xT.rearrange("(k p) t -> p k t", p=128)
        mixT_v = mixT.rearrange("(k p) t -> p k t", p=128)
        oT_v = oT.rearrange("(k p) t -> p k t", p=128)

        P.dma("sp", out=gf[:], in_=gF, w=["gf"])
        P.dma("sp", out=gl[:], in_=gL, w=["gl"])
        P.dve("memset", ones[:], 1.0, w=["ones"])

        def rmsnorm(gt, gkey, outfn):
            for c in range(KC):
                s = sq[c % 2]
                P.act("activation", out=s[:], in_=x_sb[:, c, :], func=AF.Square, r=[("x", c)], w=[("sq", c % 2)])
                P.pe("matmul", PS[7][:, 0:TP], lhsT=ones[:], rhs=s[:], start=(c == 0), stop=(c == KC - 1),
                     r=["ones", ("sq", c % 2)], w=[("ps", 7)])
            P.dve("tensor_scalar", out=rstd[:], in0=PS[7][:, 0:TP], scalar1=1.0 / D, scalar2=NORM_EPS, op0=ALU.mult, op1=ALU.add,
                  r=[("ps", 7)], w=["rstd"])
            P.act("activation", out=rstd[:], in_=rstd[:], func=AF.Sqrt, r=["rstd"], w=["rstd"])
            P.dve("reciprocal", out=rstd[:], in_=rstd[:], r=["rstd"], w=["rstd"])
            for c in range(KC):
                outfn(c, gt, gkey)

        pcnt = [0]
        wcnt = {"wo": 0, "wgu": 0, "wd": 0}
        for tp in range(T // TP):
            ts = slice(tp * TP, (tp + 1) * TP)
            for c0 in range(0, KC, 4):
                P.dma("sp", out=x_sb[:, c0:c0 + 4, :], in_=xT_v[:, c0:c0 + 4, ts], w=[("x", c) for c in range(c0, c0 + 4)])
                P.dma("pool", out=mh[:, c0:c0 + 4, :], in_=mixT_v[:, c0:c0 + 4, ts], w=[("mh", c) for c in range(c0, c0 + 4)])
            for pi in range(D // 512):
                b = wcnt["wo"] % 2
                wcnt["wo"] += 1
                P.dma("pool", out=wo[b][:], in_=wo_v[:, :, pi * 512:(pi + 1) * 512], w=[("wo", b)])
                for mi in range(4):
                    m = pi * 4 + mi
                    pb_ = pcnt[0] % 4
                    pcnt[0] += 1
                    for k in range(KC):
                        P.pe("matmul", PS[pb_][:, 0:TP], lhsT=wo[b][:, k, mi * 128:(mi + 1) * 128], rhs=mh[:, k, :],
                             start=(k == 0), stop=(k == KC - 1), r=[("wo", b), ("mh", k)], w=[("ps", pb_)])
                    P.dve("tensor_tensor", out=x_sb[:, m, :], in0=x_sb[:, m, :], in1=PS[pb_][:, 0:TP], op=ALU.add,
                          r=[("x", m), ("ps", pb_)], w=[("x", m)])
            rmsnorm(gf, "gf", lambda c, gt, gkey: P.dve(
                "scalar_tensor_tensor", out=mh[:, c, :], in0=x_sb[:, c, :], scalar=gt[:, c:c + 1], in1=rstd[:],
                op0=ALU.mult, op1=ALU.mult, r=[("x", c), gkey, "rstd"], w=[("mh", c)]))
            for g in range(NJ // GJ):
                b = wcnt["wgu"] % 2
                wcnt["wgu"] += 1
                P.dma("pool", out=wg[b][:], in_=wgu_v[:, :, g * 512:(g + 1) * 512], w=[("wg", b)])
                P.dma("pool", out=wu[b][:], in_=wgu_v[:, :, FH + g * 512:FH + (g + 1) * 512], w=[("wu", b)])
                bd = wcnt["wd"] % 2
                wcnt["wd"] += 1
                P.dma("pool", out=wd[bd][:], in_=wdn_v[:, g * GJ:(g + 1) * GJ, :], w=[("wd", bd)])
                ab = g % 2
                for jj in range(GJ):
                    pg = pcnt[0] % 4
                    pu = 4 + pcnt[0] % 3
                    pcnt[0] += 1
                    for k in range(KC):
                        P.pe("matmul", PS[pg][:, 0:TP], lhsT=wg[b][:, k, jj * 128:(jj + 1) * 128], rhs=mh[:, k, :],
                             start=(k == 0), stop=(k == KC - 1), r=[("wg", b), ("mh", k)], w=[("ps", pg)])
                    for k in range(KC):
                        P.pe("matmul", PS[pu][:, 0:TP], lhsT=wu[b][:, k, jj * 128:(jj + 1) * 128], rhs=mh[:, k, :],
                             start=(k == 0), stop=(k == KC - 1), r=[("wu", b), ("mh", k)], w=[("ps", pu)])
                    s = sg[jj % 2]
                    P.act("activation", out=s[:], in_=PS[pg][:, 0:TP], func=AF.Silu, r=[("ps", pg)], w=[("sg", jj % 2)])
                    P.dve("tensor_tensor", out=act[ab][:, jj, :], in0=s[:], in1=PS[pu][:, 0:TP], op=ALU.mult,
                          r=[("sg", jj % 2), ("ps", pu)], w=[("act", ab, jj)])
                for m in range(KC):
                    pb_ = pcnt[0] % 4
                    pcnt[0] += 1
                    for jj in range(GJ):
                        P.pe("matmul", PS[pb_][:, 0:TP], lhsT=wd[bd][:, jj, m * 128:(m + 1) * 128], rhs=act[ab][:, jj, :],
                             start=(jj == 0), stop=(jj == GJ - 1), r=[("wd", bd), ("act", ab, jj)], w=[("ps", pb_)])
                    P.dve("tensor_tensor", out=x_sb[:, m, :], in0=x_sb[:, m, :], in1=PS[pb_][:, 0:TP], op=ALU.add,
                          r=[("x", m), ("ps", pb_)], w=[("x", m)])
            if final:
                def outfn(c, gt, gkey):
                    o = ob[c % 2]
                    P.dve("scalar_tensor_tensor", out=o[:], in0=x_sb[:, c, :], scalar=gt[:, c:c + 1], in1=rstd[:],
                          op0=ALU.mult, op1=ALU.mult, r=[("x", c), gkey, "rstd"], w=[("ob", c % 2)])
                    P.dma("sp", out=oT_v[:, c, ts], in_=o[:], r=[("ob", c % 2)])
                rmsnorm(gl, "gl", outfn)
            else:
                for c0 in range(0, KC, 4):
                    P.dma("sp", out=oT_v[:, c0:c0 + 4, ts], in_=x_sb[:, c0:c0 + 4, :], r=[("x", c) for c in range(c0, c0 + 4)])
        P.emit()
    return nc


def _alibi_slopes():
    idx = np.arange(1, 13, dtype=np.float64)
    m = np.exp2(-8.0 * idx / 12)
    di = np.arange(2, 12, 3)
    si = np.setdiff1d(np.arange(12), di)
    return m[di].astype(np.float32), m[si].astype(np.float32)


_PROGS = {}


def _prog(name, fn):
    if name not in _PROGS:
        _PROGS[name] = fn()
    return _PROGS[name]


def _run(nc, in_maps):
    res = run_bass_kernel_spmd(nc, in_maps, core_ids=list(range(len(in_maps))))
    return res.results


def kernel(x, attn_norm_g, w_in, diff_lambda, diff_subln_g, swa_sinks, rwkv_mu,
           rwkv_w0, rwkv_w2, rwkv_a0, rwkv_a2, rwkv_g2, rwkv_k_k, rwkv_k_a, rwkv_r_k,
           rwkv_ln_w, rwkv_ln_b, w_out, ffn_norm_g, w_gate_up, w_down, final_norm_g):
    f32 = np.float32
    A = lambda z: np.ascontiguousarray(np.asarray(z, dtype=f32))
    T = SEQ
    TC = T // NCORES
    D = D_MODEL
    x = np.asarray(x, dtype=f32).reshape(T, D)
    dsl, ssl = _alibi_slopes()
    gcol = lambda g: A(np.asarray(g, dtype=f32).reshape(D // 128, 128).T)
    xTfull = A(x.T)
    TA, TCc = T // NA, T // NC
    ncA = _prog("A", lambda: build_proj(TA, D, IN_COLS))
    ncB = _prog("B", lambda: build_mix(T))
    NI = T // 1024
    for l in range(DEPTH):
        wl_ = A(w_in[l])
        ga = gcol(attn_norm_g[l])
        rA = _run(ncA, [dict(xT=A(xTfull[:, c * TA:(c + 1) * TA]), g=ga, w=wl_) for c in range(NA)])
        projT = np.concatenate([rA[c]["yT"] for c in range(NA)], axis=1)
        del rA
        linit = 0.8 - 0.6 * math.exp(-0.3 * l)
        RW = 2304
        mu = np.asarray(rwkv_mu[l], dtype=f32)
        in_maps = []
        qcols_all = []
        for c in range(NCORES):
            h, p = c % 4, c // 4
            qcols = np.concatenate([np.arange((2 * i + p) * 512, (2 * i + p + 1) * 512) for i in range(NI)])
            qcols_all.append(qcols)
            dcst = np.zeros((128, 8), f32)
            dcst[:, 0] = -dsl[h]; dcst[:, 1] = 512 * p; dcst[:, 2] = linit; dcst[:, 3] = np.asarray(diff_subln_g[l], dtype=f32)
            scst = np.zeros((128, 8), f32)
            scst[:, 0] = -ssl[c]; scst[:, 1] = np.asarray(swa_sinks[l], dtype=f32)[c]
            kvh = c // 4
            ch = slice(c * 128, (c + 1) * 128)
            rmu = np.zeros((128, 8), f32)
            rmu[:, 0] = mu[0:1024][ch]; rmu[:, 1] = mu[1024:2048][ch]; rmu[:, 2] = mu[2048:3072][ch]
            rmu[:96, 3] = mu[3072:3168]; rmu[:96, 4] = mu[3168:3264]; rmu[:, 5] = mu[3264:3392]; rmu[:, 6] = mu[3392:3520]
            rcst = np.zeros((128, 8), f32)
            rcst[:, 0] = np.asarray(rwkv_w0[l], dtype=f32)[ch]; rcst[:, 1] = np.asarray(rwkv_a0[l], dtype=f32)[ch]
            rcst[:, 2] = np.asarray(rwkv_k_k[l], dtype=f32)[ch]; rcst[:, 3] = np.asarray(rwkv_k_a[l], dtype=f32)[ch]
            rcst[:, 4] = np.asarray(rwkv_r_k[l], dtype=f32).reshape(1024)[ch]
            in_maps.append(dict(
                dq=A(projT[h * 128:(h + 1) * 128][:, qcols]),
                dk=A(projT[512 + h * 128:512 + (h + 1) * 128]),
                dv=A(projT[1024 + h * 128:1024 + (h + 1) * 128].T),
                dcst=dcst, dlam=A(np.broadcast_to(np.asarray(diff_lambda[l], dtype=f32).reshape(1, 256), (128, 256))),
                sq=A(projT[1536 + c * 64:1536 + (c + 1) * 64]),
                sk=A(projT[2048 + kvh * 64:2048 + (kvh + 1) * 64]),
                sv=A(projT[2176 + kvh * 64:2176 + (kvh + 1) * 64].T),
                scst=scst,
                rr=A(projT[RW + c * 128:RW + (c + 1) * 128]),
                rk=A(projT[RW + 1024 + c * 128:RW + 1024 + (c + 1) * 128]),
                rv=A(projT[RW + 2048 + c * 128:RW + 2048 + (c + 1) * 128]),
                rwl=A(projT[RW + 3072:RW + 3168]), ral=A(projT[RW + 3168:RW + 3264]), rgl=A(projT[RW + 3264:RW + 3520]),
                rmu=rmu, rw2=A(np.asarray(rwkv_w2[l])[:, ch]), ra2=A(np.asarray(rwkv_a2[l])[:, ch]), rg2=A(np.asarray(rwkv_g2[l])[:, ch]),
                rcst=rcst,
                rlnw=A(np.broadcast_to(np.asarray(rwkv_ln_w[l], dtype=f32)[ch], (128, 128))),
                rlnb=A(np.broadcast_to(np.asarray(rwkv_ln_b[l], dtype=f32)[ch], (128, 128)))))
        del projT
        rB = _run(ncB, in_maps)
        del in_maps
        mixT = np.empty((D, T), f32)
        for c in range(NCORES):
            h = c % 4
            mixT[h * 128:(h + 1) * 128][:, qcols_all[c]] = rB[c]["ya"]
            mixT[512 + c * 64:512 + (c + 1) * 64] = rB[c]["yb"]
            mixT[1024 + c * 128:1024 + (c + 1) * 128] = rB[c]["yc"].T
        del rB
        final = (l == DEPTH - 1)
        ncC = _prog("Cf" if final else "C", lambda: build_ffn(TCc, final=final))
        wo_, wgu_, wdn_ = A(w_out[l]), A(w_gate_up[l]), A(w_down[l])
        gf_, gl_ = gcol(ffn_norm_g[l]), gcol(final_norm_g)
        rC = _run(ncC, [dict(mixT=A(mixT[:, c * TCc:(c + 1) * TCc]), xT=A(xTfull[:, c * TCc:(c + 1) * TCc]), w_out=wo_, g_ffn=gf_,
                             g_fin=gl_, w_gu=wgu_, w_dn=wdn_) for c in range(NC)])
        xTfull = np.concatenate([rC[c]["oT"] for c in range(NC)], axis=1)
        del rC, mixT
    out = A(xTfull.T).reshape(1, T, D)
    return out
```
